# Optimizing a Trainium2 kernel written in Bass

```python
import math
import jax, jax.numpy as jnp
from jax import lax
import numpy as np

D_MODEL = 1024
BATCH = 4
SEQ = 4096
DEPTH = 1
DEC_BATCH = 32
DEC_SEQ = 16
PAST_LEN = 1024

CHUNK = 64
N_META = 16
Q_BLOCK = 128
SB_HEADS = 16
SB_HEAD_DIM = 64
SB_WIDTH = SB_HEADS * SB_HEAD_DIM
SSD_WIDTH = 2 * D_MODEL
SSD_HEAD_DIM = 64
SSD_HEADS = SSD_WIDTH // SSD_HEAD_DIM
SSD_GROUPS = 4
SSD_HEADS_PER_GROUP = SSD_HEADS // SSD_GROUPS
SSD_STATE = 128
SSD_CONV = 4
SSD_CONV_CH = SSD_WIDTH + 2 * SSD_GROUPS * SSD_STATE
D_FF = ((8 * D_MODEL // 3 + 127) // 128) * 128
FFN_CONV = 3
N_BRANCH = 2
IN_COLS = 3 * SB_WIDTH + SSD_WIDTH + SSD_CONV_CH + SSD_HEADS + N_BRANCH * D_MODEL
EPS = 1e-6

kernel_name = 'streaming_stickbreak_ssd_hybrid_step'


def rms_norm(x, g):
    xf = x.astype(jnp.float32)
    y = xf * lax.rsqrt(jnp.mean(xf * xf, axis=-1, keepdims=True) + EPS)
    return (y * g.astype(jnp.float32)).astype(x.dtype)


def causal_dwconv(x, prev, w, bias):
    k_width, length = w.shape[0], x.shape[1]
    xp = jnp.concatenate([prev.astype(x.dtype), x], axis=1)
    y = bias
    for i in range(k_width):
        y = y + xp[:, i:i + length] * w[i]
    return y, xp[:, length:]


def stick_breaking(q, k, v, q_pos, k_pos):
    z = jnp.einsum('bqhd,bshd->bhqs', q, k).astype(jnp.float32) * (SB_HEAD_DIM ** -0.5)
    reach = k_pos[None, :] < q_pos[:, None]
    log_beta = jax.nn.log_sigmoid(z)
    log_keep = jnp.where(reach, jax.nn.log_sigmoid(-z), 0.0)
    later = lax.cumsum(log_keep, axis=3, reverse=True) - log_keep
    w = jnp.where(reach, jnp.exp(log_beta + later), 0.0)
    return jnp.einsum('bhqs,bshd->bqhd', w.astype(v.dtype), v)


def stick_breaking_prompt(q, k, v):
    b, length, h, d = q.shape
    nb = -(-length // Q_BLOCK)
    pad = nb * Q_BLOCK - length
    qb = jnp.pad(q, ((0, 0), (0, pad), (0, 0), (0, 0))).reshape(b, nb, Q_BLOCK, h, d).transpose(1, 0, 2, 3, 4)
    q_pos = jnp.arange(nb * Q_BLOCK).reshape(nb, Q_BLOCK)
    k_pos = jnp.arange(length)
    out = lax.map(lambda blk: stick_breaking(blk[0], k, v, blk[1], k_pos), (qb, q_pos))
    return out.transpose(1, 0, 2, 3, 4).reshape(b, nb * Q_BLOCK, h, d)[:, :length]


def ssd_scan(x, dt, a_head, b_in, c_in, h0, pad_left):
    bsz, length = x.shape[:2]
    G, R, P, N = SSD_GROUPS, SSD_HEADS_PER_GROUP, SSD_HEAD_DIM, SSD_STATE
    pad_right = (-(pad_left + length)) % CHUNK
    nc = (pad_left + length + pad_right) // CHUNK

    def chunked(t):
        t = jnp.pad(t, [(0, 0), (pad_left, pad_right)] + [(0, 0)] * (t.ndim - 2))
        return t.reshape((bsz, nc, CHUNK) + t.shape[2:])

    xc = chunked(x).reshape(bsz, nc, CHUNK, G, R, P)
    dtc = chunked(dt).reshape(bsz, nc, CHUNK, G, R)
    bc, cc = chunked(b_in), chunked(c_in)
    dt_t = jnp.moveaxis(dtc, 2, -1)
    a_cum = jnp.cumsum(dt_t * a_head.reshape(G, R)[..., None], axis=-1)
    causal = jnp.tril(jnp.ones((CHUNK, CHUNK), dtype=bool))
    decay_ts = jnp.exp(jnp.where(causal, a_cum[..., :, None] - a_cum[..., None, :], -jnp.inf))
    cb = jnp.einsum('bctgn,bcsgn->bcgts', cc, bc)
    y_diag = jnp.einsum('bcgrts,bcsgrp->bctgrp', cb[:, :, :, None] * decay_ts * dt_t[..., None, :], xc)
    to_end = jnp.exp(a_cum[..., -1:] - a_cum) * dt_t
    chunk_states = jnp.einsum('bcgrs,bcsgn,bcsgrp->bcgrpn', to_end, bc, xc)
    chunk_decay = jnp.exp(a_cum[..., -1])

    def step(h, inp):
        dec, st = inp
        return dec[..., None, None] * h + st, h

    h_init = h0.reshape(bsz, G, R, P, N).astype(jnp.float32)
    h_final, h_in = lax.scan(step, h_init, (jnp.moveaxis(chunk_decay, 1, 0), jnp.moveaxis(chunk_states, 1, 0)))
    h_in = jnp.moveaxis(h_in, 0, 1)
    y_off = jnp.einsum('bctgn,bcgrpn,bcgrt->bctgrp', cc, h_in, jnp.exp(a_cum))
    y = (y_diag + y_off).reshape(bsz, nc * CHUNK, G * R, P)[:, pad_left:pad_left + length]
    return y.astype(x.dtype), h_final.reshape(bsz, G * R, P, N).astype(h0.dtype)


def ssd_mix(xbc, z, dt_raw, conv_prev, h0, conv_w, conv_b, dt_bias, a_log, d_skip, g_norm, pad_left):
    bsz, length, _ = xbc.shape
    xbc, conv_state = causal_dwconv(xbc, conv_prev, conv_w, conv_b)
    xbc = jax.nn.silu(xbc)
    xs, b_in, c_in = jnp.split(xbc, [SSD_WIDTH, SSD_WIDTH + SSD_GROUPS * SSD_STATE], axis=-1)
    xs = xs.reshape(bsz, length, SSD_HEADS, SSD_HEAD_DIM)
    b_in = b_in.reshape(bsz, length, SSD_GROUPS, SSD_STATE)
    c_in = c_in.reshape(bsz, length, SSD_GROUPS, SSD_STATE)
    dt = jax.nn.softplus(dt_raw.astype(jnp.float32) + dt_bias.astype(jnp.float32))
    a_head = -jnp.exp(a_log.astype(jnp.float32))
    y, h_final = ssd_scan(xs, dt, a_head, b_in, c_in, h0, pad_left)
    y = (y + d_skip[:, None] * xs).reshape(bsz, length, SSD_WIDTH) * jax.nn.silu(z)
    y = rms_norm(y.reshape(bsz, length, SSD_GROUPS, SSD_WIDTH // SSD_GROUPS), g_norm.reshape(SSD_GROUPS, -1))
    return y.reshape(bsz, length, SSD_WIDTH), h_final, conv_state


def trunk_layer(h, p, kv_past, ssd_h0, ssd_conv_prev, ffn_conv_prev, pad_left):
    bsz, length, _ = h.shape
    u = rms_norm(h, p['g_mix'])
    proj = u @ p['w_in']
    sizes = [SB_WIDTH, SB_WIDTH, SB_WIDTH, SSD_WIDTH, SSD_CONV_CH, SSD_HEADS, N_BRANCH * D_MODEL]
    q, k, v, z, xbc, dt_raw, gate_logits = jnp.split(proj, np.cumsum(sizes)[:-1].tolist(), axis=-1)
    q, k, v = (t.reshape(bsz, length, SB_HEADS, SB_HEAD_DIM) for t in (q, k, v))
    if kv_past is None:
        att = stick_breaking_prompt(q, k, v)
    else:
        n_past = kv_past[0].shape[1]
        k_all = jnp.concatenate([kv_past[0].astype(k.dtype), k], axis=1)
        v_all = jnp.concatenate([kv_past[1].astype(v.dtype), v], axis=1)
        att = stick_breaking(q, k_all, v_all, n_past + jnp.arange(length), jnp.arange(n_past + length))
    ssd, ssd_state, ssd_conv = ssd_mix(xbc, z, dt_raw, ssd_conv_prev, ssd_h0, p['conv_w_ssd'], p['conv_b_ssd'],
                                       p['dt_bias'], p['a_log'], p['d_skip'], p['g_ssd_norm'], pad_left)
    gate_att, gate_ssd = jnp.split(jax.nn.sigmoid(gate_logits), 2, axis=-1)
    merged = (gate_att * (att.reshape(bsz, length, SB_WIDTH) @ p['w_br_att'])
              + gate_ssd * (ssd @ p['w_br_ssd']))
    h = h + merged @ p['w_out']
    u = rms_norm(h, p['g_ffn'])
    up, ffn_conv = causal_dwconv(u @ p['w_up'], ffn_conv_prev, p['conv_w_ffn'], p['conv_b_ffn'])
    gate, val = jnp.split(up, 2, axis=-1)
    h = h + (jax.nn.silu(gate) * val) @ p['w_down']
    return h, k, v, ssd_state, ssd_conv, ffn_conv


def setup_inputs(seed: int = 0) -> dict:
    key = jax.random.key(seed)
    ks = jax.random.split(key, 26)
    f32 = jnp.float32

    def nrm(k, shape, scale):
        return scale * jax.random.normal(k, shape, f32)

    def gain(k, shape):
        return 1.0 + 0.01 * jax.random.normal(k, shape, f32)

    n_rows = N_META + PAST_LEN
    dt0 = jnp.exp(jax.random.uniform(ks[10], (DEPTH, SSD_HEADS), f32, math.log(1e-3), math.log(1e-1)))
    return {
        'x_prompt': nrm(ks[0], (BATCH, SEQ, D_MODEL), 1.0),
        'x_sample': nrm(ks[1], (DEC_BATCH, DEC_SEQ, D_MODEL), 1.0),
        'cache_k': nrm(ks[2], (DEPTH, DEC_BATCH, n_rows, SB_HEADS, SB_HEAD_DIM), 1.0),
        'cache_v': nrm(ks[3], (DEPTH, DEC_BATCH, n_rows, SB_HEADS, SB_HEAD_DIM), 1.0),
        'state_ssm': nrm(ks[4], (DEPTH, DEC_BATCH, SSD_HEADS, SSD_HEAD_DIM, SSD_STATE), 0.1),
        'state_ssm_conv': nrm(ks[5], (DEPTH, DEC_BATCH, SSD_CONV - 1, SSD_CONV_CH), 1.0),
        'state_ffn_conv': nrm(ks[6], (DEPTH, DEC_BATCH, FFN_CONV - 1, 2 * D_FF), 1.0),
        'meta_tokens': nrm(ks[7], (N_META, D_MODEL), 1.0),
        'g_mix': gain(ks[8], (DEPTH, D_MODEL)),
        'w_in': nrm(ks[9], (DEPTH, D_MODEL, IN_COLS), D_MODEL ** -0.5),
        'conv_w_ssd': nrm(ks[11], (DEPTH, SSD_CONV, SSD_CONV_CH), SSD_CONV ** -0.5),
        'conv_b_ssd': nrm(ks[12], (DEPTH, SSD_CONV_CH), 0.02),
        'dt_bias': dt0 + jnp.log(-jnp.expm1(-dt0)),
        'a_log': jnp.log(jax.random.uniform(ks[13], (DEPTH, SSD_HEADS), f32, 1.0, 16.0)),
        'd_skip': 1.0 + 0.1 * jax.random.normal(ks[14], (DEPTH, SSD_HEADS), f32),
        'g_ssd_norm': gain(ks[15], (DEPTH, SSD_WIDTH)),
        'w_br_att': nrm(ks[16], (DEPTH, SB_WIDTH, D_MODEL), SB_WIDTH ** -0.5),
        'w_br_ssd': nrm(ks[17], (DEPTH, SSD_WIDTH, D_MODEL), SSD_WIDTH ** -0.5),
        'w_out': nrm(ks[18], (DEPTH, D_MODEL, D_MODEL), D_MODEL ** -0.5),
        'g_ffn': gain(ks[19], (DEPTH, D_MODEL)),
        'w_up': nrm(ks[20], (DEPTH, D_MODEL, 2 * D_FF), D_MODEL ** -0.5),
        'conv_w_ffn': nrm(ks[21], (DEPTH, FFN_CONV, 2 * D_FF), FFN_CONV ** -0.5),
        'conv_b_ffn': nrm(ks[22], (DEPTH, 2 * D_FF), 0.02),
        'w_down': nrm(ks[23], (DEPTH, D_FF, D_MODEL), D_FF ** -0.5),
        'g_final': gain(ks[24], (D_MODEL,)),
    }


def reference(x_prompt, x_sample, cache_k, cache_v, state_ssm, state_ssm_conv, state_ffn_conv, meta_tokens,
              g_mix, w_in, conv_w_ssd, conv_b_ssd, dt_bias, a_log, d_skip, g_ssd_norm, w_br_att, w_br_ssd,
              w_out, g_ffn, w_up, conv_w_ffn, conv_b_ffn, w_down, g_final):
    bp, bs = x_prompt.shape[0], x_sample.shape[0]
    dtype = x_prompt.dtype
    meta = jnp.broadcast_to(meta_tokens.astype(dtype)[None], (bp, N_META, D_MODEL))
    hp = jnp.concatenate([meta, x_prompt], axis=1)
    hs = x_sample
    meta_pad = (-N_META) % CHUNK
    kp_l, vp_l, sp_l, cp_l, fp_l = [], [], [], [], []
    ks_l, vs_l, ss_l, cs_l, fs_l = [], [], [], [], []
    for l in range(DEPTH):
        p = {'g_mix': g_mix[l], 'w_in': w_in[l], 'conv_w_ssd': conv_w_ssd[l], 'conv_b_ssd': conv_b_ssd[l],
             'dt_bias': dt_bias[l], 'a_log': a_log[l], 'd_skip': d_skip[l], 'g_ssd_norm': g_ssd_norm[l],
             'w_br_att': w_br_att[l], 'w_br_ssd': w_br_ssd[l], 'w_out': w_out[l], 'g_ffn': g_ffn[l],
             'w_up': w_up[l], 'conv_w_ffn': conv_w_ffn[l], 'conv_b_ffn': conv_b_ffn[l], 'w_down': w_down[l]}
        hp, kp, vp, sp, cp, fp = trunk_layer(
            hp, p, None,
            jnp.zeros((bp, SSD_HEADS, SSD_HEAD_DIM, SSD_STATE), dtype),
            jnp.zeros((bp, SSD_CONV - 1, SSD_CONV_CH), dtype),
            jnp.zeros((bp, FFN_CONV - 1, 2 * D_FF), dtype),
            meta_pad)
        hs, ks_, vs_, ss, cs, fs = trunk_layer(
            hs, p, (cache_k[l], cache_v[l]), state_ssm[l], state_ssm_conv[l], state_ffn_conv[l], 0)
        kp_l.append(kp); vp_l.append(vp); sp_l.append(sp); cp_l.append(cp); fp_l.append(fp)
        ks_l.append(ks_); vs_l.append(vs_); ss_l.append(ss); cs_l.append(cs); fs_l.append(fs)
    y_prompt = rms_norm(hp, g_final)[:, N_META:]
    y_sample = rms_norm(hs, g_final)
    return (y_prompt, y_sample,
            jnp.stack(kp_l), jnp.stack(vp_l), jnp.stack(sp_l), jnp.stack(cp_l), jnp.stack(fp_l),
            jnp.stack(ks_l), jnp.stack(vs_l), jnp.stack(ss_l), jnp.stack(cs_l), jnp.stack(fs_l))
```

```python
import contextlib
import os
import numpy as np
import concourse.bass as bass
import concourse.mybir as mybir
from concourse.bass_utils import run_bass_kernel_spmd

F32 = mybir.dt.float32
BF16 = mybir.dt.bfloat16
AF = mybir.ActivationFunctionType
ALU = mybir.AluOpType

D = 1024
NPRE = 16
NMAIN = 17
NPT = NPRE + NMAIN
NT = NPT + 1
NM1 = NMAIN + 1
TOK = NT * 128
TOKM = NM1 * 128
SBW = 1024
SSDW = 2048
XBC = 3072
DFF = 2816
C_Q, C_K, C_V, C_Z, C_X, C_DT, C_G = 0, 1024, 2048, 3072, 5120, 8192, 8224
INC = 10272
EPS = 1e-6
NSTREAM = 4
NCACHE = 1040


_STOP = float(os.environ.get("MK_STOP", "99"))
_DEBUG = int(os.environ.get("MK_DEBUG", "0"))
_LAST = {}


class _Stop(Exception):
    pass


def checkpoint(n):
    if _STOP <= n:
        raise _Stop()


class Buf:
    __slots__ = ("name", "lw", "rd", "wr")

    def __init__(self, name):
        self.name = name
        self.lw = None
        self.rd = {}
        self.wr = {}


class KB:
    def __init__(self, nc, stack, n_dma_sp=48, n_dma_pool=32):
        self.nc = nc
        self.engs = {"pe": nc.tensor, "act": nc.scalar, "dve": nc.vector, "pool": nc.gpsimd, "sp": nc.sync}
        self.sems = {}
        self.cnt = {}
        for e in ("pe", "act", "dve", "pool"):
            self.sems[e] = stack.enter_context(nc.semaphore("s_" + e))
            self.cnt[e] = 0
        self.dq = {"sp": [], "pool": []}
        for q, n in (("sp", n_dma_sp), ("pool", n_dma_pool)):
            for i in range(n):
                sid = "d_%s_%d" % (q, i)
                self.sems[sid] = stack.enter_context(nc.semaphore(sid))
                self.cnt[sid] = 0
                self.dq[q].append(sid)
        self.dq_next = {"sp": 0, "pool": 0}
        self.seen = {e: {} for e in self.engs}
        self.n_ins = 0
        self.n_wait = 0
        self._nb = 0
        self._rr = 0
        self.pstack = stack

    def sb(self, name, shape, dt):
        self._nm = getattr(self, "_nm", 0) + 1
        return self.pstack.enter_context(self.nc.sbuf_tensor("%s_%d" % (name, self._nm), list(shape), dt))

    def barrier(self):
        for e in self.engs:
            for sid, c in self.cnt.items():
                self._wait(e, sid, c)

    def new_phase(self):
        self.barrier()
        self.pstack.close()
        self.pstack = contextlib.ExitStack()

    def buf(self, name=None):
        self._nb += 1
        return Buf(name or ("b%d" % self._nb))

    def _wait(self, eng, sid, val):
        if val <= 0:
            return
        s = self.seen[eng]
        if s.get(sid, 0) >= val:
            return
        self.engs[eng].wait_ge(self.sems[sid], val)
        s[sid] = val
        self.n_wait += 1

    def _wait_ev(self, eng, ev):
        sid, val = ev
        if sid == "pe" and eng == "pe":
            return
        self._wait(eng, sid, val)

    def _deps(self, eng, reads, writes, swrites=()):
        for b in reads:
            if b.lw is not None:
                self._wait_ev(eng, b.lw)
            for sid, val in b.wr.items():
                self._wait_ev(eng, (sid, val))
        for b in writes:
            if b.lw is not None:
                self._wait_ev(eng, b.lw)
            for sid, val in b.wr.items():
                self._wait_ev(eng, (sid, val))
            for sid, val in b.rd.items():
                self._wait_ev(eng, (sid, val))
        for b in swrites:
            if b.lw is not None:
                self._wait_ev(eng, b.lw)
            for sid, val in b.rd.items():
                self._wait_ev(eng, (sid, val))

    def _commit(self, ev, reads, writes, swrites=()):
        sid, val = ev
        for b in writes:
            b.lw = ev
            b.rd = {}
            b.wr = {}
        for b in swrites:
            b.rd = {}
            if b.wr.get(sid, 0) < val:
                b.wr[sid] = val
        for b in reads:
            if b.rd.get(sid, 0) < val:
                b.rd[sid] = val

    def op(self, eng, fn, reads=(), writes=()):
        self._deps(eng, reads, writes)
        ins = fn(self.engs[eng])
        self.cnt[eng] += 1
        ins.then_inc(self.sems[eng], 1)
        ev = (eng, self.cnt[eng])
        self._commit(ev, reads, writes)
        self.n_ins += 1
        return ev

    def cp(self, out, in_, reads=(), writes=(), eng=None):
        eng = eng or self.anyop()
        if eng == "act":
            return self.op("act", lambda e: e.copy(out=out, in_=in_), reads, writes)
        return self.op(eng, lambda e: e.tensor_copy(out=out, in_=in_), reads, writes)

    def anyop(self):
        self._rr ^= 1
        return "act" if self._rr else "dve"

    def dma(self, q, out, in_, reads=(), writes=(), swrites=(), **kw):
        self._deps(q, reads, writes, swrites)
        i = self.dq_next[q]
        self.dq_next[q] = (i + 1) % len(self.dq[q])
        sid = self.dq[q][i]
        self._wait(q, sid, self.cnt[sid])
        ins = self.engs[q].dma_start(out=out, in_=in_, **kw)
        self.cnt[sid] += 16
        ins.then_inc(self.sems[sid], 16)
        ev = (sid, self.cnt[sid])
        self._commit(ev, reads, writes, swrites)
        self.n_ins += 1
        return ev

    def finish(self):
        for q in ("sp", "pool"):
            for sid in self.dq[q]:
                self._wait("sp", sid, self.cnt[sid])
        for e in ("pe", "act", "dve", "pool"):
            self._wait("sp", e, self.cnt[e])


class Rot:
    def __init__(self, k, name, shape, dt, n):
        self.items = [(k.sb("%s%d" % (name, i), shape, dt), k.buf("%s%d" % (name, i))) for i in range(n)]
        self.i = 0

    def next(self):
        it = self.items[self.i]
        self.i = (self.i + 1) % len(self.items)
        return it


def build_program():
    nc = bass.Bass("TRN2", target_bir_lowering=False)

    def din(name, shape, dt=F32):
        return nc.dram_tensor(name, list(shape), dt, kind="ExternalInput").ap()

    def dout(name, shape, dt=F32):
        return nc.dram_tensor(name, list(shape), dt, kind="ExternalOutput").ap()

    def dscr(name, shape, dt):
        return nc.dram_tensor(name, list(shape), dt, kind="ExternalOutput" if _DEBUG else "Internal").ap()

    xin = din("xin", [TOK, D])
    tmask = din("tmask", [128, NT])
    w_in = din("w_in", [D, INC])
    w_br_att = din("w_br_att", [SBW, D])
    w_br_ssd = din("w_br_ssd", [SSDW, D])
    w_out = din("w_out", [D, D])
    w_up = din("w_up", [D, 2 * DFF])
    w_down = din("w_down", [DFF, D])
    g_mix = din("g_mix", [1, D])
    g_ffn = din("g_ffn", [1, D])
    g_final = din("g_final", [1, D])
    g_ssd = din("g_ssd", [1, SSDW])
    cw_ssd = din("cw_ssd", [128, 24, 4])
    cb_ssd = din("cb_ssd", [128, 24])
    cw_ffn = din("cw_ffn", [128, 44, 3])
    cb_ffn = din("cb_ffn", [128, 44])
    dt_bias = din("dt_bias", [1, 32])
    a_log = din("a_log", [1, 32])
    d_skip = din("d_skip", [1, 32])
    cache_k = din("cache_k", [NSTREAM, NCACHE, SBW])
    cache_v = din("cache_v", [NSTREAM, NCACHE, SBW])
    st_ssm = din("st_ssm", [NSTREAM, SSDW, 128])
    st_sconv = din("st_sconv", [NSTREAM * 3, XBC])
    st_fconv = din("st_fconv", [NSTREAM * 2, 2 * DFF])
    c_ident = din("c_ident", [128, 128])
    c_attmask = din("c_attmask", [128, 4, 512])
    c_trineg = din("c_trineg", [128, 128])
    c_le = din("c_le", [128, 128])
    c_gt = din("c_gt", [128, 128])
    c_leS = din("c_leS", [128, 128])
    c_gtS = din("c_gtS", [128, 128])
    c_blkS = din("c_blkS", [128, 4, 128])

    y_out = dout("y_out", [TOKM, D])
    k_out = dout("k_out", [TOKM, SBW])
    v_out = dout("v_out", [TOKM, SBW])
    ssm_out = dout("ssm_out", [5, SSDW, 128])
    sconv_out = dout("sconv_out", [5, 24, 128, 3])
    fconv_out = dout("fconv_out", [5, 44, 128, 2])

    qT_s = dscr("qT_s", [16, 64, TOKM], BF16)
    kT_s = dscr("kT_s", [16, 64, TOK], BF16)
    v_s = dscr("v_s", [TOK, SBW], BF16)
    z_s = dscr("z_s", [TOKM, SSDW], F32)
    xc_s = dscr("xc_s", [24, 128, TOK], BF16)
    g_s = dscr("g_s", [16, 128, TOKM], F32)
    attT_s = dscr("attT_s", [16, 64, TOKM], BF16)
    ssdT_s = dscr("ssdT_s", [16, 128, TOKM], BF16)
    h_s = dscr("h_s", [TOKM, D], F32)
    mT_s = dscr("mT_s", [8, 128, TOKM], BF16)
    act_s = dscr("act_s", [22, 128, TOKM], BF16)

    with contextlib.ExitStack() as st:
        k = KB(nc, st)
        op, dma = k.op, k.dma
        B_qT, B_kT, B_v, B_z, B_xc, B_g, B_h, B_mT, B_act = [k.buf(n) for n in
                                                          ("qT_s", "kT_s", "v_s", "z_s", "xc_s", "g_s", "h_s", "mT_s", "act_s")]

        psbig = nc.alloc_psum_tensor("psbig", [128, 8 * 512], F32)
        PS = [(psbig[:, i * 512:(i + 1) * 512], k.buf("ps%d" % i)) for i in range(8)]
        psi = [0]

        def ps_next():
            it = PS[psi[0]]
            psi[0] = (psi[0] + 1) % 6
            return it

        psa = [0]

        def ps_acc_next():
            it = PS[6 + psa[0]]
            psa[0] ^= 1
            return it

        def const_tile(name, src, shape, dt=F32, q="sp"):
            t = k.sb(name, shape, dt)
            b = k.buf(name)
            dma("pool" if dt == BF16 else q, t[:], src, writes=[b])
            return t, b

        ident_f, b_identf = const_tile("ident_f", c_ident, [128, 128])
        ident_b, b_identb = const_tile("ident_b", c_ident, [128, 128], BF16)
        attmask, b_attmask = const_tile("attmask", c_attmask, [128, 4, 512], BF16)
        trineg, b_trineg = const_tile("trineg", c_trineg, [128, 128], BF16)
        onesneg = k.sb("onesneg", [128, 128], BF16)
        b_onesneg = k.buf("onesneg")
        op("pool", lambda e: e.memset(onesneg[:], -1.0), writes=[b_onesneg])
        le_f, b_lef = const_tile("le_f", c_le, [128, 128])
        le_b, b_leb = const_tile("le_b", c_le, [128, 128], BF16)
        gt_f, b_gtf = const_tile("gt_f", c_gt, [128, 128])
        leS_f, b_leSf = const_tile("leS_f", c_leS, [128, 128])
        leS_b, b_leSb = const_tile("leS_b", c_leS, [128, 128], BF16)
        gtS_f, b_gtSf = const_tile("gtS_f", c_gtS, [128, 128])
        blkS_f, b_blkSf = const_tile("blkS_f", c_blkS, [128, 4, 128])
        ones_f = k.sb("ones_f", [128, 128], F32)
        b_onesf = k.buf("ones_f")
        op("pool", lambda e: e.memset(ones_f[:], 1.0), writes=[b_onesf])
        cws, b_cws = const_tile("cws", cw_ssd, [128, 24, 4])
        cbs, b_cbs = const_tile("cbs", cb_ssd, [128, 24])
        cwf, b_cwf = const_tile("cwf", cw_ffn, [128, 44, 3])
        cbf, b_cbf = const_tile("cbf", cb_ffn, [128, 44])
        dtb_bc, b_dtb = const_tile("dtb_bc", dt_bias.partition_broadcast(128), [128, 32])
        a_bc, b_abc = const_tile("a_bc", a_log.partition_broadcast(128), [128, 32])
        dsk_bc, b_dsk = const_tile("dsk_bc", d_skip.partition_broadcast(128), [128, 32])
        tm, b_tm = const_tile("tm", tmask, [128, NT])
        op("act", lambda e: e.activation(out=a_bc[:], in_=a_bc[:], func=AF.Exp), reads=[b_abc], writes=[b_abc])
        op("dve", lambda e: e.tensor_scalar(out=a_bc[:], in0=a_bc[:], scalar1=-1.0, scalar2=None, op0=ALU.mult),
           reads=[b_abc], writes=[b_abc])

        small = Rot(k, "small", [128, 8], F32, 4)
        junk_f = k.sb("junk_f", [128, SSDW], F32)
        b_junk = k.buf("junk")
        dtm = k.sb("dtm", [128, NT, 32], F32)
        b_dtm = [k.buf("dtm%d" % i) for i in range(NT)]
        k.pstack = contextlib.ExitStack()

        def rmsnorm_to_bf16(x_t, b_x, gain, b_gain, out_bf, b_out, junk, b_junk, n=D):
            ss, b_ss = small.next()
            op("act", lambda e: e.activation(out=junk, in_=x_t, func=AF.Square, accum_out=ss[:, 0:1]),
               reads=[b_x], writes=[b_junk, b_ss])
            op("act", lambda e: e.activation(out=ss[:, 1:2], in_=ss[:, 0:1], func=AF.Sqrt, scale=1.0 / n, bias=EPS),
               reads=[b_ss], writes=[b_ss])
            op("dve", lambda e: e.reciprocal(out=ss[:, 2:3], in_=ss[:, 1:2]), reads=[b_ss], writes=[b_ss])
            op("dve", lambda e: e.scalar_tensor_tensor(out=out_bf, in0=x_t, scalar=ss[:, 2:3], in1=gain,
                                                       op0=ALU.mult, op1=ALU.mult),
               reads=[b_x, b_ss, b_gain], writes=[b_out])

        def transpose_to(src_bf, b_src, nchunk, dst_fn, b_dst):
            for c0 in range(0, nchunk, 8):
                n = min(8, nchunk - c0)
                pt, b_pt = ps_next()
                ptb = pt[:].bitcast(BF16)
                for j in range(n):
                    c = c0 + j
                    op("pe", lambda e, c=c, j=j: e.transpose(out=ptb[:, j * 128:(j + 1) * 128],
                                                             in_=src_bf[:, c * 128:(c + 1) * 128], identity=ident_b[:]),
                       reads=[b_src, b_identb], writes=[b_pt])
                eng = k.anyop()
                src = ptb[:, 0:n * 128].rearrange("p (c t) -> p c t", t=128)
                if eng == "act":
                    op("act", lambda e: e.copy(out=dst_fn(c0, n), in_=src), reads=[b_pt], writes=[b_dst])
                else:
                    op("dve", lambda e: e.tensor_copy(out=dst_fn(c0, n), in_=src), reads=[b_pt], writes=[b_dst])

        def body():
            gmix_bc, b_gmix = const_tile("gmix_bc", g_mix.partition_broadcast(128), [128, D])
            uT = k.sb("uT", [128, 8, TOK], BF16)
            b_uT = [k.buf("uT%d" % i) for i in range(NT)]
            xrot = Rot(k, "xrot", [128, D], F32, 2)
            ubrot = Rot(k, "ubrot", [128, D], BF16, 2)
            for i in range(NT):
                xt, b_xt = xrot.next()
                dma("sp", xt[:], xin[i * 128:(i + 1) * 128, :], writes=[b_xt])
                ub, b_ub = ubrot.next()
                rmsnorm_to_bf16(xt[:], b_xt, gmix_bc[:], b_gmix, ub[:], b_ub, junk_f[:, 0:D], b_junk)
                transpose_to(ub, b_ub, 8, lambda c0, n, i=i: uT[:, c0:c0 + n, i * 128:(i + 1) * 128], b_uT[i])

            checkpoint(0)
            wrot = Rot(k, "wrot", [128, 8, 512], BF16, 2)

            def load_w(src2d, c0, nc_):
                wt, b_wt = wrot.next()
                dma("pool", wt[:, :, 0:nc_], src2d[:, c0:c0 + nc_].rearrange("(kc p) c -> p kc c", p=128), writes=[b_wt])
                return wt, b_wt

            tiles_all = list(range(NT))
            tiles_m1 = list(range(NPRE, NT))

            def mloc(i):
                return i - NPRE

            ev32 = Rot(k, "ev32", [128, 512], F32, 3)
            ev16 = Rot(k, "ev16", [128, 512], BF16, 3)

            def tokmajor_block(wt, b_wt, ncol, tiles, consume):
                for i in tiles:
                    pt, b_pt = ps_next()
                    for kc in range(8):
                        op("pe", lambda e, kc=kc: e.matmul(pt[:, 0:ncol], lhsT=uT[:, kc, i * 128:(i + 1) * 128],
                                                           rhs=wt[:, kc, 0:ncol], start=(kc == 0), stop=(kc == 7)),
                           reads=[b_uT[i], b_wt], writes=[b_pt])
                    consume(i, pt, b_pt)

            def groups(tiles):
                gs = []
                cur = []
                for i in tiles:
                    if i == NT - 1 or (cur and (cur[-1] + 1 != i or len(cur) == 4)):
                        if cur:
                            gs.append(cur)
                        cur = []
                    cur.append(i)
                    if i == NT - 1:
                        gs.append(cur)
                        cur = []
                if cur:
                    gs.append(cur)
                return gs

            G_ALL = groups(tiles_all)
            G_M1 = groups(tiles_m1)

            class Deferred:
                def __init__(self, depth):
                    self.q, self.depth = [], depth

                def push(self, fn):
                    self.q.append(fn)
                    while len(self.q) > self.depth:
                        self.q.pop(0)()

                def flush(self):
                    while self.q:
                        self.q.pop(0)()

            def featmajor(wt, b_wt, wc0, m, grp, consume, tofs=0, defer=None):
                t0 = grp[0] * 128
                n = len(grp) * 128
                pt, b_pt = ps_next()
                for kc in range(8):
                    op("pe", lambda e, kc=kc: e.matmul(pt[0:m, 0:n], lhsT=wt[:, kc, wc0:wc0 + m],
                                                       rhs=uT[:, kc, t0 - tofs:t0 - tofs + n], start=(kc == 0), stop=(kc == 7)),
                       reads=[b_uT[i] for i in grp] + [b_wt], writes=[b_pt])
                if defer is not None:
                    defer.push(lambda: consume(grp, t0, n, pt, b_pt))
                else:
                    consume(grp, t0, n, pt, b_pt)

            for which in ("q", "k"):
                cbase = C_Q if which == "q" else C_K
                for blk in range(2):
                    wt, b_wt = load_w(w_in, cbase + blk * 512, 512)
                    for hp in range(4):
                        h = blk * 8 + hp * 2
                        for grp in (G_M1 if which == "q" else G_ALL):
                            def cons(grp, t0, n, pt, b_pt, h=h, which=which):
                                o, b_o = ev16.next()
                                if which == "q":
                                    op("act", lambda e: e.activation(out=o[:, 0:n], in_=pt[:, 0:n], func=AF.Copy,
                                                                     scale=0.125), reads=[b_pt], writes=[b_o])
                                    for two in range(2):
                                        dma("sp", qT_s[h + two, :, t0 - NPRE * 128:t0 - NPRE * 128 + n], o[two * 64:(two + 1) * 64, 0:n],
                                            reads=[b_o], swrites=[B_qT])
                                else:
                                    op("dve", lambda e: e.tensor_copy(out=o[:, 0:n], in_=pt[:, 0:n]),
                                       reads=[b_pt], writes=[b_o])
                                    for two in range(2):
                                        dma("sp", kT_s[h + two, :, t0:t0 + n], o[two * 64:(two + 1) * 64, 0:n], reads=[b_o], swrites=[B_kT])
                            featmajor(wt, b_wt, hp * 128, 128, grp, cons)

            checkpoint(0.1)
            for blk in range(2):
                wt, b_wt = load_w(w_in, C_K + blk * 512, 512)

                def cons_k(i, pt, b_pt, blk=blk):
                    o, b_o = ev32.next()
                    k.cp(o[:], pt[:], reads=[b_pt], writes=[b_o])
                    dma("sp", k_out[mloc(i) * 128:(mloc(i) + 1) * 128, blk * 512:(blk + 1) * 512], o[:], reads=[b_o])
                tokmajor_block(wt, b_wt, 512, tiles_m1, cons_k)
            for blk in range(2):
                wt, b_wt = load_w(w_in, C_V + blk * 512, 512)

                def cons_v(i, pt, b_pt, blk=blk):
                    o2, b_o2 = ev16.next()
                    if i >= NPRE:
                        o, b_o = ev32.next()
                        op("act", lambda e: e.copy(out=o[:], in_=pt[:]), reads=[b_pt], writes=[b_o])
                        dma("sp", v_out[mloc(i) * 128:(mloc(i) + 1) * 128, blk * 512:(blk + 1) * 512], o[:], reads=[b_o])
                        op("dve", lambda e: e.tensor_copy(out=o2[:], in_=o[:]), reads=[b_o], writes=[b_o2])
                    else:
                        op("dve", lambda e: e.tensor_copy(out=o2[:], in_=pt[:]), reads=[b_pt], writes=[b_o2])
                    dma("sp", v_s[i * 128:(i + 1) * 128, blk * 512:(blk + 1) * 512], o2[:], reads=[b_o2], swrites=[B_v])
                tokmajor_block(wt, b_wt, 512, tiles_all, cons_v)

            checkpoint(0.2)
            for blk in range(4):
                wt, b_wt = load_w(w_in, C_Z + blk * 512, 512)

                def cons_z(i, pt, b_pt, blk=blk):
                    o, b_o = ev32.next()
                    op("act", lambda e: e.activation(out=o[:], in_=pt[:], func=AF.Silu), reads=[b_pt], writes=[b_o])
                    dma("sp", z_s[mloc(i) * 128:(mloc(i) + 1) * 128, blk * 512:(blk + 1) * 512], o[:], reads=[b_o], swrites=[B_z])
                tokmajor_block(wt, b_wt, 512, tiles_m1, cons_z)

            checkpoint(0.3)
            wt, b_wt = load_w(w_in, C_DT, 32)

            def cons_dt(i, pt, b_pt):
                op("dve", lambda e: e.tensor_tensor(out=dtm[:, i, :], in0=pt[:, 0:32], in1=dtb_bc[:], op=ALU.add),
                   reads=[b_pt, b_dtb], writes=[b_dtm[i]])
                op("act", lambda e: e.activation(out=dtm[:, i, :], in_=dtm[:, i, :], func=AF.Exp),
                   reads=[b_dtm[i]], writes=[b_dtm[i]])
                op("act", lambda e: e.activation(out=dtm[:, i, :], in_=dtm[:, i, :], func=AF.Ln, bias=1.0),
                   reads=[b_dtm[i]], writes=[b_dtm[i]])
                op("dve", lambda e: e.tensor_scalar(out=dtm[:, i, :], in0=dtm[:, i, :], scalar1=tm[:, i:i + 1], scalar2=None,
                                                    op0=ALU.mult), reads=[b_dtm[i], b_tm], writes=[b_dtm[i]])
            tokmajor_block(wt, b_wt, 32, tiles_all, cons_dt)

            checkpoint(0.4)
            for blk in range(4):
                wt, b_wt = load_w(w_in, C_G + blk * 512, 512)
                for cc in range(4):
                    c = blk * 4 + cc
                    for grp in G_M1:
                        def cons_g(grp, t0, n, pt, b_pt, c=c):
                            o, b_o = ev32.next()
                            op("act", lambda e: e.activation(out=o[:, 0:n], in_=pt[:, 0:n], func=AF.Sigmoid),
                               reads=[b_pt], writes=[b_o])
                            dma("sp", g_s[c, :, t0 - NPRE * 128:t0 - NPRE * 128 + n], o[:, 0:n], reads=[b_o], swrites=[B_g])
                        featmajor(wt, b_wt, cc * 128, 128, grp, cons_g)

            checkpoint(0.5)
            sch = k.sb("sch", [128, 24, 12], F32)
            b_sch = k.buf("sch")
            s12r = Rot(k, "s12", [12, 512], F32, 2)
            for c0 in range(0, 24, 4):
                s12, b_s12 = s12r.next()
                dma("sp", s12[:], st_sconv[:, c0 * 128:(c0 + 4) * 128], writes=[b_s12])
                pt, b_pt = ps_next()
                for j in range(4):
                    op("pe", lambda e, j=j: e.transpose(out=pt[:, j * 12:(j + 1) * 12], in_=s12[:, j * 128:(j + 1) * 128],
                                                        identity=ident_f[0:12, 0:12]), reads=[b_s12, b_identf], writes=[b_pt])
                op("dve", lambda e: e.tensor_copy(out=sch[:, c0:c0 + 4, :], in_=pt[:, 0:48].rearrange("p (c t) -> p c t", t=12)),
                   reads=[b_pt], writes=[b_sch])
            checkpoint(0.6)
            xraw = Rot(k, "xraw", [128, 4, 3 + 128], F32, 3)
            xacc = Rot(k, "xacc", [128, 512], F32, 2)
            dq_x = Deferred(1)
            sconv_sb = k.sb("sconv_sb", [128, 24, 5, 3], F32)
            b_sconv = k.buf("sconv_sb")
            for blk in range(6):
                wt, b_wt = load_w(w_in, C_X + blk * 512, 512)
                for cc in range(4):
                    c = blk * 4 + cc
                    prev = [None]
                    for grp in G_ALL:
                        def cons_x(grp, t0, n, pt, b_pt, c=c, prev=prev):
                            xr, b_xr = xraw.next()
                            issample = grp[0] == NT - 1
                            if not issample:
                                flat = xr[:].rearrange("p s t -> p (s t)")
                                if prev[0] is None:
                                    op("pool", lambda e: e.memset(flat[:, 0:3], 0.0), writes=[b_xr])
                                else:
                                    pf, b_pf, pn = prev[0]
                                    op("pool", lambda e: e.tensor_copy(out=flat[:, 0:3], in_=pf[:, pn:pn + 3]),
                                       reads=[b_pf], writes=[b_xr])
                                op("act", lambda e: e.copy(out=flat[:, 3:3 + n], in_=pt[:, 0:n]), reads=[b_pt], writes=[b_xr])
                                prev[0] = (flat, b_xr, n)
                                if grp[-1] == NPT - 1:
                                    op("pool", lambda e: e.tensor_copy(out=sconv_sb[:, c, 0, :], in_=flat[:, n:n + 3]),
                                       reads=[b_xr], writes=[b_sconv])
                                acc, b_acc = xacc.next()
                                op("act", lambda e: e.activation(out=acc[:, 0:n], in_=pt[:, 0:n], func=AF.Identity,
                                                                 scale=cws[:, c, 3:4], bias=cbs[:, c:c + 1]),
                                   reads=[b_pt, b_cws, b_cbs], writes=[b_acc])
                                for i in range(0, 3):
                                    op("dve", lambda e, i=i: e.scalar_tensor_tensor(out=acc[:, 0:n], in0=flat[:, i:i + n],
                                                                                    scalar=cws[:, c, i:i + 1], in1=acc[:, 0:n],
                                                                                    op0=ALU.mult, op1=ALU.add),
                                       reads=[b_xr, b_cws, b_acc], writes=[b_acc])
                                o, b_o = ev16.next()
                                op("act", lambda e: e.activation(out=o[:, 0:n], in_=acc[:, 0:n], func=AF.Silu),
                                   reads=[b_acc], writes=[b_o])
                                dma("sp", xc_s[c, :, t0:t0 + n], o[:, 0:n], reads=[b_o], swrites=[B_xc])
                            else:
                                op("pool", lambda e: e.tensor_copy(out=xr[:, :, 0:3],
                                                                   in_=sch[:, c, :].rearrange("p (s t) -> p s t", t=3)),
                                   reads=[b_sch], writes=[b_xr])
                                op("act", lambda e: e.copy(out=xr[:, :, 3:35], in_=pt[:, 0:128].rearrange("p (s t) -> p s t", t=32)),
                                   reads=[b_pt], writes=[b_xr])
                                op("pool", lambda e: e.tensor_copy(out=sconv_sb[:, c, 1:5, :], in_=xr[:, :, 16:19]),
                                   reads=[b_xr], writes=[b_sconv])
                                acc, b_acc = xacc.next()
                                a3 = acc[:, 0:128].rearrange("p (s t) -> p s t", t=32)
                                op("act", lambda e: e.activation(out=acc[:, 0:128], in_=pt[:, 0:128], func=AF.Identity,
                                                                 scale=cws[:, c, 3:4], bias=cbs[:, c:c + 1]),
                                   reads=[b_pt, b_cws, b_cbs], writes=[b_acc])
                                for i in range(0, 3):
                                    op("dve", lambda e, i=i: e.scalar_tensor_tensor(out=a3, in0=xr[:, :, i:i + 32],
                                                                                    scalar=cws[:, c, i:i + 1], in1=a3,
                                                                                    op0=ALU.mult, op1=ALU.add),
                                       reads=[b_xr, b_cws, b_acc], writes=[b_acc])
                                o, b_o = ev16.next()
                                op("act", lambda e: e.activation(out=o[:, 0:128], in_=acc[:, 0:128], func=AF.Silu),
                                   reads=[b_acc], writes=[b_o])
                                dma("sp", xc_s[c, :, t0:t0 + 128], o[:, 0:128], reads=[b_o], swrites=[B_xc])
                        featmajor(wt, b_wt, cc * 128, 128, grp, cons_x, defer=dq_x)
            dq_x.flush()
            for q5 in range(5):
                dma("sp", sconv_out[q5].rearrange("c p t -> p c t"), sconv_sb[:, :, q5, :], reads=[b_sconv])

            checkpoint(1)
            k.new_phase()
            b_attT = k.buf("attT_s")
            def alloc_att_bufs():
                return dict(e=Rot(k, "e_rot", [128, 1024], F32, 4), sp=Rot(k, "sp_rot", [128, 1024], BF16, 5),
                            x=Rot(k, "x_rot", [128, 1024], F32, 2), w=Rot(k, "w_rot", [128, 1024], BF16, 3),
                            acc=Rot(k, "acc_rot", [128, 1024], BF16, 3), ao=Rot(k, "ao_rot", [128, 512], BF16, 2),
                            spf=(k.sb("spf", [128, 512], BF16), k.buf("spf")))
            AB = alloc_att_bufs()
            ppair = [0]

            def ps_pair_next():
                b0 = 2 * ppair[0]
                ppair[0] = (ppair[0] + 1) % 3
                return b0

            def sb_pipeline(blocks, W, b_mask):
                n = len(blocks)
                partial_first = blocks[0]["jn"] < 128
                units = []
                i = 0
                if partial_first:
                    units.append([0])
                    i = 1
                while i < n:
                    if i + 1 < n:
                        units.append([i, i + 1])
                        i += 2
                    else:
                        units.append([i])
                        i += 1
                nu = len(units)
                U = [dict() for _ in units]
                SPV = {}
                ACC = {}
                spf, b_spf = AB["spf"]

                def psv(b0, ns, jn):
                    return psbig[0:jn, b0 * 512:(b0 + ns) * 512].rearrange("p (s c) -> p s c", c=512)[:, :, 0:W]

                def sbv(t, ns, jn):
                    return t[0:jn, 0:ns * W].rearrange("p (s c) -> p s c", c=W)

                def stage_a(u):
                    idx, c = units[u], U[u]
                    ns = len(idx)
                    jn = blocks[idx[0]]["jn"]
                    b0 = ps_pair_next()
                    bufs = [PS[b0 + s_][1] for s_ in range(ns)]
                    for s_, bi in enumerate(idx):
                        blocks[bi]["zmm"](PS[b0 + s_][0], PS[b0 + s_][1], True)
                    e_t, b_e = AB["e"].next()
                    op("act", lambda e: e.activation(out=sbv(e_t, ns, jn), in_=psv(b0, ns, jn), func=AF.Exp), reads=bufs, writes=[b_e])
                    if u == 0 and partial_first:
                        sp_t, b_sp = spf, b_spf
                    else:
                        sp_t, b_sp = AB["sp"].next()
                    op("act", lambda e: e.activation(out=sp_t[0:jn, 0:ns * W], in_=e_t[0:jn, 0:ns * W], func=AF.Ln, bias=1.0),
                       reads=[b_e], writes=[b_sp])
                    for s_, bi in enumerate(idx):
                        bl = blocks[bi]
                        if bl["mask"] is not None:
                            v_ = bl["mv"](sp_t[0:jn, s_ * W:(s_ + 1) * W])
                            op("pool", lambda e, v_=v_, bl=bl: e.tensor_tensor(out=v_, in0=v_, in1=bl["mask"], op=ALU.mult),
                               reads=[b_sp, b_mask], writes=[b_sp])
                        SPV[bi] = (sp_t[:, s_ * W:(s_ + 1) * W], b_sp)
                    c.update(e=(e_t, b_e), ns=ns, jn=jn)

                def stage_b1_acc(u):
                    idx = units[u]
                    first_full = 1 if partial_first else 0
                    a_t = None
                    for s_, bi in enumerate(idx):
                        nfull_before = bi - first_full
                        if nfull_before <= 0:
                            ACC[bi] = None
                        elif nfull_before == 1:
                            ACC[bi] = SPV[bi - 1]
                        else:
                            if a_t is None:
                                a_t, b_a = AB["acc"].next()
                            pa_ap, b_pa = ACC[bi - 1]
                            ps_ap, b_ps = SPV[bi - 1]
                            o_ap = a_t[:, s_ * W:(s_ + 1) * W]
                            op("dve", lambda e, o_ap=o_ap, pa_ap=pa_ap, ps_ap=ps_ap: e.tensor_tensor(out=o_ap, in0=pa_ap, in1=ps_ap, op=ALU.add),
                               reads=[b_pa, b_ps], writes=[b_a])
                            ACC[bi] = (o_ap, b_a)

                def stage_b1(u):
                    idx, c = units[u], U[u]
                    ns, jn = c["ns"], c["jn"]
                    b0 = ps_pair_next()
                    tl = []
                    for s_, bi in enumerate(idx):
                        sp_ap, b_sp = SPV[bi]
                        terms = [(trineg[0:jn, 0:jn], sp_ap[0:jn, :], [b_trineg, b_sp])]
                        if partial_first and bi > 0:
                            j0 = blocks[0]["jn"]
                            terms.append((onesneg[0:j0, 0:jn], spf[0:j0, 0:W], [b_onesneg, b_spf]))
                        if ACC[bi] is not None:
                            terms.append((onesneg[:, 0:jn], ACC[bi][0], [b_onesneg, ACC[bi][1]]))
                        tl.append(terms)
                    for ti in range(max(len(t_) for t_ in tl)):
                        for s_, terms in enumerate(tl):
                            if ti < len(terms):
                                l_, r_, rd = terms[ti]
                                ap_, b_ap = PS[b0 + s_]
                                op("pe", lambda e, l_=l_, r_=r_, ti=ti, ap_=ap_, nt=len(terms): e.matmul(ap_[0:jn, 0:W], lhsT=l_, rhs=r_, start=(ti == 0),
                                                                                                         stop=(ti == nt - 1)), reads=rd, writes=[b_ap])
                    c["apb"] = b0

                def stage_b2a(u):
                    idx, c = units[u], U[u]
                    ns, jn, b0 = c["ns"], c["jn"], c["apb"]
                    e_t, b_e = c["e"]
                    x_t, b_x = AB["x"].next()
                    op("act", lambda e: e.activation(out=sbv(x_t, ns, jn), in_=psv(b0, ns, jn), func=AF.Exp),
                       reads=[PS[b0 + s_][1] for s_ in range(ns)], writes=[b_x])
                    w_t, b_w = AB["w"].next()
                    op("dve", lambda e: e.tensor_tensor(out=w_t[0:jn, 0:ns * W], in0=x_t[0:jn, 0:ns * W], in1=e_t[0:jn, 0:ns * W], op=ALU.mult),
                       reads=[b_x, b_e], writes=[b_w])
                    for s_, bi in enumerate(idx):
                        bl = blocks[bi]
                        if bl["mask"] is not None:
                            v_ = bl["mv"](w_t[0:jn, s_ * W:(s_ + 1) * W])
                            op("pool", lambda e, v_=v_, bl=bl: e.tensor_tensor(out=v_, in0=v_, in1=bl["mask"], op=ALU.mult),
                               reads=[b_w, b_mask], writes=[b_w])
                    c["w"] = (w_t, b_w)

                def stage_b2b(u):
                    w_t, b_w = U[u]["w"]
                    for s_, bi in enumerate(units[u]):
                        blocks[bi]["pv"](w_t[:, s_ * W:(s_ + 1) * W], b_w, bi == 0, bi == n - 1)

                L1, L2, L3 = 1, 2, 3
                for s_ in range(nu + L3):
                    if 0 <= s_ - L1 < nu:
                        stage_b1_acc(s_ - L1)
                    if 0 <= s_ - L2 < nu:
                        stage_b2a(s_ - L2)
                    if s_ < nu:
                        stage_a(s_)
                    if 0 <= s_ - L1 < nu:
                        stage_b1(s_ - L1)
                    if 0 <= s_ - L3 < nu:
                        stage_b2b(s_ - L3)

            ident_mv = lambda a: a
            kTh_rot = Rot(k, "kTh", [128, NPT * 128], BF16, 2)
            vh_rot = Rot(k, "vh", [128, NPT, 128], BF16, 2)
            qTh_rot = Rot(k, "qTh", [128, NMAIN * 128], BF16, 2)
            for (t_, b_) in kTh_rot.items + qTh_rot.items:
                op("pool", lambda e, t_=t_: e.memset(t_[64:128, :], 0.0), writes=[b_])
            pfh = {}
            vcur = [None]

            def load_head(h):
                kTh, b_kTh = kTh_rot.next()
                dma("sp", kTh[0:64, :], kT_s[h, :, 0:NPT * 128], reads=[B_kT], swrites=[b_kTh])
                if h % 2 == 0:
                    vh, b_vh = vh_rot.next()
                    dma("sp", vh[:], v_s[0:NPT * 128, h * 64:(h + 2) * 64].rearrange("(kt p) d -> p kt d", p=128), reads=[B_v], writes=[b_vh])
                    vcur[0] = (vh, b_vh)
                qTh, b_qTh = qTh_rot.next()
                dma("sp", qTh[0:64, :], qT_s[h, :, 0:NMAIN * 128], reads=[B_qT], swrites=[b_qTh])
                pfh[h] = (kTh, b_kTh, qTh, b_qTh) + vcur[0]

            load_head(0)
            for h in range(16):
                hh2 = h % 2
                if h + 1 < 16:
                    load_head(h + 1)
                kTh, b_kTh, qTh, b_qTh, vh, b_vh = pfh.pop(h)
                for (g0, nq) in ((0, 4), (4, 4), (8, 3), (11, 3), (14, 3)):
                    W = nq * 128
                    q0 = g0 * 128
                    op_, b_op = ps_acc_next()
                    blocks = []
                    for kt in range(NPRE + g0 + nq - 1, -1, -1):
                        r = kt - (NPRE + g0)

                        def zmm(ps, b_ps, final, kt=kt, kTh=kTh, qTh=qTh, b_kTh=b_kTh, b_qTh=b_qTh, W=W, q0=q0):
                            op("pe", lambda e: e.matmul(ps[:, 0:W], lhsT=kTh[:, kt * 128:(kt + 1) * 128], rhs=qTh[:, q0:q0 + W],
                                                        start=True, stop=final), reads=[b_kTh, b_qTh], writes=[b_ps])

                        def pv(w_t, b_w, first, last, kt=kt, vh=vh, b_vh=b_vh, op_=op_, b_op=b_op, W=W):
                            op("pe", lambda e: e.matmul(op_[:, 0:W], lhsT=vh[:, kt, :], rhs=w_t, start=first, stop=last),
                               reads=[b_vh, b_w], writes=[b_op])
                        blocks.append(dict(jn=128, zmm=zmm, mask=(attmask[:, r, 0:W] if r >= 0 else None), mv=ident_mv, pv=pv))
                    sb_pipeline(blocks, W, b_attmask)
                    ao, b_ao = AB["ao"].next()
                    rs = slice(hh2 * 64, (hh2 + 1) * 64)
                    op("dve", lambda e, rs=rs: e.tensor_copy(out=ao[rs, 0:W], in_=op_[rs, 0:W]), reads=[b_op], writes=[b_ao])
                    dma("sp", attT_s[h, :, q0:q0 + W], ao[rs, 0:W], reads=[b_ao], swrites=[b_attT])

            checkpoint(2)
            k.new_phase()
            AB = alloc_att_bufs()
            kc_bf = k.sb("kc_bf", [128, 9, SBW], BF16)
            b_kc = k.buf("kc_bf")
            vc_bf = k.sb("vc_bf", [128, 9, SBW], BF16)
            b_vc = k.buf("vc_bf")
            kTc = k.sb("kTc", [128, 16, 9 * 128], BF16)
            b_kTc = k.buf("kTc")
            kTn = k.sb("kTn", [128, 16, 32], BF16)
            b_kTn = k.buf("kTn")
            vn = k.sb("vn", [32, SBW], BF16)
            b_vn = k.buf("vn")
            qs = k.sb("qs", [128, 16, 32], BF16)
            b_qs = k.buf("qs")
            for (t_, b_) in ((kTc, b_kTc), (kTn, b_kTn), (qs, b_qs)):
                op("pool", lambda e, t_=t_: e.memset(t_[64:128, :, :], 0.0), writes=[b_])
            smask = attmask[0:32, 0, 0:32].unsqueeze(1).to_broadcast([32, 16, 32])
            smv = lambda a: a.rearrange("p (h t) -> p h t", t=32)
            S0 = NMAIN * 128
            for b in range(NSTREAM):
                op("pool", lambda e: e.memset(kc_bf[0:96, 0, :], 0.0), writes=[b_kc])
                op("pool", lambda e: e.memset(vc_bf[0:96, 0, :], 0.0), writes=[b_vc])
                op("pool", lambda e: e.memset(kc_bf[96:128, 0, :], 0.0), writes=[b_kc])
                op("pool", lambda e: e.memset(vc_bf[96:128, 0, :], 0.0), writes=[b_vc])
                dma("pool", kc_bf[112:128, 0, :], cache_k[b, 0:16, :], writes=[b_kc])
                dma("pool", vc_bf[112:128, 0, :], cache_v[b, 0:16, :], writes=[b_vc])
                for half in range(2):
                    dma("pool", kc_bf[:, 1 + half * 4:5 + half * 4, :],
                        cache_k[b, 16 + half * 512:16 + (half + 1) * 512, :].rearrange("(kt p) d -> p kt d", p=128), swrites=[b_kc])
                    dma("pool", vc_bf[:, 1 + half * 4:5 + half * 4, :],
                        cache_v[b, 16 + half * 512:16 + (half + 1) * 512, :].rearrange("(kt p) d -> p kt d", p=128), swrites=[b_vc])
                for kt in range(9):
                    for h0 in (0, 8):
                        pt, b_pt = ps_next()
                        ptb = pt[:].bitcast(BF16)
                        for j in range(8):
                            op("pe", lambda e, j=j: e.transpose(out=ptb[0:64, j * 128:(j + 1) * 128],
                                                                in_=kc_bf[:, kt, (h0 + j) * 64:(h0 + j + 1) * 64], identity=ident_b[:]),
                               reads=[b_kc, b_identb], writes=[b_pt])
                        k.cp(kTc[0:64, h0:h0 + 8, kt * 128:(kt + 1) * 128], ptb[0:64, :].rearrange("p (h t) -> p h t", t=128),
                             reads=[b_pt], writes=[b_kTc])
                c0 = S0 + b * 32
                dma("sp", kTn[0:64, :, :], kT_s[:, :, NPT * 128 + b * 32:NPT * 128 + b * 32 + 32].rearrange("h d t -> d h t"), reads=[B_kT], writes=[b_kTn])
                dma("sp", qs[0:64, :, :], qT_s[:, :, c0:c0 + 32].rearrange("h d t -> d h t"), reads=[B_qT], writes=[b_qs])
                dma("sp", vn[:], v_s[NPT * 128 + b * 32:NPT * 128 + b * 32 + 32, :], reads=[B_v], writes=[b_vn])
                op_, b_op = ps_acc_next()
                blocks = []
                pvfirst = [True]
                for kt in ["new"] + list(range(8, -1, -1)):
                    jn = 32 if kt == "new" else 128

                    def zmm(ps, b_ps, final, kt=kt, jn=jn):
                        for h in range(16):
                            lhsT = kTn[:, h, :] if kt == "new" else kTc[:, h, kt * 128:(kt + 1) * 128]
                            op("pe", lambda e, h=h, lhsT=lhsT: e.matmul(ps[0:jn, h * 32:(h + 1) * 32], lhsT=lhsT, rhs=qs[:, h, :],
                                                                        start=(h == 0), stop=(final and h == 15), skip_group_check=True),
                               reads=[b_kTn if kt == "new" else b_kTc, b_qs], writes=[b_ps])

                    def pv(w_t, b_w, first, last, kt=kt, jn=jn, op_=op_, b_op=b_op):
                        for h in range(16):
                            pr = (h // 2) * 128
                            lhsT = vn[:, pr:pr + 128] if kt == "new" else vc_bf[:, kt, pr:pr + 128]
                            st_ = pvfirst[0]
                            pvfirst[0] = False
                            op("pe", lambda e, h=h, lhsT=lhsT, st_=st_: e.matmul(op_[:, h * 32:(h + 1) * 32], lhsT=lhsT,
                                                                                 rhs=w_t[0:jn, h * 32:(h + 1) * 32], start=st_,
                                                                                 stop=(last and h == 15), skip_group_check=True),
                               reads=[b_vn if kt == "new" else b_vc, b_w], writes=[b_op])
                    blocks.append(dict(jn=jn, zmm=zmm, mask=(smask if kt == "new" else None), mv=smv, pv=pv))
                sb_pipeline(blocks, 512, b_attmask)
                ao, b_ao = AB["ao"].next()
                op("dve", lambda e: e.tensor_copy(out=ao[:, :], in_=op_[:, :]), reads=[b_op], writes=[b_ao])
                for two in range(2):
                    dma("sp", attT_s[two::2, :, c0:c0 + 32].rearrange("h d t -> d h t"),
                        ao[two * 64:(two + 1) * 64, :].rearrange("d (hp x t) -> d hp x t", x=2, t=32)[:, :, two, :],
                        reads=[b_ao], swrites=[b_attT])

            checkpoint(3)
            k.new_phase()
            b_ssdT = k.buf("ssdT_s")
            gssd_bc, b_gssd = const_tile("gssd_bc", g_ssd.partition_broadcast(128), [128, SSDW])
            HT = k.sb("HT", [128, SSDW], F32)
            b_HT = k.buf("HT")
            HTb = k.sb("HTb", [128, SSDW], BF16)
            b_HTb = k.buf("HTb")
            HS = [(k.sb("HS%d" % b, [128, SSDW], F32), k.buf("HS%d" % b), k.sb("HSb%d" % b, [128, SSDW], BF16), k.buf("HSb%d" % b))
                  for b in range(NSTREAM)]
            op("pool", lambda e: e.memset(HT[:], 0.0), writes=[b_HT])
            op("pool", lambda e: e.memset(HTb[:], 0.0), writes=[b_HTb])
            hld = k.sb("hld", [128, 16, 128], F32)
            b_hld = k.buf("hld")
            xc_rot = Rot(k, "xc_rot", [128, 24, 128], BF16, 2)
            da_rot = Rot(k, "da_rot", [128, 32], F32, 2)
            LM = k.sb("LM", [128, 32, 128], BF16)
            b_LM = k.buf("LM")
            dec_rot = Rot(k, "dec_rot", [128, 6, 32], F32, 2)
            dmk_rot = Rot(k, "dmk_rot", [128, 4, 2, 32], F32, 1)
            xdt = k.sb("xdt", [128, SSDW], BF16)
            b_xdt = k.buf("xdt")
            xtok = k.sb("xtok", [128, SSDW], BF16)
            b_xtok = k.buf("xtok")
            xw = k.sb("xw", [128, SSDW], BF16)
            b_xw = k.buf("xw")
            Btok = k.sb("Btok", [128, 512], BF16)
            b_Btok = k.buf("Btok")
            CBm = k.sb("CBm", [128, 4, 128], F32)
            b_CBm = k.buf("CBm")
            expL_rot = Rot(k, "expL", [128, 512], F32, 2)
            MT_rot = Rot(k, "MT", [128, 512], BF16, 2)
            ysb = k.sb("ysb", [128, SSDW], F32)
            b_ysb = k.buf("ysb")
            zt_rot = Rot(k, "zt_rot", [128, SSDW], F32, 1)
            ynb = k.sb("ynb", [128, SSDW], BF16)
            b_ynb = k.buf("ynb")
            sT_rot = Rot(k, "sT_rot", [128, 16, 128], BF16, 1)
            hout, b_hout = hld, b_hld

            def load_state(b):
                H, b_H, Hb, b_Hb = HS[b]
                dma("sp", hld[:], st_ssm[b].rearrange("(c p) n -> p c n", p=128), writes=[b_hld])
                for c0 in range(0, 16, 4):
                    pt, b_pt = ps_next()
                    for j in range(4):
                        op("pe", lambda e, j=j: e.transpose(out=pt[:, j * 128:(j + 1) * 128], in_=hld[:, c0 + j, :], identity=ident_f[:]),
                           reads=[b_hld, b_identf], writes=[b_pt])
                    op("dve", lambda e: e.tensor_copy(out=H[:, c0 * 128:(c0 + 4) * 128], in_=pt[:]), reads=[b_pt], writes=[b_H])
                    op("act", lambda e: e.copy(out=Hb[:, c0 * 128:(c0 + 4) * 128], in_=H[:, c0 * 128:(c0 + 4) * 128]), reads=[b_H], writes=[b_Hb])

            def store_state(H, b_H, q5):
                for c0 in range(0, 16, 4):
                    pt, b_pt = ps_next()
                    for j in range(4):
                        op("pe", lambda e, j=j: e.transpose(out=pt[:, j * 128:(j + 1) * 128], in_=H[:, (c0 + j) * 128:(c0 + j + 1) * 128],
                                                            identity=ident_f[:]), reads=[b_H, b_identf], writes=[b_pt])
                    op("dve", lambda e: e.tensor_copy(out=hout[:, c0:c0 + 4, :], in_=pt[:].rearrange("p (c n) -> p c n", n=128)),
                       reads=[b_pt], writes=[b_hout])
                dma("sp", ssm_out[q5].rearrange("(c p) n -> p c n", p=128), hout[:], reads=[b_hout])

            xcs = {}

            def load_xc(i):
                xc, b_xc = xc_rot.next()
                dma("sp", xc[:], xc_s[:, :, i * 128:(i + 1) * 128].rearrange("c p t -> p c t"), reads=[B_xc], writes=[b_xc])
                xcs[i] = (xc, b_xc)

            for i in range(NT):
                issample = i == NT - 1
                need_y = i >= NPRE
                if issample:
                    store_state(HT, b_HT, 0)
                    for b in range(NSTREAM):
                        load_state(b)
                    blocks = [(b * 32, 32, HS[b]) for b in range(NSTREAM)]
                    m_le_f, bm_lef, m_le_b, bm_leb, m_gt_f, bm_gtf = leS_f, b_leSf, leS_b, b_leSb, gtS_f, b_gtSf
                else:
                    blocks = [(0, 128, (HT, b_HT, HTb, b_HTb))]
                    m_le_f, bm_lef, m_le_b, bm_leb, m_gt_f, bm_gtf = le_f, b_lef, le_b, b_leb, gt_f, b_gtf
                nb = len(blocks)
                if i == 0:
                    load_xc(0)
                if i + 1 < NT:
                    load_xc(i + 1)
                xc, b_xc = xcs.pop(i)
                if need_y:
                    zt, b_zt = zt_rot.next()
                    dma("sp", zt[:], z_s[mloc(i) * 128:(mloc(i) + 1) * 128, :], reads=[B_z], writes=[b_zt])
                xsT = lambda c: xc[:, c, :]
                BT = lambda g: xc[:, 16 + g, :]
                CT = lambda g: xc[:, 20 + g, :]
                da, b_da = da_rot.next()
                op("dve", lambda e: e.tensor_tensor(out=da[:], in0=dtm[:, i, :], in1=a_bc[:], op=ALU.mult),
                   reads=[b_dtm[i], b_abc], writes=[b_da])
                if need_y:
                    op("pool", lambda e: e.tensor_tensor(out=LM[:], in0=m_gt_f[:].unsqueeze(1).to_broadcast([128, 32, 128]),
                                                         in1=da[:].unsqueeze(2).to_broadcast([128, 32, 128]), op=ALU.mult),
                       reads=[bm_gtf, b_da], writes=[b_LM])
                pd, b_pd = ps_next()
                op("pe", lambda e: e.matmul(pd[:, 0:32], lhsT=m_le_f[:], rhs=da[:], start=True, stop=True),
                   reads=[bm_lef, b_da], writes=[b_pd])
                op("pe", lambda e: e.matmul(pd[:, 32:64], lhsT=m_gt_f[:], rhs=da[:], start=True, stop=True),
                   reads=[bm_gtf, b_da], writes=[b_pd])
                for bi in range(nb):
                    lhsT = blkS_f[:, bi, :] if issample else ones_f[:]
                    op("pe", lambda e, bi=bi, lhsT=lhsT: e.matmul(pd[:, 64 + bi * 32:96 + bi * 32], lhsT=lhsT, rhs=da[:], start=True, stop=True),
                       reads=[b_blkSf, b_onesf, b_da], writes=[b_pd])
                dec, b_dec = dec_rot.next()
                op("act", lambda e: e.activation(out=dec[:, 0:2 + nb, :], in_=pd[:, 0:64 + nb * 32].rearrange("p (a h) -> p a h", h=32),
                                                 func=AF.Exp), reads=[b_pd], writes=[b_dec])
                decay_t = dec[:, 0, :]
                toend = dec[:, 1, :]
                for half in range(2):
                    pt, b_pt = ps_next()
                    ptb = pt[:].bitcast(BF16)
                    for j in range(8):
                        c = half * 8 + j
                        op("pe", lambda e, j=j, c=c: e.transpose(out=ptb[:, j * 128:(j + 1) * 128], in_=xsT(c), identity=ident_b[:]),
                           reads=[b_xc, b_identb], writes=[b_pt])
                    sl = slice(half * 1024, (half + 1) * 1024)
                    op("act", lambda e: e.copy(out=xtok[:, sl], in_=ptb[:, :]), reads=[b_pt], writes=[b_xtok])
                    op("dve", lambda e: e.tensor_tensor(out=xdt[:, sl].rearrange("p (h d) -> p h d", d=64),
                                                        in0=xtok[:, sl].rearrange("p (h d) -> p h d", d=64),
                                                        in1=dtm[:, i, half * 16:(half + 1) * 16].unsqueeze(2).to_broadcast([128, 16, 64]),
                                                        op=ALU.mult), reads=[b_xtok, b_dtm[i]], writes=[b_xdt])
                pt, b_pt = ps_next()
                ptb = pt[:].bitcast(BF16)
                for g in range(4):
                    op("pe", lambda e, g=g: e.transpose(out=ptb[:, g * 128:(g + 1) * 128], in_=BT(g), identity=ident_b[:]),
                       reads=[b_xc, b_identb], writes=[b_pt])
                op("act", lambda e: e.copy(out=Btok[:], in_=ptb[:, 0:512]), reads=[b_pt], writes=[b_Btok])
                dmk, b_dmk = dmk_rot.next()
                if issample:
                    for bi in range(nb):
                        op("pool", lambda e, bi=bi: e.tensor_tensor(out=dmk[:, bi, :, :], in0=dec[:, 0:2, :],
                                                                    in1=blkS_f[:, bi, 0:32].unsqueeze(1).to_broadcast([128, 2, 32]),
                                                                    op=ALU.mult), reads=[b_dec, b_blkSf], writes=[b_dmk])
                if need_y:
                    pc, b_pc = ps_next()
                    for g in range(4):
                        op("pe", lambda e, g=g: e.matmul(pc[:, g * 128:(g + 1) * 128], lhsT=BT(g), rhs=CT(g), start=True, stop=True),
                           reads=[b_xc], writes=[b_pc])
                    op("dve", lambda e: e.tensor_tensor(out=CBm[:], in0=pc[:].rearrange("p (g t) -> p g t", t=128),
                                                        in1=m_le_f[:].unsqueeze(1).to_broadcast([128, 4, 128]), op=ALU.mult),
                       reads=[b_pc, bm_lef], writes=[b_CBm])
                    for g in range(4):
                        for bi, (p0, pn, (H, b_H, Hb, b_Hb)) in enumerate(blocks):
                            po, b_po = ps_next()
                            op("pe", lambda e, Hb=Hb: e.matmul(po[:], lhsT=CT(g), rhs=Hb[:, g * 512:(g + 1) * 512], start=True, stop=True),
                               reads=[b_xc, b_Hb], writes=[b_po])
                            dcy = dmk[:, bi, 0, :] if issample else decay_t
                            ysl = ysb[:, g * 512:(g + 1) * 512].rearrange("p (h d) -> p h d", d=64)
                            if bi == 0:
                                op("dve", lambda e, g=g, dcy=dcy, ysl=ysl: e.tensor_tensor(
                                    out=ysl, in0=po[:].rearrange("p (h d) -> p h d", d=64),
                                    in1=dcy[:, g * 8:(g + 1) * 8].unsqueeze(2).to_broadcast([128, 8, 64]), op=ALU.mult),
                                   reads=[b_po, b_dec, b_dmk], writes=[b_ysb])
                            else:
                                jv = junk_f[:, 0:512].rearrange("p (h d) -> p h d", d=64)
                                op("dve", lambda e, g=g, dcy=dcy, jv=jv: e.tensor_tensor(
                                    out=jv, in0=po[:].rearrange("p (h d) -> p h d", d=64),
                                    in1=dcy[:, g * 8:(g + 1) * 8].unsqueeze(2).to_broadcast([128, 8, 64]), op=ALU.mult),
                                   reads=[b_po, b_dec, b_dmk], writes=[b_junk])
                                op("dve", lambda e, g=g: e.tensor_tensor(out=ysb[:, g * 512:(g + 1) * 512], in0=ysb[:, g * 512:(g + 1) * 512],
                                                                         in1=junk_f[:, 0:512], op=ALU.add),
                                   reads=[b_junk, b_ysb], writes=[b_ysb])
                for bi, (p0, pn, (H, b_H, Hb, b_Hb)) in enumerate(blocks):
                    dH = dec[:, 2 + bi, :]
                    te = dmk[:, bi, 1, :] if issample else toend
                    op("pool" if need_y else "dve", lambda e, te=te: e.tensor_tensor(out=xw[:].rearrange("p (h d) -> p h d", d=64),
                                                                in0=xdt[:].rearrange("p (h d) -> p h d", d=64),
                                                                in1=te.unsqueeze(2).to_broadcast([128, 32, 64]), op=ALU.mult),
                       reads=[b_xdt, b_dec, b_dmk], writes=[b_xw])
                    for g in range(4):
                        pu, b_pu = ps_next()
                        op("pe", lambda e, g=g: e.matmul(pu[:], lhsT=Btok[:, g * 128:(g + 1) * 128],
                                                         rhs=xw[:, g * 512:(g + 1) * 512], start=True, stop=True),
                           reads=[b_Btok, b_xw], writes=[b_pu])
                        op("pool" if need_y else "dve", lambda e, g=g: e.tensor_tensor(out=H[:, g * 512:(g + 1) * 512].rearrange("p (h d) -> p h d", d=64),
                                                                  in0=H[:, g * 512:(g + 1) * 512].rearrange("p (h d) -> p h d", d=64),
                                                                  in1=dH[:, g * 8:(g + 1) * 8].unsqueeze(2).to_broadcast([128, 8, 64]),
                                                                  op=ALU.mult), reads=[b_H, b_dec], writes=[b_H])
                        op("dve", lambda e, g=g: e.tensor_tensor(out=H[:, g * 512:(g + 1) * 512], in0=pu[:], in1=H[:, g * 512:(g + 1) * 512],
                                                                 op=ALU.add), reads=[b_pu, b_H], writes=[b_H])
                    if not issample:
                        op("act", lambda e: e.copy(out=Hb[:], in_=H[:]), reads=[b_H], writes=[b_Hb])
                    else:
                        store_state(H, b_H, 1 + bi)
                if need_y:
                    for g in range(4):
                        py, b_py = ps_acc_next()
                        for hb in range(2):
                            h0 = g * 8 + hb * 4
                            pL, b_pL = ps_next()
                            for j in range(4):
                                op("pe", lambda e, j=j: e.matmul(pL[:, j * 128:(j + 1) * 128], lhsT=LM[:, h0 + j, :], rhs=m_le_b[:],
                                                                 start=True, stop=True), reads=[b_LM, bm_leb], writes=[b_pL])
                            eL, b_eL = expL_rot.next()
                            op("act", lambda e: e.activation(out=eL[:], in_=pL[:], func=AF.Exp), reads=[b_pL], writes=[b_eL])
                            MT, b_MT = MT_rot.next()
                            op("dve", lambda e, g=g: e.tensor_tensor(out=MT[:].rearrange("p (h t) -> p h t", t=128),
                                                                     in0=eL[:].rearrange("p (h t) -> p h t", t=128),
                                                                     in1=CBm[:, g, :].unsqueeze(1).to_broadcast([128, 4, 128]),
                                                                     op=ALU.mult), reads=[b_eL, b_CBm], writes=[b_MT])
                            for j in range(4):
                                hh = hb * 4 + j
                                op("pe", lambda e, j=j, hh=hh: e.matmul(py[:, hh * 64:(hh + 1) * 64], lhsT=MT[:, j * 128:(j + 1) * 128],
                                                                        rhs=xdt[:, (h0 + j) * 64:(h0 + j + 1) * 64], start=True, stop=True),
                                   reads=[b_MT, b_xdt], writes=[b_py])
                        op("dve", lambda e, g=g: e.tensor_tensor(out=ysb[:, g * 512:(g + 1) * 512], in0=py[:], in1=ysb[:, g * 512:(g + 1) * 512],
                                                                 op=ALU.add), reads=[b_py, b_ysb], writes=[b_ysb])
                    op("pool", lambda e: e.tensor_tensor(out=junk_f[:].rearrange("p (h d) -> p h d", d=64),
                                                         in0=xtok[:].rearrange("p (h d) -> p h d", d=64),
                                                         in1=dsk_bc[:].unsqueeze(2).to_broadcast([128, 32, 64]), op=ALU.mult),
                       reads=[b_xtok, b_dsk], writes=[b_junk])
                    op("dve", lambda e: e.tensor_tensor(out=ysb[:], in0=ysb[:], in1=junk_f[:], op=ALU.add),
                       reads=[b_ysb, b_junk], writes=[b_ysb])
                    op("dve", lambda e: e.tensor_tensor(out=ysb[:], in0=ysb[:], in1=zt[:], op=ALU.mult),
                       reads=[b_ysb, b_zt], writes=[b_ysb])
                    ss, b_ss = small.next()
                    for g in range(4):
                        op("act", lambda e, g=g: e.activation(out=junk_f[:, g * 512:(g + 1) * 512], in_=ysb[:, g * 512:(g + 1) * 512],
                                                              func=AF.Square, accum_out=ss[:, g:g + 1]),
                           reads=[b_ysb], writes=[b_junk, b_ss])
                    op("act", lambda e: e.activation(out=ss[:, 0:4], in_=ss[:, 0:4], func=AF.Sqrt, scale=1.0 / 512, bias=EPS),
                       reads=[b_ss], writes=[b_ss])
                    op("dve", lambda e: e.reciprocal(out=ss[:, 4:8], in_=ss[:, 0:4]), reads=[b_ss], writes=[b_ss])
                    op("dve", lambda e: e.tensor_tensor(out=ysb[:].rearrange("p (g c) -> p g c", c=512),
                                                        in0=ysb[:].rearrange("p (g c) -> p g c", c=512),
                                                        in1=ss[:, 4:8].unsqueeze(2).to_broadcast([128, 4, 512]), op=ALU.mult),
                       reads=[b_ysb, b_ss], writes=[b_ysb])
                    op("dve", lambda e: e.tensor_tensor(out=ynb[:], in0=ysb[:], in1=gssd_bc[:], op=ALU.mult),
                       reads=[b_ysb, b_gssd], writes=[b_ynb])
                    sT, b_sT = sT_rot.next()
                    transpose_to(ynb, b_ynb, 16, lambda c0, n: sT[:, c0:c0 + n, :], b_sT)
                    dma("sp", ssdT_s[:, :, mloc(i) * 128:(mloc(i) + 1) * 128].rearrange("c p t -> p c t"), sT[:], reads=[b_sT],
                        swrites=[b_ssdT])

            checkpoint(4)
            k.new_phase()
            wba = k.sb("wba", [128, 8, D], BF16)
            b_wba = k.buf("wba")
            wbs = k.sb("wbs", [128, 16, D], BF16)
            b_wbs = k.buf("wbs")
            for hf in range(2):
                dma("pool", wba[:, hf * 4:(hf + 1) * 4, :], w_br_att[hf * 512:(hf + 1) * 512, :].rearrange("(h p) c -> p h c", p=128), swrites=[b_wba])
            for q4 in range(4):
                dma("pool", wbs[:, q4 * 4:(q4 + 1) * 4, :], w_br_ssd[q4 * 512:(q4 + 1) * 512, :].rearrange("(c p) n -> p c n", p=128), swrites=[b_wbs])
            aT_rot = Rot(k, "aT_rot", [128, 8, 512], BF16, 2)
            sTg_rot = Rot(k, "sTg_rot", [128, 16, 512], BF16, 2)
            gT_rot = Rot(k, "gT_rot", [128, 2, 512], F32, 2)
            mo_rot = Rot(k, "mo_rot", [128, 512], BF16, 2)
            t1_rot = Rot(k, "t1_rot", [128, 512], F32, 4)
            pfa = {}

            def load5a(gi):
                grp = G_M1[gi]
                t0 = (grp[0] - NPRE) * 128
                n = len(grp) * 128
                aT, b_aT = aT_rot.next()
                for two in range(2):
                    dma("sp", aT[two * 64:(two + 1) * 64, :, 0:n], attT_s[two::2, :, t0:t0 + n].rearrange("h d t -> d h t"),
                        reads=[b_attT], swrites=[b_aT])
                sTg, b_sTg = sTg_rot.next()
                dma("sp", sTg[:, :, 0:n], ssdT_s[:, :, t0:t0 + n].rearrange("c p t -> p c t"), reads=[b_ssdT], writes=[b_sTg])
                pfa[gi] = (aT, b_aT, sTg, b_sTg)

            load5a(0)
            for gi, grp in enumerate(G_M1):
                t0 = (grp[0] - NPRE) * 128
                n = len(grp) * 128
                if gi + 1 < len(G_M1):
                    load5a(gi + 1)
                aT, b_aT, sTg, b_sTg = pfa.pop(gi)
                for j in range(8):
                    gT, b_gT = gT_rot.next()
                    dma("sp", gT[:, 0, 0:n], g_s[j, :, t0:t0 + n], reads=[B_g], writes=[b_gT])
                    dma("sp", gT[:, 1, 0:n], g_s[8 + j, :, t0:t0 + n], reads=[B_g], writes=[b_gT])
                    pa, b_pa = ps_next()
                    for h in range(8):
                        op("pe", lambda e, h=h: e.matmul(pa[:, 0:n], lhsT=wba[:, h, j * 128:(j + 1) * 128], rhs=aT[:, h, 0:n],
                                                         start=(h == 0), stop=(h == 7)), reads=[b_wba, b_aT], writes=[b_pa])
                    pb, b_pb = ps_next()
                    for c in range(16):
                        op("pe", lambda e, c=c: e.matmul(pb[:, 0:n], lhsT=wbs[:, c, j * 128:(j + 1) * 128], rhs=sTg[:, c, 0:n],
                                                         start=(c == 0), stop=(c == 15)), reads=[b_wbs, b_sTg], writes=[b_pb])
                    t1, b_t1 = t1_rot.next()
                    op("dve", lambda e: e.tensor_tensor(out=t1[:, 0:n], in0=pa[:, 0:n], in1=gT[:, 0, 0:n], op=ALU.mult),
                       reads=[b_pa, b_gT], writes=[b_t1])
                    t2, b_t2 = t1_rot.next()
                    op("dve", lambda e: e.tensor_tensor(out=t2[:, 0:n], in0=pb[:, 0:n], in1=gT[:, 1, 0:n], op=ALU.mult),
                       reads=[b_pb, b_gT], writes=[b_t2])
                    mo, b_mo = mo_rot.next()
                    op("pool", lambda e: e.tensor_tensor(out=mo[:, 0:n], in0=t1[:, 0:n], in1=t2[:, 0:n], op=ALU.add),
                       reads=[b_t1, b_t2], writes=[b_mo])
                    dma("sp", mT_s[j, :, t0:t0 + n], mo[:, 0:n], reads=[b_mo], swrites=[B_mT])

            checkpoint(5)
            k.new_phase()
            gffn_bc, b_gffn = const_tile("gffn_bc", g_ffn.partition_broadcast(128), [128, D])
            wo = k.sb("wo", [128, 8, D], BF16)
            b_wo = k.buf("wo")
            for hf in range(2):
                dma("pool", wo[:, hf * 4:(hf + 1) * 4, :], w_out[hf * 512:(hf + 1) * 512, :].rearrange("(c p) n -> p c n", p=128), swrites=[b_wo])
            uT = k.sb("u2T", [128, 8, TOKM], BF16)
            b_uT = {i: k.buf("u2T%d" % i) for i in tiles_m1}
            mT_rot = Rot(k, "mT_rot", [128, 8, 128], BF16, 2)
            xrot = Rot(k, "xrot2", [128, D], F32, 2)
            ubrot = Rot(k, "ubrot2", [128, D], BF16, 2)
            hrot = Rot(k, "hrot", [128, D], F32, 2)
            pf5 = {}

            def load5(i):
                ml = mloc(i)
                mT, b_mT = mT_rot.next()
                dma("sp", mT[:], mT_s[:, :, ml * 128:(ml + 1) * 128].rearrange("c p t -> p c t"), reads=[B_mT], writes=[b_mT])
                xt, b_xt = xrot.next()
                dma("sp", xt[:], xin[i * 128:(i + 1) * 128, :], writes=[b_xt])
                pf5[i] = (mT, b_mT, xt, b_xt)

            load5(tiles_m1[0])
            for ii5, i in enumerate(tiles_m1):
                ml = mloc(i)
                if ii5 + 1 < len(tiles_m1):
                    load5(tiles_m1[ii5 + 1])
                mT, b_mT, xt, b_xt = pf5.pop(i)
                ht, b_ht = hrot.next()
                for hf in range(2):
                    ph, b_ph = ps_next()
                    for kc in range(8):
                        op("pe", lambda e, kc=kc: e.matmul(ph[:], lhsT=mT[:, kc, :], rhs=wo[:, kc, hf * 512:(hf + 1) * 512],
                                                           start=(kc == 0), stop=(kc == 7)), reads=[b_mT, b_wo], writes=[b_ph])
                    op("dve", lambda e, hf=hf: e.tensor_tensor(out=ht[:, hf * 512:(hf + 1) * 512], in0=ph[:], in1=xt[:, hf * 512:(hf + 1) * 512],
                                                               op=ALU.add), reads=[b_ph, b_xt], writes=[b_ht])
                dma("sp", h_s[ml * 128:(ml + 1) * 128, :], ht[:], reads=[b_ht], swrites=[B_h])
                ub, b_ub = ubrot.next()
                rmsnorm_to_bf16(ht[:], b_ht, gffn_bc[:], b_gffn, ub[:], b_ub, junk_f[:, 0:D], b_junk)
                transpose_to(ub, b_ub, 8, lambda c0, n_, ml=ml: uT[:, c0:c0 + n_, ml * 128:(ml + 1) * 128], b_uT[i])

            fch = k.sb("fch", [128, 44, 8], F32)
            b_fch = k.buf("fch")
            f8r = Rot(k, "f8", [8, 512], F32, 2)
            for c0 in range(0, 44, 4):
                f8, b_f8 = f8r.next()
                dma("sp", f8[:], st_fconv[:, c0 * 128:(c0 + 4) * 128], writes=[b_f8])
                pt, b_pt = ps_next()
                for j in range(4):
                    op("pe", lambda e, j=j: e.transpose(out=pt[:, j * 8:(j + 1) * 8], in_=f8[:, j * 128:(j + 1) * 128],
                                                        identity=ident_f[0:8, 0:8]), reads=[b_f8, b_identf], writes=[b_pt])
                op("dve", lambda e: e.tensor_copy(out=fch[:, c0:c0 + 4, :], in_=pt[:, 0:32].rearrange("p (c t) -> p c t", t=8)),
                   reads=[b_pt], writes=[b_fch])
            fconv_sb = k.sb("fconv_sb", [128, 44, 5, 2], F32)
            b_fconv = k.buf("fconv_sb")
            ev16 = Rot(k, "ev16b", [128, 512], BF16, 2)
            fraw = Rot(k, "fraw", [128, 4, 2 + 128], F32, 6)
            facc = Rot(k, "facc", [128, 512], F32, 4)
            dq_f = Deferred(3)
            wrot2 = Rot(k, "wrot2", [128, 8, 256], BF16, 2)
            for c in range(22):
                wt, b_wt = wrot2.next()
                dma("pool", wt[:, :, 0:128], w_up[:, c * 128:(c + 1) * 128].rearrange("(kc p) c -> p kc c", p=128), writes=[b_wt])
                dma("pool", wt[:, :, 128:256], w_up[:, DFF + c * 128:DFF + (c + 1) * 128].rearrange("(kc p) c -> p kc c", p=128), writes=[b_wt])
                prev = {0: None, 1: None}
                for grp in G_M1:
                    accs = []
                    for part in range(2):
                        ch = c + part * 22

                        def cons_f(grp, t0, n, pt, b_pt, ch=ch, part=part, prev=prev, accs=accs):
                            xr, b_xr = fraw.next()
                            acc, b_acc = facc.next()
                            if grp[0] != NT - 1:
                                flat = xr[:].rearrange("p s t -> p (s t)")
                                if prev[part] is None:
                                    op("pool", lambda e: e.memset(flat[:, 0:2], 0.0), writes=[b_xr])
                                else:
                                    pf, b_pf, pn = prev[part]
                                    op("pool", lambda e: e.tensor_copy(out=flat[:, 0:2], in_=pf[:, pn:pn + 2]), reads=[b_pf], writes=[b_xr])
                                op("act", lambda e: e.copy(out=flat[:, 2:2 + n], in_=pt[:, 0:n]), reads=[b_pt], writes=[b_xr])
                                prev[part] = (flat, b_xr, n)
                                if grp[-1] == NPT - 1:
                                    op("pool", lambda e: e.tensor_copy(out=fconv_sb[:, ch, 0, :], in_=flat[:, n:n + 2]),
                                       reads=[b_xr], writes=[b_fconv])
                                op("act", lambda e: e.activation(out=acc[:, 0:n], in_=pt[:, 0:n], func=AF.Identity,
                                                                 scale=cwf[:, ch, 2:3], bias=cbf[:, ch:ch + 1]),
                                   reads=[b_pt, b_cwf, b_cbf], writes=[b_acc])
                                for ii in range(0, 2):
                                    op("dve", lambda e, ii=ii: e.scalar_tensor_tensor(out=acc[:, 0:n], in0=flat[:, ii:ii + n],
                                                                                      scalar=cwf[:, ch, ii:ii + 1], in1=acc[:, 0:n],
                                                                                      op0=ALU.mult, op1=ALU.add),
                                       reads=[b_xr, b_cwf, b_acc], writes=[b_acc])
                            else:
                                op("pool", lambda e: e.tensor_copy(out=xr[:, :, 0:2], in_=fch[:, ch, :].rearrange("p (s t) -> p s t", t=2)),
                                   reads=[b_fch], writes=[b_xr])
                                op("act", lambda e: e.copy(out=xr[:, :, 2:34], in_=pt[:, 0:128].rearrange("p (s t) -> p s t", t=32)),
                                   reads=[b_pt], writes=[b_xr])
                                op("pool", lambda e: e.tensor_copy(out=fconv_sb[:, ch, 1:5, :], in_=xr[:, :, 16:18]),
                                   reads=[b_xr], writes=[b_fconv])
                                a3 = acc[:, 0:128].rearrange("p (s t) -> p s t", t=32)
                                op("act", lambda e: e.activation(out=acc[:, 0:128], in_=pt[:, 0:128], func=AF.Identity,
                                                                 scale=cwf[:, ch, 2:3], bias=cbf[:, ch:ch + 1]),
                                   reads=[b_pt, b_cwf, b_cbf], writes=[b_acc])
                                for ii in range(0, 2):
                                    op("dve", lambda e, ii=ii: e.scalar_tensor_tensor(out=a3, in0=xr[:, :, ii:ii + 32],
                                                                                      scalar=cwf[:, ch, ii:ii + 1], in1=a3,
                                                                                      op0=ALU.mult, op1=ALU.add),
                                       reads=[b_xr, b_cwf, b_acc], writes=[b_acc])
                            accs.append((acc, b_acc))
                        featmajor(wt, b_wt, part * 128, 128, grp, cons_f, tofs=NPRE * 128, defer=dq_f)

                    def fin_f(accs=accs, grp=grp, c=c):
                        (ga, b_ga), (va, b_va) = accs
                        n = len(grp) * 128
                        tl = (grp[0] - NPRE) * 128
                        op("act", lambda e: e.activation(out=ga[:, 0:n], in_=ga[:, 0:n], func=AF.Silu), reads=[b_ga], writes=[b_ga])
                        ao_, b_ao_ = ev16.next()
                        op("pool", lambda e: e.tensor_tensor(out=ao_[:, 0:n], in0=ga[:, 0:n], in1=va[:, 0:n], op=ALU.mult),
                           reads=[b_ga, b_va], writes=[b_ao_])
                        dma("sp", act_s[c, :, tl:tl + n], ao_[:, 0:n], reads=[b_ao_], swrites=[B_act])
                    dq_f.push(fin_f)
            dq_f.flush()
            for q5 in range(5):
                dma("sp", fconv_out[q5].rearrange("c p t -> p c t"), fconv_sb[:, :, q5, :], reads=[b_fconv])

            checkpoint(6)
            k.new_phase()
            gfin_bc, b_gfin = const_tile("gfin_bc", g_final.partition_broadcast(128), [128, D])
            hrot = Rot(k, "hrot7", [128, D], F32, 2)
            aTl_rot = Rot(k, "aTl", [128, 22, 128], BF16, 2)
            wd = k.sb("wd", [128, 22, D], BF16)
            b_wd = k.buf("wd")
            for c0 in range(0, 22, 4):
                n4 = min(4, 22 - c0)
                dma("pool", wd[:, c0:c0 + n4, :], w_down[c0 * 128:(c0 + n4) * 128, :].rearrange("(c p) n -> p c n", p=128), swrites=[b_wd])
            yrot = Rot(k, "yrot", [128, D], F32, 2)
            pf7 = {}

            def load7(i):
                ml = mloc(i)
                ht, b_ht = hrot.next()
                dma("sp", ht[:], h_s[ml * 128:(ml + 1) * 128, :], reads=[B_h], writes=[b_ht])
                aTl, b_aTl = aTl_rot.next()
                dma("sp", aTl[:], act_s[:, :, ml * 128:(ml + 1) * 128].rearrange("c p t -> p c t"), reads=[B_act], writes=[b_aTl])
                pf7[i] = (ht, b_ht, aTl, b_aTl)

            load7(tiles_m1[0])
            for ii7, i in enumerate(tiles_m1):
                ml = mloc(i)
                if ii7 + 1 < len(tiles_m1):
                    load7(tiles_m1[ii7 + 1])
                ht, b_ht, aTl, b_aTl = pf7.pop(i)
                for hf in range(2):
                    pdn, b_pdn = ps_next()
                    for c in range(22):
                        op("pe", lambda e, c=c: e.matmul(pdn[:], lhsT=aTl[:, c, :], rhs=wd[:, c, hf * 512:(hf + 1) * 512],
                                                         start=(c == 0), stop=(c == 21)), reads=[b_aTl, b_wd], writes=[b_pdn])
                    op("dve", lambda e, hf=hf: e.tensor_tensor(out=ht[:, hf * 512:(hf + 1) * 512], in0=pdn[:], in1=ht[:, hf * 512:(hf + 1) * 512],
                                                               op=ALU.add), reads=[b_pdn, b_ht], writes=[b_ht])
                yt, b_yt = yrot.next()
                rmsnorm_to_bf16(ht[:], b_ht, gfin_bc[:], b_gfin, yt[:], b_yt, junk_f[:, 0:D], b_junk)
                dma("sp", y_out[ml * 128:(ml + 1) * 128, :], yt[:], reads=[b_yt])

        try:
            body()
        except _Stop:
            pass
        k.finish()
        k.pstack.close()
        print("program: instructions=%d waits=%d" % (k.n_ins, k.n_wait))
    return nc


def _consts():
    j = np.arange(128)
    strict = (j[:, None] < j[None, :]).astype(np.float32)
    attmask = np.zeros((128, 4, 512), np.float32)
    for r in range(4):
        attmask[:, r, r * 128:(r + 1) * 128] = strict
        attmask[:, r, (r + 1) * 128:] = 1.0
    trineg = -(j[:, None] >= j[None, :]).astype(np.float32)
    le = (j[:, None] <= j[None, :]).astype(np.float32)
    gt = (j[:, None] > j[None, :]).astype(np.float32)
    blk = j // 32
    same = (blk[:, None] == blk[None, :]).astype(np.float32)
    blkS = np.zeros((128, 4, 128), np.float32)
    for b in range(4):
        blkS[blk == b, b, :] = 1.0
    return {"c_ident": np.eye(128, dtype=np.float32), "c_attmask": attmask, "c_trineg": trineg, "c_le": le, "c_gt": gt,
            "c_leS": le * same, "c_gtS": gt * same, "c_blkS": blkS}


_NC_CACHE = {}


def kernel(x_prompt, x_sample, cache_k, cache_v, state_ssm, state_ssm_conv, state_ffn_conv, meta_tokens,
           g_mix, w_in, conv_w_ssd, conv_b_ssd, dt_bias, a_log, d_skip, g_ssd_norm, w_br_att, w_br_ssd,
           w_out, g_ffn, w_up, conv_w_ffn, conv_b_ffn, w_down, g_final):
    f = lambda a: np.ascontiguousarray(np.asarray(a, dtype=np.float32))
    x_prompt, x_sample, cache_k, cache_v = f(x_prompt), f(x_sample), f(cache_k), f(cache_v)
    state_ssm, state_ssm_conv, state_ffn_conv, meta_tokens = f(state_ssm), f(state_ssm_conv), f(state_ffn_conv), f(meta_tokens)
    if "nc" not in _NC_CACHE:
        _NC_CACHE["nc"] = build_program()
    nc = _NC_CACHE["nc"]
    consts = _consts()
    shared = {
        "w_in": f(w_in[0]), "w_br_att": f(w_br_att[0]), "w_br_ssd": f(w_br_ssd[0]), "w_out": f(w_out[0]),
        "w_up": f(w_up[0]), "w_down": f(w_down[0]),
        "g_mix": f(g_mix[0]).reshape(1, D), "g_ffn": f(g_ffn[0]).reshape(1, D), "g_final": f(g_final).reshape(1, D),
        "g_ssd": f(g_ssd_norm[0]).reshape(1, SSDW),
        "cw_ssd": f(f(conv_w_ssd[0]).reshape(4, 24, 128).transpose(2, 1, 0)),
        "cb_ssd": f(f(conv_b_ssd[0]).reshape(24, 128).T),
        "cw_ffn": f(f(conv_w_ffn[0]).reshape(3, 44, 128).transpose(2, 1, 0)),
        "cb_ffn": f(f(conv_b_ffn[0]).reshape(44, 128).T),
        "dt_bias": f(dt_bias[0]).reshape(1, 32), "a_log": f(a_log[0]).reshape(1, 32), "d_skip": f(d_skip[0]).reshape(1, 32),
    }
    shared.update(consts)
    in_maps = []
    for c in range(8):
        b, half = c // 2, c % 2
        seq = np.zeros((33 * 128, D), np.float32)
        seq[112:128] = meta_tokens
        seq[128:] = x_prompt[b]
        smask = np.zeros((33 * 128,), np.float32)
        smask[112:] = 1.0
        xin = np.zeros((TOK, D), np.float32)
        mask = np.zeros((TOK,), np.float32)
        if half == 0:
            xin[NPRE * 128:NPT * 128] = seq[0:NMAIN * 128]
            mask[NPRE * 128:NPT * 128] = smask[0:NMAIN * 128]
        else:
            xin[0:NPRE * 128] = seq[0:NPRE * 128]
            mask[0:NPRE * 128] = smask[0:NPRE * 128]
            xin[NPRE * 128:NPT * 128] = seq[NPRE * 128:33 * 128]
            mask[NPRE * 128:NPT * 128] = smask[NPRE * 128:33 * 128]
        for s in range(NSTREAM):
            r0 = NPT * 128 + s * 32
            xin[r0:r0 + 16] = x_sample[c * NSTREAM + s]
            mask[r0:r0 + 16] = 1.0
        m = dict(shared)
        m["xin"] = xin
        m["tmask"] = f(mask.reshape(NT, 128).T)
        sl = slice(c * NSTREAM, (c + 1) * NSTREAM)
        m["cache_k"] = f(cache_k[0, sl].reshape(NSTREAM, NCACHE, SBW))
        m["cache_v"] = f(cache_v[0, sl].reshape(NSTREAM, NCACHE, SBW))
        m["st_ssm"] = f(state_ssm[0, sl].reshape(NSTREAM, SSDW, 128))
        m["st_sconv"] = f(state_ssm_conv[0, sl].reshape(NSTREAM * 3, XBC))
        m["st_fconv"] = f(state_ffn_conv[0, sl].reshape(NSTREAM * 2, 2 * DFF))
        in_maps.append(m)
    res = run_bass_kernel_spmd(nc, in_maps, core_ids=list(range(8)))
    R = res.results
    _LAST["R"] = R
    _LAST["in_maps"] = in_maps

    BATCH, SEQ, DB, DS = 4, 4096, 32, 16
    y_prompt = np.zeros((BATCH, SEQ, D), np.float32)
    y_sample = np.zeros((DB, DS, D), np.float32)
    k_prompt = np.zeros((1, BATCH, 16 + SEQ, 16, 64), np.float32)
    v_prompt = np.zeros_like(k_prompt)
    ssm_prompt = np.zeros((1, BATCH, 32, 64, 128), np.float32)
    sconv_prompt = np.zeros((1, BATCH, 3, XBC), np.float32)
    fconv_prompt = np.zeros((1, BATCH, 2, 2 * DFF), np.float32)
    k_sample = np.zeros((1, DB, DS, 16, 64), np.float32)
    v_sample = np.zeros_like(k_sample)
    ssm_sample = np.zeros((1, DB, 32, 64, 128), np.float32)
    sconv_sample = np.zeros((1, DB, 3, XBC), np.float32)
    fconv_sample = np.zeros((1, DB, 2, 2 * DFF), np.float32)
    for c in range(8):
        b, half = c // 2, c % 2
        r = R[c]
        yo, ko, vo = r["y_out"], r["k_out"], r["v_out"]
        kp = k_prompt[0, b].reshape(16 + SEQ, SBW)
        vp = v_prompt[0, b].reshape(16 + SEQ, SBW)
        if half == 0:
            y_prompt[b, 0:16 * 128] = yo[128:17 * 128]
            kp[0:16] = ko[112:128]
            vp[0:16] = vo[112:128]
            kp[16:16 + 16 * 128] = ko[128:17 * 128]
            vp[16:16 + 16 * 128] = vo[128:17 * 128]
        else:
            y_prompt[b, 16 * 128:] = yo[128:17 * 128]
            kp[16 + 16 * 128:] = ko[128:17 * 128]
            vp[16 + 16 * 128:] = vo[128:17 * 128]
            ssm_prompt[0, b] = r["ssm_out"][0].reshape(32, 64, 128)
            sconv_prompt[0, b] = r["sconv_out"][0].reshape(XBC, 3).T
            fconv_prompt[0, b] = r["fconv_out"][0].reshape(2 * DFF, 2).T
        for s in range(NSTREAM):
            gi = c * NSTREAM + s
            r0 = NMAIN * 128 + s * 32
            y_sample[gi] = yo[r0:r0 + 16]
            k_sample[0, gi] = ko[r0:r0 + 16].reshape(16, 16, 64)
            v_sample[0, gi] = vo[r0:r0 + 16].reshape(16, 16, 64)
            ssm_sample[0, gi] = r["ssm_out"][1 + s].reshape(32, 64, 128)
            sconv_sample[0, gi] = r["sconv_out"][1 + s].reshape(XBC, 3).T
            fconv_sample[0, gi] = r["fconv_out"][1 + s].reshape(2 * DFF, 2).T
    return (y_prompt, y_sample, k_prompt, v_prompt, ssm_prompt, sconv_prompt, fconv_prompt,
            k_sample, v_sample, ssm_sample, sconv_sample, fconv_sample)
```

```python
import contextlib
import os
import numpy as np
import concourse.bass as bass
import concourse.mybir as mybir
from concourse.bass_utils import run_bass_kernel_spmd

F32 = mybir.dt.float32
BF16 = mybir.dt.bfloat16
AF = mybir.ActivationFunctionType
ALU = mybir.AluOpType

D = 1024
NPRE = 16
NMAIN = 17
NPT = NPRE + NMAIN
NT = NPT + 1
NM1 = NMAIN + 1
TOK = NT * 128
TOKM = NM1 * 128
SBW = 1024
SSDW = 2048
XBC = 3072
DFF = 2816
C_Q, C_K, C_V, C_Z, C_X, C_DT, C_G = 0, 1024, 2048, 3072, 5120, 8192, 8224
INC = 10272
EPS = 1e-6
NSTREAM = 4
NCACHE = 1040


_STOP = float(os.environ.get("MK_STOP", "99"))
_DEBUG = int(os.environ.get("MK_DEBUG", "0"))
_LAST = {}


class _Stop(Exception):
    pass


def checkpoint(n):
    if _STOP <= n:
        raise _Stop()


class Buf:
    __slots__ = ("name", "lw", "rd", "wr")

    def __init__(self, name):
        self.name = name
        self.lw = None
        self.rd = {}
        self.wr = {}


class KB:
    def __init__(self, nc, stack, n_dma_sp=48, n_dma_pool=32):
        self.nc = nc
        self.engs = {"pe": nc.tensor, "act": nc.scalar, "dve": nc.vector, "pool": nc.gpsimd, "sp": nc.sync}
        self.sems = {}
        self.cnt = {}
        for e in ("pe", "act", "dve", "pool"):
            self.sems[e] = stack.enter_context(nc.semaphore("s_" + e))
            self.cnt[e] = 0
        self.dq = {"sp": [], "pool": []}
        for q, n in (("sp", n_dma_sp), ("pool", n_dma_pool)):
            for i in range(n):
                sid = "d_%s_%d" % (q, i)
                self.sems[sid] = stack.enter_context(nc.semaphore(sid))
                self.cnt[sid] = 0
                self.dq[q].append(sid)
        self.dq_next = {"sp": 0, "pool": 0}
        self.seen = {e: {} for e in self.engs}
        self.n_ins = 0
        self.n_wait = 0
        self._nb = 0
        self._rr = 0
        self.pstack = stack

    def sb(self, name, shape, dt):
        self._nm = getattr(self, "_nm", 0) + 1
        return self.pstack.enter_context(self.nc.sbuf_tensor("%s_%d" % (name, self._nm), list(shape), dt))

    def barrier(self):
        for e in self.engs:
            for sid, c in self.cnt.items():
                self._wait(e, sid, c)

    def new_phase(self):
        self.barrier()
        self.pstack.close()
        self.pstack = contextlib.ExitStack()

    def buf(self, name=None):
        self._nb += 1
        return Buf(name or ("b%d" % self._nb))

    def _wait(self, eng, sid, val):
        if val <= 0:
            return
        s = self.seen[eng]
        if s.get(sid, 0) >= val:
            return
        self.engs[eng].wait_ge(self.sems[sid], val)
        s[sid] = val
        self.n_wait += 1

    def _wait_ev(self, eng, ev):
        sid, val = ev
        if sid == "pe" and eng == "pe":
            return
        self._wait(eng, sid, val)

    def _deps(self, eng, reads, writes, swrites=()):
        for b in reads:
            if b.lw is not None:
                self._wait_ev(eng, b.lw)
            for sid, val in b.wr.items():
                self._wait_ev(eng, (sid, val))
        for b in writes:
            if b.lw is not None:
                self._wait_ev(eng, b.lw)
            for sid, val in b.wr.items():
                self._wait_ev(eng, (sid, val))
            for sid, val in b.rd.items():
                self._wait_ev(eng, (sid, val))
        for b in swrites:
            if b.lw is not None:
                self._wait_ev(eng, b.lw)
            for sid, val in b.rd.items():
                self._wait_ev(eng, (sid, val))

    def _commit(self, ev, reads, writes, swrites=()):
        sid, val = ev
        for b in writes:
            b.lw = ev
            b.rd = {}
            b.wr = {}
        for b in swrites:
            b.rd = {}
            if b.wr.get(sid, 0) < val:
                b.wr[sid] = val
        for b in reads:
            if b.rd.get(sid, 0) < val:
                b.rd[sid] = val

    def op(self, eng, fn, reads=(), writes=()):
        self._deps(eng, reads, writes)
        ins = fn(self.engs[eng])
        self.cnt[eng] += 1
        ins.then_inc(self.sems[eng], 1)
        ev = (eng, self.cnt[eng])
        self._commit(ev, reads, writes)
        self.n_ins += 1
        return ev

    def cp(self, out, in_, reads=(), writes=(), eng=None):
        eng = eng or self.anyop()
        if eng == "act":
            return self.op("act", lambda e: e.copy(out=out, in_=in_), reads, writes)
        return self.op(eng, lambda e: e.tensor_copy(out=out, in_=in_), reads, writes)

    def anyop(self):
        self._rr ^= 1
        return "act" if self._rr else "dve"

    def dma(self, q, out, in_, reads=(), writes=(), swrites=(), **kw):
        self._deps(q, reads, writes, swrites)
        i = self.dq_next[q]
        self.dq_next[q] = (i + 1) % len(self.dq[q])
        sid = self.dq[q][i]
        self._wait(q, sid, self.cnt[sid])
        ins = self.engs[q].dma_start(out=out, in_=in_, **kw)
        self.cnt[sid] += 16
        ins.then_inc(self.sems[sid], 16)
        ev = (sid, self.cnt[sid])
        self._commit(ev, reads, writes, swrites)
        self.n_ins += 1
        return ev

    def finish(self):
        for q in ("sp", "pool"):
            for sid in self.dq[q]:
                self._wait("sp", sid, self.cnt[sid])
        for e in ("pe", "act", "dve", "pool"):
            self._wait("sp", e, self.cnt[e])


class Rot:
    def __init__(self, k, name, shape, dt, n):
        self.items = [(k.sb("%s%d" % (name, i), shape, dt), k.buf("%s%d" % (name, i))) for i in range(n)]
        self.i = 0

    def next(self):
        it = self.items[self.i]
        self.i = (self.i + 1) % len(self.items)
        return it


def build_program():
    nc = bass.Bass("TRN2", target_bir_lowering=False)

    def din(name, shape, dt=F32):
        return nc.dram_tensor(name, list(shape), dt, kind="ExternalInput").ap()

    def dout(name, shape, dt=F32):
        return nc.dram_tensor(name, list(shape), dt, kind="ExternalOutput").ap()

    def dscr(name, shape, dt):
        return nc.dram_tensor(name, list(shape), dt, kind="ExternalOutput" if _DEBUG else "Internal").ap()

    xin = din("xin", [TOK, D])
    tmask = din("tmask", [128, NT])
    w_in = din("w_in", [D, INC])
    w_br_att = din("w_br_att", [SBW, D])
    w_br_ssd = din("w_br_ssd", [SSDW, D])
    w_out = din("w_out", [D, D])
    w_up = din("w_up", [D, 2 * DFF])
    w_down = din("w_down", [DFF, D])
    g_mix = din("g_mix", [1, D])
    g_ffn = din("g_ffn", [1, D])
    g_final = din("g_final", [1, D])
    g_ssd = din("g_ssd", [1, SSDW])
    cw_ssd = din("cw_ssd", [128, 24, 4])
    cb_ssd = din("cb_ssd", [128, 24])
    cw_ffn = din("cw_ffn", [128, 44, 3])
    cb_ffn = din("cb_ffn", [128, 44])
    dt_bias = din("dt_bias", [1, 32])
    a_log = din("a_log", [1, 32])
    d_skip = din("d_skip", [1, 32])
    cache_k = din("cache_k", [NSTREAM, NCACHE, SBW])
    cache_v = din("cache_v", [NSTREAM, NCACHE, SBW])
    st_ssm = din("st_ssm", [NSTREAM, SSDW, 128])
    st_sconv = din("st_sconv", [NSTREAM * 3, XBC])
    st_fconv = din("st_fconv", [NSTREAM * 2, 2 * DFF])
    c_ident = din("c_ident", [128, 128])
    c_attmask = din("c_attmask", [128, 4, 512])
    c_trineg = din("c_trineg", [128, 128])
    c_le = din("c_le", [128, 128])
    c_gt = din("c_gt", [128, 128])
    c_leS = din("c_leS", [128, 128])
    c_gtS = din("c_gtS", [128, 128])
    c_blkS = din("c_blkS", [128, 4, 128])

    y_out = dout("y_out", [TOKM, D])
    k_out = dout("k_out", [TOKM, SBW])
    v_out = dout("v_out", [TOKM, SBW])
    ssm_out = dout("ssm_out", [5, SSDW, 128])
    sconv_out = dout("sconv_out", [5, 24, 128, 3])
    fconv_out = dout("fconv_out", [5, 44, 128, 2])

    qT_s = dscr("qT_s", [16, 64, TOKM], BF16)
    kT_s = dscr("kT_s", [16, 64, TOK], BF16)
    v_s = dscr("v_s", [TOK, SBW], BF16)
    z_s = dscr("z_s", [TOKM, SSDW], F32)
    xc_s = dscr("xc_s", [24, 128, TOK], BF16)
    g_s = dscr("g_s", [16, 128, TOKM], F32)
    attT_s = dscr("attT_s", [16, 64, TOKM], BF16)
    ssdT_s = dscr("ssdT_s", [16, 128, TOKM], BF16)
    h_s = dscr("h_s", [TOKM, D], F32)
    mT_s = dscr("mT_s", [8, 128, TOKM], BF16)
    act_s = dscr("act_s", [22, 128, TOKM], BF16)

    with contextlib.ExitStack() as st:
        k = KB(nc, st)
        op, dma = k.op, k.dma
        B_qT, B_kT, B_v, B_z, B_xc, B_g, B_h, B_mT, B_act = [k.buf(n) for n in
                                                          ("qT_s", "kT_s", "v_s", "z_s", "xc_s", "g_s", "h_s", "mT_s", "act_s")]

        psbig = nc.alloc_psum_tensor("psbig", [128, 8 * 512], F32)
        PS = [(psbig[:, i * 512:(i + 1) * 512], k.buf("ps%d" % i)) for i in range(8)]
        psi = [0]

        def ps_next():
            it = PS[psi[0]]
            psi[0] = (psi[0] + 1) % 6
            return it

        psa = [0]

        def ps_acc_next():
            it = PS[6 + psa[0]]
            psa[0] ^= 1
            return it

        def const_tile(name, src, shape, dt=F32, q="sp"):
            t = k.sb(name, shape, dt)
            b = k.buf(name)
            dma("pool" if dt == BF16 else q, t[:], src, writes=[b])
            return t, b

        ident_f, b_identf = const_tile("ident_f", c_ident, [128, 128])
        ident_b, b_identb = const_tile("ident_b", c_ident, [128, 128], BF16)
        attmask, b_attmask = const_tile("attmask", c_attmask, [128, 4, 512], BF16)
        trineg, b_trineg = const_tile("trineg", c_trineg, [128, 128], BF16)
        onesneg = k.sb("onesneg", [128, 128], BF16)
        b_onesneg = k.buf("onesneg")
        op("pool", lambda e: e.memset(onesneg[:], -1.0), writes=[b_onesneg])
        le_f, b_lef = const_tile("le_f", c_le, [128, 128])
        le_b, b_leb = const_tile("le_b", c_le, [128, 128], BF16)
        gt_f, b_gtf = const_tile("gt_f", c_gt, [128, 128])
        leS_f, b_leSf = const_tile("leS_f", c_leS, [128, 128])
        leS_b, b_leSb = const_tile("leS_b", c_leS, [128, 128], BF16)
        gtS_f, b_gtSf = const_tile("gtS_f", c_gtS, [128, 128])
        blkS_f, b_blkSf = const_tile("blkS_f", c_blkS, [128, 4, 128])
        ones_f = k.sb("ones_f", [128, 128], F32)
        b_onesf = k.buf("ones_f")
        op("pool", lambda e: e.memset(ones_f[:], 1.0), writes=[b_onesf])
        cws, b_cws = const_tile("cws", cw_ssd, [128, 24, 4])
        cbs, b_cbs = const_tile("cbs", cb_ssd, [128, 24])
        cwf, b_cwf = const_tile("cwf", cw_ffn, [128, 44, 3])
        cbf, b_cbf = const_tile("cbf", cb_ffn, [128, 44])
        dtb_bc, b_dtb = const_tile("dtb_bc", dt_bias.partition_broadcast(128), [128, 32])
        a_bc, b_abc = const_tile("a_bc", a_log.partition_broadcast(128), [128, 32])
        dsk_bc, b_dsk = const_tile("dsk_bc", d_skip.partition_broadcast(128), [128, 32])
        tm, b_tm = const_tile("tm", tmask, [128, NT])
        op("act", lambda e: e.activation(out=a_bc[:], in_=a_bc[:], func=AF.Exp), reads=[b_abc], writes=[b_abc])
        op("dve", lambda e: e.tensor_scalar(out=a_bc[:], in0=a_bc[:], scalar1=-1.0, scalar2=None, op0=ALU.mult),
           reads=[b_abc], writes=[b_abc])

        small = Rot(k, "small", [128, 8], F32, 4)
        junk_f = k.sb("junk_f", [128, SSDW], F32)
        b_junk = k.buf("junk")
        dtm = k.sb("dtm", [128, NT, 32], F32)
        b_dtm = [k.buf("dtm%d" % i) for i in range(NT)]
        k.pstack = contextlib.ExitStack()

        def rmsnorm_to_bf16(x_t, b_x, gain, b_gain, out_bf, b_out, junk, b_junk, n=D):
            ss, b_ss = small.next()
            op("act", lambda e: e.activation(out=junk, in_=x_t, func=AF.Square, accum_out=ss[:, 0:1]),
               reads=[b_x], writes=[b_junk, b_ss])
            op("act", lambda e: e.activation(out=ss[:, 1:2], in_=ss[:, 0:1], func=AF.Sqrt, scale=1.0 / n, bias=EPS),
               reads=[b_ss], writes=[b_ss])
            op("dve", lambda e: e.reciprocal(out=ss[:, 2:3], in_=ss[:, 1:2]), reads=[b_ss], writes=[b_ss])
            op("dve", lambda e: e.scalar_tensor_tensor(out=out_bf, in0=x_t, scalar=ss[:, 2:3], in1=gain,
                                                       op0=ALU.mult, op1=ALU.mult),
               reads=[b_x, b_ss, b_gain], writes=[b_out])

        def transpose_to(src_bf, b_src, nchunk, dst_fn, b_dst):
            for c0 in range(0, nchunk, 8):
                n = min(8, nchunk - c0)
                pt, b_pt = ps_next()
                ptb = pt[:].bitcast(BF16)
                for j in range(n):
                    c = c0 + j
                    op("pe", lambda e, c=c, j=j: e.transpose(out=ptb[:, j * 128:(j + 1) * 128],
                                                             in_=src_bf[:, c * 128:(c + 1) * 128], identity=ident_b[:]),
                       reads=[b_src, b_identb], writes=[b_pt])
                eng = k.anyop()
                src = ptb[:, 0:n * 128].rearrange("p (c t) -> p c t", t=128)
                if eng == "act":
                    op("act", lambda e: e.copy(out=dst_fn(c0, n), in_=src), reads=[b_pt], writes=[b_dst])
                else:
                    op("dve", lambda e: e.tensor_copy(out=dst_fn(c0, n), in_=src), reads=[b_pt], writes=[b_dst])

        def body():
            gmix_bc, b_gmix = const_tile("gmix_bc", g_mix.partition_broadcast(128), [128, D])
            uT = k.sb("uT", [128, 8, TOK], BF16)
            b_uT = [k.buf("uT%d" % i) for i in range(NT)]
            xrot = Rot(k, "xrot", [128, D], F32, 2)
            ubrot = Rot(k, "ubrot", [128, D], BF16, 2)
            for i in range(NT):
                xt, b_xt = xrot.next()
                dma("sp", xt[:], xin[i * 128:(i + 1) * 128, :], writes=[b_xt])
                ub, b_ub = ubrot.next()
                rmsnorm_to_bf16(xt[:], b_xt, gmix_bc[:], b_gmix, ub[:], b_ub, junk_f[:, 0:D], b_junk)
                transpose_to(ub, b_ub, 8, lambda c0, n, i=i: uT[:, c0:c0 + n, i * 128:(i + 1) * 128], b_uT[i])

            checkpoint(0)
            wrot = Rot(k, "wrot", [128, 8, 512], BF16, 2)

            def load_w(src2d, c0, nc_):
                wt, b_wt = wrot.next()
                dma("pool", wt[:, :, 0:nc_], src2d[:, c0:c0 + nc_].rearrange("(kc p) c -> p kc c", p=128), writes=[b_wt])
                return wt, b_wt

            tiles_all = list(range(NT))
            tiles_m1 = list(range(NPRE, NT))

            def mloc(i):
                return i - NPRE

            ev32 = Rot(k, "ev32", [128, 512], F32, 3)
            ev16 = Rot(k, "ev16", [128, 512], BF16, 3)

            def tokmajor_block(wt, b_wt, ncol, tiles, consume):
                for i in tiles:
                    pt, b_pt = ps_next()
                    for kc in range(8):
                        op("pe", lambda e, kc=kc: e.matmul(pt[:, 0:ncol], lhsT=uT[:, kc, i * 128:(i + 1) * 128],
                                                           rhs=wt[:, kc, 0:ncol], start=(kc == 0), stop=(kc == 7)),
                           reads=[b_uT[i], b_wt], writes=[b_pt])
                    consume(i, pt, b_pt)

            def groups(tiles):
                gs = []
                cur = []
                for i in tiles:
                    if i == NT - 1 or (cur and (cur[-1] + 1 != i or len(cur) == 4)):
                        if cur:
                            gs.append(cur)
                        cur = []
                    cur.append(i)
                    if i == NT - 1:
                        gs.append(cur)
                        cur = []
                if cur:
                    gs.append(cur)
                return gs

            def chunks(first, sizes):
                out, t = [], first
                for sz in sizes:
                    out.append(list(range(t, t + sz)))
                    t += sz
                return out

            G_ALL = chunks(0, [4] * 6 + [3] * 3) + [[NT - 1]]
            G_M1 = chunks(NPRE, [4, 4, 3, 3, 3]) + [[NT - 1]]
            assert G_ALL[-2][-1] == NPT - 1 and G_M1[-2][-1] == NPT - 1

            class Deferred:
                def __init__(self, depth):
                    self.q, self.depth = [], depth

                def push(self, fn):
                    self.q.append(fn)
                    while len(self.q) > self.depth:
                        self.q.pop(0)()

                def flush(self):
                    while self.q:
                        self.q.pop(0)()

            def featmajor(wt, b_wt, wc0, m, grp, consume, tofs=0, defer=None):
                t0 = grp[0] * 128
                n = len(grp) * 128
                pt, b_pt = ps_next()
                for kc in range(8):
                    op("pe", lambda e, kc=kc: e.matmul(pt[0:m, 0:n], lhsT=wt[:, kc, wc0:wc0 + m],
                                                       rhs=uT[:, kc, t0 - tofs:t0 - tofs + n], start=(kc == 0), stop=(kc == 7)),
                       reads=[b_uT[i] for i in grp] + [b_wt], writes=[b_pt])
                if defer is not None:
                    defer.push(lambda: consume(grp, t0, n, pt, b_pt))
                else:
                    consume(grp, t0, n, pt, b_pt)

            for which in ("q", "k"):
                cbase = C_Q if which == "q" else C_K
                for blk in range(2):
                    wt, b_wt = load_w(w_in, cbase + blk * 512, 512)
                    for hp in range(4):
                        h = blk * 8 + hp * 2
                        for grp in (G_M1 if which == "q" else G_ALL):
                            def cons(grp, t0, n, pt, b_pt, h=h, which=which):
                                o, b_o = ev16.next()
                                if which == "q":
                                    op("act", lambda e: e.activation(out=o[:, 0:n], in_=pt[:, 0:n], func=AF.Copy,
                                                                     scale=0.125), reads=[b_pt], writes=[b_o])
                                    for two in range(2):
                                        dma("sp", qT_s[h + two, :, t0 - NPRE * 128:t0 - NPRE * 128 + n], o[two * 64:(two + 1) * 64, 0:n],
                                            reads=[b_o], swrites=[B_qT])
                                else:
                                    op("dve", lambda e: e.tensor_copy(out=o[:, 0:n], in_=pt[:, 0:n]),
                                       reads=[b_pt], writes=[b_o])
                                    for two in range(2):
                                        dma("sp", kT_s[h + two, :, t0:t0 + n], o[two * 64:(two + 1) * 64, 0:n], reads=[b_o], swrites=[B_kT])
                            featmajor(wt, b_wt, hp * 128, 128, grp, cons)

            checkpoint(0.1)
            for blk in range(2):
                wt, b_wt = load_w(w_in, C_K + blk * 512, 512)

                def cons_k(i, pt, b_pt, blk=blk):
                    o, b_o = ev32.next()
                    k.cp(o[:], pt[:], reads=[b_pt], writes=[b_o])
                    dma("sp", k_out[mloc(i) * 128:(mloc(i) + 1) * 128, blk * 512:(blk + 1) * 512], o[:], reads=[b_o])
                tokmajor_block(wt, b_wt, 512, tiles_m1, cons_k)
            for blk in range(2):
                wt, b_wt = load_w(w_in, C_V + blk * 512, 512)

                def cons_v(i, pt, b_pt, blk=blk):
                    o2, b_o2 = ev16.next()
                    if i >= NPRE:
                        o, b_o = ev32.next()
                        op("act", lambda e: e.copy(out=o[:], in_=pt[:]), reads=[b_pt], writes=[b_o])
                        dma("sp", v_out[mloc(i) * 128:(mloc(i) + 1) * 128, blk * 512:(blk + 1) * 512], o[:], reads=[b_o])
                        op("dve", lambda e: e.tensor_copy(out=o2[:], in_=o[:]), reads=[b_o], writes=[b_o2])
                    else:
                        op("dve", lambda e: e.tensor_copy(out=o2[:], in_=pt[:]), reads=[b_pt], writes=[b_o2])
                    dma("sp", v_s[i * 128:(i + 1) * 128, blk * 512:(blk + 1) * 512], o2[:], reads=[b_o2], swrites=[B_v])
                tokmajor_block(wt, b_wt, 512, tiles_all, cons_v)

            checkpoint(0.2)
            for blk in range(4):
                wt, b_wt = load_w(w_in, C_Z + blk * 512, 512)

                def cons_z(i, pt, b_pt, blk=blk):
                    o, b_o = ev32.next()
                    op("act", lambda e: e.activation(out=o[:], in_=pt[:], func=AF.Silu), reads=[b_pt], writes=[b_o])
                    dma("sp", z_s[mloc(i) * 128:(mloc(i) + 1) * 128, blk * 512:(blk + 1) * 512], o[:], reads=[b_o], swrites=[B_z])
                tokmajor_block(wt, b_wt, 512, tiles_m1, cons_z)

            checkpoint(0.3)
            wt, b_wt = load_w(w_in, C_DT, 32)

            def cons_dt(i, pt, b_pt):
                op("dve", lambda e: e.tensor_tensor(out=dtm[:, i, :], in0=pt[:, 0:32], in1=dtb_bc[:], op=ALU.add),
                   reads=[b_pt, b_dtb], writes=[b_dtm[i]])
                op("act", lambda e: e.activation(out=dtm[:, i, :], in_=dtm[:, i, :], func=AF.Exp),
                   reads=[b_dtm[i]], writes=[b_dtm[i]])
                op("act", lambda e: e.activation(out=dtm[:, i, :], in_=dtm[:, i, :], func=AF.Ln, bias=1.0),
                   reads=[b_dtm[i]], writes=[b_dtm[i]])
                op("dve", lambda e: e.tensor_scalar(out=dtm[:, i, :], in0=dtm[:, i, :], scalar1=tm[:, i:i + 1], scalar2=None,
                                                    op0=ALU.mult), reads=[b_dtm[i], b_tm], writes=[b_dtm[i]])
            tokmajor_block(wt, b_wt, 32, tiles_all, cons_dt)

            checkpoint(0.4)
            for blk in range(4):
                wt, b_wt = load_w(w_in, C_G + blk * 512, 512)
                for cc in range(4):
                    c = blk * 4 + cc
                    for grp in G_M1:
                        def cons_g(grp, t0, n, pt, b_pt, c=c):
                            o, b_o = ev32.next()
                            op("act", lambda e: e.activation(out=o[:, 0:n], in_=pt[:, 0:n], func=AF.Sigmoid),
                               reads=[b_pt], writes=[b_o])
                            dma("sp", g_s[c, :, t0 - NPRE * 128:t0 - NPRE * 128 + n], o[:, 0:n], reads=[b_o], swrites=[B_g])
                        featmajor(wt, b_wt, cc * 128, 128, grp, cons_g)

            checkpoint(0.5)
            sch = k.sb("sch", [128, 24, 12], F32)
            b_sch = k.buf("sch")
            s12r = Rot(k, "s12", [12, 512], F32, 2)
            for c0 in range(0, 24, 4):
                s12, b_s12 = s12r.next()
                dma("sp", s12[:], st_sconv[:, c0 * 128:(c0 + 4) * 128], writes=[b_s12])
                pt, b_pt = ps_next()
                for j in range(4):
                    op("pe", lambda e, j=j: e.transpose(out=pt[:, j * 12:(j + 1) * 12], in_=s12[:, j * 128:(j + 1) * 128],
                                                        identity=ident_f[0:12, 0:12]), reads=[b_s12, b_identf], writes=[b_pt])
                op("dve", lambda e: e.tensor_copy(out=sch[:, c0:c0 + 4, :], in_=pt[:, 0:48].rearrange("p (c t) -> p c t", t=12)),
                   reads=[b_pt], writes=[b_sch])
            checkpoint(0.6)
            xraw = Rot(k, "xraw", [128, 4, 3 + 128], F32, 3)
            xacc = Rot(k, "xacc", [128, 512], F32, 2)
            dq_x = Deferred(1)
            sconv_sb = k.sb("sconv_sb", [128, 24, 5, 3], F32)
            b_sconv = k.buf("sconv_sb")
            for blk in range(6):
                wt, b_wt = load_w(w_in, C_X + blk * 512, 512)
                for cc in range(4):
                    c = blk * 4 + cc
                    prev = [None]
                    for grp in G_ALL:
                        def cons_x(grp, t0, n, pt, b_pt, c=c, prev=prev):
                            xr, b_xr = xraw.next()
                            issample = grp[0] == NT - 1
                            if not issample:
                                flat = xr[:].rearrange("p s t -> p (s t)")
                                if prev[0] is None:
                                    op("pool", lambda e: e.memset(flat[:, 0:3], 0.0), writes=[b_xr])
                                else:
                                    pf, b_pf, pn = prev[0]
                                    op("pool", lambda e: e.tensor_copy(out=flat[:, 0:3], in_=pf[:, pn:pn + 3]),
                                       reads=[b_pf], writes=[b_xr])
                                op("act", lambda e: e.copy(out=flat[:, 3:3 + n], in_=pt[:, 0:n]), reads=[b_pt], writes=[b_xr])
                                prev[0] = (flat, b_xr, n)
                                if grp[-1] == NPT - 1:
                                    op("pool", lambda e: e.tensor_copy(out=sconv_sb[:, c, 0, :], in_=flat[:, n:n + 3]),
                                       reads=[b_xr], writes=[b_sconv])
                                acc, b_acc = xacc.next()
                                op("act", lambda e: e.activation(out=acc[:, 0:n], in_=pt[:, 0:n], func=AF.Identity,
                                                                 scale=cws[:, c, 3:4], bias=cbs[:, c:c + 1]),
                                   reads=[b_pt, b_cws, b_cbs], writes=[b_acc])
                                for i in range(0, 3):
                                    op("dve", lambda e, i=i: e.scalar_tensor_tensor(out=acc[:, 0:n], in0=flat[:, i:i + n],
                                                                                    scalar=cws[:, c, i:i + 1], in1=acc[:, 0:n],
                                                                                    op0=ALU.mult, op1=ALU.add),
                                       reads=[b_xr, b_cws, b_acc], writes=[b_acc])
                                o, b_o = ev16.next()
                                op("act", lambda e: e.activation(out=o[:, 0:n], in_=acc[:, 0:n], func=AF.Silu),
                                   reads=[b_acc], writes=[b_o])
                                dma("sp", xc_s[c, :, t0:t0 + n], o[:, 0:n], reads=[b_o], swrites=[B_xc])
                            else:
                                op("pool", lambda e: e.tensor_copy(out=xr[:, :, 0:3],
                                                                   in_=sch[:, c, :].rearrange("p (s t) -> p s t", t=3)),
                                   reads=[b_sch], writes=[b_xr])
                                op("act", lambda e: e.copy(out=xr[:, :, 3:35], in_=pt[:, 0:128].rearrange("p (s t) -> p s t", t=32)),
                                   reads=[b_pt], writes=[b_xr])
                                op("pool", lambda e: e.tensor_copy(out=sconv_sb[:, c, 1:5, :], in_=xr[:, :, 16:19]),
                                   reads=[b_xr], writes=[b_sconv])
                                acc, b_acc = xacc.next()
                                a3 = acc[:, 0:128].rearrange("p (s t) -> p s t", t=32)
                                op("act", lambda e: e.activation(out=acc[:, 0:128], in_=pt[:, 0:128], func=AF.Identity,
                                                                 scale=cws[:, c, 3:4], bias=cbs[:, c:c + 1]),
                                   reads=[b_pt, b_cws, b_cbs], writes=[b_acc])
                                for i in range(0, 3):
                                    op("dve", lambda e, i=i: e.scalar_tensor_tensor(out=a3, in0=xr[:, :, i:i + 32],
                                                                                    scalar=cws[:, c, i:i + 1], in1=a3,
                                                                                    op0=ALU.mult, op1=ALU.add),
                                       reads=[b_xr, b_cws, b_acc], writes=[b_acc])
                                o, b_o = ev16.next()
                                op("act", lambda e: e.activation(out=o[:, 0:128], in_=acc[:, 0:128], func=AF.Silu),
                                   reads=[b_acc], writes=[b_o])
                                dma("sp", xc_s[c, :, t0:t0 + 128], o[:, 0:128], reads=[b_o], swrites=[B_xc])
                        featmajor(wt, b_wt, cc * 128, 128, grp, cons_x, defer=dq_x)
            dq_x.flush()
            for q5 in range(5):
                dma("sp", sconv_out[q5].rearrange("c p t -> p c t"), sconv_sb[:, :, q5, :], reads=[b_sconv])

            checkpoint(1)
            k.new_phase()
            b_attT = k.buf("attT_s")
            def alloc_att_bufs():
                return dict(e=Rot(k, "e_rot", [128, 1024], F32, 4), sp=Rot(k, "sp_rot", [128, 1024], BF16, 5),
                            x=Rot(k, "x_rot", [128, 1024], F32, 2), w=Rot(k, "w_rot", [128, 1024], BF16, 3),
                            acc=Rot(k, "acc_rot", [128, 1024], BF16, 3), ao=Rot(k, "ao_rot", [128, 512], BF16, 2),
                            spf=(k.sb("spf", [128, 512], BF16), k.buf("spf")))
            AB = alloc_att_bufs()
            ppair = [0]

            def ps_pair_next():
                b0 = 2 * ppair[0]
                ppair[0] = (ppair[0] + 1) % 3
                return b0

            def sb_pipeline(blocks, W, b_mask):
                n = len(blocks)
                partial_first = blocks[0]["jn"] < 128
                units = []
                i = 0
                if partial_first:
                    units.append([0])
                    i = 1
                while i < n:
                    if i + 1 < n:
                        units.append([i, i + 1])
                        i += 2
                    else:
                        units.append([i])
                        i += 1
                nu = len(units)
                U = [dict() for _ in units]
                SPV = {}
                ACC = {}
                spf, b_spf = AB["spf"]

                def psv(b0, ns, jn):
                    return psbig[0:jn, b0 * 512:(b0 + ns) * 512].rearrange("p (s c) -> p s c", c=512)[:, :, 0:W]

                def sbv(t, ns, jn):
                    return t[0:jn, 0:ns * W].rearrange("p (s c) -> p s c", c=W)

                def stage_a(u):
                    idx, c = units[u], U[u]
                    ns = len(idx)
                    jn = blocks[idx[0]]["jn"]
                    b0 = ps_pair_next()
                    bufs = [PS[b0 + s_][1] for s_ in range(ns)]
                    for s_, bi in enumerate(idx):
                        blocks[bi]["zmm"](PS[b0 + s_][0], PS[b0 + s_][1], True)
                    e_t, b_e = AB["e"].next()
                    op("act", lambda e: e.activation(out=sbv(e_t, ns, jn), in_=psv(b0, ns, jn), func=AF.Exp), reads=bufs, writes=[b_e])
                    if u == 0 and partial_first:
                        sp_t, b_sp = spf, b_spf
                    else:
                        sp_t, b_sp = AB["sp"].next()
                    op("act", lambda e: e.activation(out=sp_t[0:jn, 0:ns * W], in_=e_t[0:jn, 0:ns * W], func=AF.Ln, bias=1.0),
                       reads=[b_e], writes=[b_sp])
                    for s_, bi in enumerate(idx):
                        bl = blocks[bi]
                        if bl["mask"] is not None:
                            v_ = bl["mv"](sp_t[0:jn, s_ * W:(s_ + 1) * W])
                            op("pool", lambda e, v_=v_, bl=bl: e.tensor_tensor(out=v_, in0=v_, in1=bl["mask"], op=ALU.mult),
                               reads=[b_sp, b_mask], writes=[b_sp])
                        SPV[bi] = (sp_t[:, s_ * W:(s_ + 1) * W], b_sp)
                    c.update(e=(e_t, b_e), ns=ns, jn=jn)

                def stage_b1_acc(u):
                    idx = units[u]
                    first_full = 1 if partial_first else 0
                    a_t = None
                    for s_, bi in enumerate(idx):
                        nfull_before = bi - first_full
                        if nfull_before <= 0:
                            ACC[bi] = None
                        elif nfull_before == 1:
                            ACC[bi] = SPV[bi - 1]
                        else:
                            if a_t is None:
                                a_t, b_a = AB["acc"].next()
                            pa_ap, b_pa = ACC[bi - 1]
                            ps_ap, b_ps = SPV[bi - 1]
                            o_ap = a_t[:, s_ * W:(s_ + 1) * W]
                            op("dve", lambda e, o_ap=o_ap, pa_ap=pa_ap, ps_ap=ps_ap: e.tensor_tensor(out=o_ap, in0=pa_ap, in1=ps_ap, op=ALU.add),
                               reads=[b_pa, b_ps], writes=[b_a])
                            ACC[bi] = (o_ap, b_a)

                def stage_b1(u):
                    idx, c = units[u], U[u]
                    ns, jn = c["ns"], c["jn"]
                    b0 = ps_pair_next()
                    tl = []
                    for s_, bi in enumerate(idx):
                        sp_ap, b_sp = SPV[bi]
                        terms = [(trineg[0:jn, 0:jn], sp_ap[0:jn, :], [b_trineg, b_sp])]
                        if partial_first and bi > 0:
                            j0 = blocks[0]["jn"]
                            terms.append((onesneg[0:j0, 0:jn], spf[0:j0, 0:W], [b_onesneg, b_spf]))
                        if ACC[bi] is not None:
                            terms.append((onesneg[:, 0:jn], ACC[bi][0], [b_onesneg, ACC[bi][1]]))
                        tl.append(terms)
                    for ti in range(max(len(t_) for t_ in tl)):
                        for s_, terms in enumerate(tl):
                            if ti < len(terms):
                                l_, r_, rd = terms[ti]
                                ap_, b_ap = PS[b0 + s_]
                                op("pe", lambda e, l_=l_, r_=r_, ti=ti, ap_=ap_, nt=len(terms): e.matmul(ap_[0:jn, 0:W], lhsT=l_, rhs=r_, start=(ti == 0),
                                                                                                         stop=(ti == nt - 1)), reads=rd, writes=[b_ap])
                    c["apb"] = b0

                def stage_b2a(u):
                    idx, c = units[u], U[u]
                    ns, jn, b0 = c["ns"], c["jn"], c["apb"]
                    e_t, b_e = c["e"]
                    x_t, b_x = AB["x"].next()
                    op("act", lambda e: e.activation(out=sbv(x_t, ns, jn), in_=psv(b0, ns, jn), func=AF.Exp),
                       reads=[PS[b0 + s_][1] for s_ in range(ns)], writes=[b_x])
                    w_t, b_w = AB["w"].next()
                    op("dve", lambda e: e.tensor_tensor(out=w_t[0:jn, 0:ns * W], in0=x_t[0:jn, 0:ns * W], in1=e_t[0:jn, 0:ns * W], op=ALU.mult),
                       reads=[b_x, b_e], writes=[b_w])
                    for s_, bi in enumerate(idx):
                        bl = blocks[bi]
                        if bl["mask"] is not None:
                            v_ = bl["mv"](w_t[0:jn, s_ * W:(s_ + 1) * W])
                            op("pool", lambda e, v_=v_, bl=bl: e.tensor_tensor(out=v_, in0=v_, in1=bl["mask"], op=ALU.mult),
                               reads=[b_w, b_mask], writes=[b_w])
                    c["w"] = (w_t, b_w)

                def stage_b2b(u):
                    w_t, b_w = U[u]["w"]
                    for s_, bi in enumerate(units[u]):
                        blocks[bi]["pv"](w_t[:, s_ * W:(s_ + 1) * W], b_w, bi == 0, bi == n - 1)

                L1, L2, L3 = 1, 2, 3
                for s_ in range(nu + L3):
                    if 0 <= s_ - L1 < nu:
                        stage_b1_acc(s_ - L1)
                    if 0 <= s_ - L2 < nu:
                        stage_b2a(s_ - L2)
                    if s_ < nu:
                        stage_a(s_)
                    if 0 <= s_ - L1 < nu:
                        stage_b1(s_ - L1)
                    if 0 <= s_ - L3 < nu:
                        stage_b2b(s_ - L3)

            ident_mv = lambda a: a
            kTh_rot = Rot(k, "kTh", [128, NPT * 128], BF16, 2)
            vh_rot = Rot(k, "vh", [128, NPT, 128], BF16, 2)
            qTh_rot = Rot(k, "qTh", [128, NMAIN * 128], BF16, 2)
            for (t_, b_) in kTh_rot.items + qTh_rot.items:
                op("pool", lambda e, t_=t_: e.memset(t_[64:128, :], 0.0), writes=[b_])
            pfh = {}
            vcur = [None]

            def load_head(h):
                kTh, b_kTh = kTh_rot.next()
                dma("sp", kTh[0:64, :], kT_s[h, :, 0:NPT * 128], reads=[B_kT], swrites=[b_kTh])
                if h % 2 == 0:
                    vh, b_vh = vh_rot.next()
                    dma("sp", vh[:], v_s[0:NPT * 128, h * 64:(h + 2) * 64].rearrange("(kt p) d -> p kt d", p=128), reads=[B_v], writes=[b_vh])
                    vcur[0] = (vh, b_vh)
                qTh, b_qTh = qTh_rot.next()
                dma("sp", qTh[0:64, :], qT_s[h, :, 0:NMAIN * 128], reads=[B_qT], swrites=[b_qTh])
                pfh[h] = (kTh, b_kTh, qTh, b_qTh) + vcur[0]

            load_head(0)
            for h in range(16):
                hh2 = h % 2
                if h + 1 < 16:
                    load_head(h + 1)
                kTh, b_kTh, qTh, b_qTh, vh, b_vh = pfh.pop(h)
                for (g0, nq) in ((0, 4), (4, 4), (8, 3), (11, 3), (14, 3)):
                    W = nq * 128
                    q0 = g0 * 128
                    op_, b_op = ps_acc_next()
                    blocks = []
                    for kt in range(NPRE + g0 + nq - 1, -1, -1):
                        r = kt - (NPRE + g0)

                        def zmm(ps, b_ps, final, kt=kt, kTh=kTh, qTh=qTh, b_kTh=b_kTh, b_qTh=b_qTh, W=W, q0=q0):
                            op("pe", lambda e: e.matmul(ps[:, 0:W], lhsT=kTh[:, kt * 128:(kt + 1) * 128], rhs=qTh[:, q0:q0 + W],
                                                        start=True, stop=final), reads=[b_kTh, b_qTh], writes=[b_ps])

                        def pv(w_t, b_w, first, last, kt=kt, vh=vh, b_vh=b_vh, op_=op_, b_op=b_op, W=W):
                            op("pe", lambda e: e.matmul(op_[:, 0:W], lhsT=vh[:, kt, :], rhs=w_t, start=first, stop=last),
                               reads=[b_vh, b_w], writes=[b_op])
                        blocks.append(dict(jn=128, zmm=zmm, mask=(attmask[:, r, 0:W] if r >= 0 else None), mv=ident_mv, pv=pv))
                    sb_pipeline(blocks, W, b_attmask)
                    ao, b_ao = AB["ao"].next()
                    rs = slice(hh2 * 64, (hh2 + 1) * 64)
                    op("dve", lambda e, rs=rs: e.tensor_copy(out=ao[rs, 0:W], in_=op_[rs, 0:W]), reads=[b_op], writes=[b_ao])
                    dma("sp", attT_s[h, :, q0:q0 + W], ao[rs, 0:W], reads=[b_ao], swrites=[b_attT])

            checkpoint(2)
            k.new_phase()
            AB = alloc_att_bufs()
            kc_bf = k.sb("kc_bf", [128, 9, SBW], BF16)
            b_kc = k.buf("kc_bf")
            vc_bf = k.sb("vc_bf", [128, 9, SBW], BF16)
            b_vc = k.buf("vc_bf")
            kTc = k.sb("kTc", [128, 16, 9 * 128], BF16)
            b_kTc = k.buf("kTc")
            kTn = k.sb("kTn", [128, 16, 32], BF16)
            b_kTn = k.buf("kTn")
            vn = k.sb("vn", [32, SBW], BF16)
            b_vn = k.buf("vn")
            qs = k.sb("qs", [128, 16, 32], BF16)
            b_qs = k.buf("qs")
            for (t_, b_) in ((kTc, b_kTc), (kTn, b_kTn), (qs, b_qs)):
                op("pool", lambda e, t_=t_: e.memset(t_[64:128, :, :], 0.0), writes=[b_])
            smask = attmask[0:32, 0, 0:32].unsqueeze(1).to_broadcast([32, 16, 32])
            smv = lambda a: a.rearrange("p (h t) -> p h t", t=32)
            S0 = NMAIN * 128
            for b in range(NSTREAM):
                op("pool", lambda e: e.memset(kc_bf[0:96, 0, :], 0.0), writes=[b_kc])
                op("pool", lambda e: e.memset(vc_bf[0:96, 0, :], 0.0), writes=[b_vc])
                op("pool", lambda e: e.memset(kc_bf[96:128, 0, :], 0.0), writes=[b_kc])
                op("pool", lambda e: e.memset(vc_bf[96:128, 0, :], 0.0), writes=[b_vc])
                dma("pool", kc_bf[112:128, 0, :], cache_k[b, 0:16, :], writes=[b_kc])
                dma("pool", vc_bf[112:128, 0, :], cache_v[b, 0:16, :], writes=[b_vc])
                for half in range(2):
                    dma("pool", kc_bf[:, 1 + half * 4:5 + half * 4, :],
                        cache_k[b, 16 + half * 512:16 + (half + 1) * 512, :].rearrange("(kt p) d -> p kt d", p=128), swrites=[b_kc])
                    dma("pool", vc_bf[:, 1 + half * 4:5 + half * 4, :],
                        cache_v[b, 16 + half * 512:16 + (half + 1) * 512, :].rearrange("(kt p) d -> p kt d", p=128), swrites=[b_vc])
                for kt in range(9):
                    for h0 in (0, 8):
                        pt, b_pt = ps_next()
                        ptb = pt[:].bitcast(BF16)
                        for j in range(8):
                            op("pe", lambda e, j=j: e.transpose(out=ptb[0:64, j * 128:(j + 1) * 128],
                                                                in_=kc_bf[:, kt, (h0 + j) * 64:(h0 + j + 1) * 64], identity=ident_b[:]),
                               reads=[b_kc, b_identb], writes=[b_pt])
                        k.cp(kTc[0:64, h0:h0 + 8, kt * 128:(kt + 1) * 128], ptb[0:64, :].rearrange("p (h t) -> p h t", t=128),
                             reads=[b_pt], writes=[b_kTc])
                c0 = S0 + b * 32
                dma("sp", kTn[0:64, :, :], kT_s[:, :, NPT * 128 + b * 32:NPT * 128 + b * 32 + 32].rearrange("h d t -> d h t"), reads=[B_kT], writes=[b_kTn])
                dma("sp", qs[0:64, :, :], qT_s[:, :, c0:c0 + 32].rearrange("h d t -> d h t"), reads=[B_qT], writes=[b_qs])
                dma("sp", vn[:], v_s[NPT * 128 + b * 32:NPT * 128 + b * 32 + 32, :], reads=[B_v], writes=[b_vn])
                op_, b_op = ps_acc_next()
                blocks = []
                pvfirst = [True]
                for kt in ["new"] + list(range(8, -1, -1)):
                    jn = 32 if kt == "new" else 128

                    def zmm(ps, b_ps, final, kt=kt, jn=jn):
                        for h in range(16):
                            lhsT = kTn[:, h, :] if kt == "new" else kTc[:, h, kt * 128:(kt + 1) * 128]
                            op("pe", lambda e, h=h, lhsT=lhsT: e.matmul(ps[0:jn, h * 32:(h + 1) * 32], lhsT=lhsT, rhs=qs[:, h, :],
                                                                        start=(h == 0), stop=(final and h == 15), skip_group_check=True),
                               reads=[b_kTn if kt == "new" else b_kTc, b_qs], writes=[b_ps])

                    def pv(w_t, b_w, first, last, kt=kt, jn=jn, op_=op_, b_op=b_op):
                        for h in range(16):
                            pr = (h // 2) * 128
                            lhsT = vn[:, pr:pr + 128] if kt == "new" else vc_bf[:, kt, pr:pr + 128]
                            st_ = pvfirst[0]
                            pvfirst[0] = False
                            op("pe", lambda e, h=h, lhsT=lhsT, st_=st_: e.matmul(op_[:, h * 32:(h + 1) * 32], lhsT=lhsT,
                                                                                 rhs=w_t[0:jn, h * 32:(h + 1) * 32], start=st_,
                                                                                 stop=(last and h == 15), skip_group_check=True),
                               reads=[b_vn if kt == "new" else b_vc, b_w], writes=[b_op])
                    blocks.append(dict(jn=jn, zmm=zmm, mask=(smask if kt == "new" else None), mv=smv, pv=pv))
                sb_pipeline(blocks, 512, b_attmask)
                ao, b_ao = AB["ao"].next()
                op("dve", lambda e: e.tensor_copy(out=ao[:, :], in_=op_[:, :]), reads=[b_op], writes=[b_ao])
                for two in range(2):
                    dma("sp", attT_s[two::2, :, c0:c0 + 32].rearrange("h d t -> d h t"),
                        ao[two * 64:(two + 1) * 64, :].rearrange("d (hp x t) -> d hp x t", x=2, t=32)[:, :, two, :],
                        reads=[b_ao], swrites=[b_attT])

            checkpoint(3)
            k.new_phase()
            b_ssdT = k.buf("ssdT_s")
            gssd_bc, b_gssd = const_tile("gssd_bc", g_ssd.partition_broadcast(128), [128, SSDW])
            HT = k.sb("HT", [128, SSDW], F32)
            b_HT = k.buf("HT")
            HTb = k.sb("HTb", [128, SSDW], BF16)
            b_HTb = k.buf("HTb")
            HS = [(k.sb("HS%d" % b, [128, SSDW], F32), k.buf("HS%d" % b), k.sb("HSb%d" % b, [128, SSDW], BF16), k.buf("HSb%d" % b))
                  for b in range(NSTREAM)]
            op("pool", lambda e: e.memset(HT[:], 0.0), writes=[b_HT])
            op("pool", lambda e: e.memset(HTb[:], 0.0), writes=[b_HTb])
            hld = k.sb("hld", [128, 16, 128], F32)
            b_hld = k.buf("hld")
            xc_rot = Rot(k, "xc_rot", [128, 24, 128], BF16, 2)
            da_rot = Rot(k, "da_rot", [128, 32], F32, 2)
            LM = k.sb("LM", [128, 32, 128], BF16)
            b_LM = k.buf("LM")
            dec_rot = Rot(k, "dec_rot", [128, 6, 32], F32, 2)
            dmk_rot = Rot(k, "dmk_rot", [128, 4, 2, 32], F32, 1)
            xdt = k.sb("xdt", [128, SSDW], BF16)
            b_xdt = k.buf("xdt")
            xtok = k.sb("xtok", [128, SSDW], BF16)
            b_xtok = k.buf("xtok")
            xw = k.sb("xw", [128, SSDW], BF16)
            b_xw = k.buf("xw")
            Btok = k.sb("Btok", [128, 512], BF16)
            b_Btok = k.buf("Btok")
            CBm = k.sb("CBm", [128, 4, 128], F32)
            b_CBm = k.buf("CBm")
            expL_rot = Rot(k, "expL", [128, 512], F32, 2)
            MT_rot = Rot(k, "MT", [128, 512], BF16, 2)
            ysb = k.sb("ysb", [128, SSDW], F32)
            b_ysb = k.buf("ysb")
            zt_rot = Rot(k, "zt_rot", [128, SSDW], F32, 1)
            ynb = k.sb("ynb", [128, SSDW], BF16)
            b_ynb = k.buf("ynb")
            sT_rot = Rot(k, "sT_rot", [128, 16, 128], BF16, 1)
            hout, b_hout = hld, b_hld

            def load_state(b):
                H, b_H, Hb, b_Hb = HS[b]
                dma("sp", hld[:], st_ssm[b].rearrange("(c p) n -> p c n", p=128), writes=[b_hld])
                for c0 in range(0, 16, 4):
                    pt, b_pt = ps_next()
                    for j in range(4):
                        op("pe", lambda e, j=j: e.transpose(out=pt[:, j * 128:(j + 1) * 128], in_=hld[:, c0 + j, :], identity=ident_f[:]),
                           reads=[b_hld, b_identf], writes=[b_pt])
                    op("dve", lambda e: e.tensor_copy(out=H[:, c0 * 128:(c0 + 4) * 128], in_=pt[:]), reads=[b_pt], writes=[b_H])
                    op("act", lambda e: e.copy(out=Hb[:, c0 * 128:(c0 + 4) * 128], in_=H[:, c0 * 128:(c0 + 4) * 128]), reads=[b_H], writes=[b_Hb])

            def store_state(H, b_H, q5):
                for c0 in range(0, 16, 4):
                    pt, b_pt = ps_next()
                    for j in range(4):
                        op("pe", lambda e, j=j: e.transpose(out=pt[:, j * 128:(j + 1) * 128], in_=H[:, (c0 + j) * 128:(c0 + j + 1) * 128],
                                                            identity=ident_f[:]), reads=[b_H, b_identf], writes=[b_pt])
                    op("dve", lambda e: e.tensor_copy(out=hout[:, c0:c0 + 4, :], in_=pt[:].rearrange("p (c n) -> p c n", n=128)),
                       reads=[b_pt], writes=[b_hout])
                dma("sp", ssm_out[q5].rearrange("(c p) n -> p c n", p=128), hout[:], reads=[b_hout])

            xcs = {}

            def load_xc(i):
                xc, b_xc = xc_rot.next()
                dma("sp", xc[:], xc_s[:, :, i * 128:(i + 1) * 128].rearrange("c p t -> p c t"), reads=[B_xc], writes=[b_xc])
                xcs[i] = (xc, b_xc)

            for i in range(NT):
                issample = i == NT - 1
                need_y = i >= NPRE
                if issample:
                    store_state(HT, b_HT, 0)
                    for b in range(NSTREAM):
                        load_state(b)
                    blocks = [(b * 32, 32, HS[b]) for b in range(NSTREAM)]
                    m_le_f, bm_lef, m_le_b, bm_leb, m_gt_f, bm_gtf = leS_f, b_leSf, leS_b, b_leSb, gtS_f, b_gtSf
                else:
                    blocks = [(0, 128, (HT, b_HT, HTb, b_HTb))]
                    m_le_f, bm_lef, m_le_b, bm_leb, m_gt_f, bm_gtf = le_f, b_lef, le_b, b_leb, gt_f, b_gtf
                nb = len(blocks)
                if i == 0:
                    load_xc(0)
                if i + 1 < NT:
                    load_xc(i + 1)
                xc, b_xc = xcs.pop(i)
                if need_y:
                    zt, b_zt = zt_rot.next()
                    dma("sp", zt[:], z_s[mloc(i) * 128:(mloc(i) + 1) * 128, :], reads=[B_z], writes=[b_zt])
                xsT = lambda c: xc[:, c, :]
                BT = lambda g: xc[:, 16 + g, :]
                CT = lambda g: xc[:, 20 + g, :]
                da, b_da = da_rot.next()
                op("dve", lambda e: e.tensor_tensor(out=da[:], in0=dtm[:, i, :], in1=a_bc[:], op=ALU.mult),
                   reads=[b_dtm[i], b_abc], writes=[b_da])
                if need_y:
                    op("pool", lambda e: e.tensor_tensor(out=LM[:], in0=m_gt_f[:].unsqueeze(1).to_broadcast([128, 32, 128]),
                                                         in1=da[:].unsqueeze(2).to_broadcast([128, 32, 128]), op=ALU.mult),
                       reads=[bm_gtf, b_da], writes=[b_LM])
                pd, b_pd = ps_next()
                op("pe", lambda e: e.matmul(pd[:, 0:32], lhsT=m_le_f[:], rhs=da[:], start=True, stop=True),
                   reads=[bm_lef, b_da], writes=[b_pd])
                op("pe", lambda e: e.matmul(pd[:, 32:64], lhsT=m_gt_f[:], rhs=da[:], start=True, stop=True),
                   reads=[bm_gtf, b_da], writes=[b_pd])
                for bi in range(nb):
                    lhsT = blkS_f[:, bi, :] if issample else ones_f[:]
                    op("pe", lambda e, bi=bi, lhsT=lhsT: e.matmul(pd[:, 64 + bi * 32:96 + bi * 32], lhsT=lhsT, rhs=da[:], start=True, stop=True),
                       reads=[b_blkSf, b_onesf, b_da], writes=[b_pd])
                dec, b_dec = dec_rot.next()
                op("act", lambda e: e.activation(out=dec[:, 0:2 + nb, :], in_=pd[:, 0:64 + nb * 32].rearrange("p (a h) -> p a h", h=32),
                                                 func=AF.Exp), reads=[b_pd], writes=[b_dec])
                decay_t = dec[:, 0, :]
                toend = dec[:, 1, :]
                for half in range(2):
                    pt, b_pt = ps_next()
                    ptb = pt[:].bitcast(BF16)
                    for j in range(8):
                        c = half * 8 + j
                        op("pe", lambda e, j=j, c=c: e.transpose(out=ptb[:, j * 128:(j + 1) * 128], in_=xsT(c), identity=ident_b[:]),
                           reads=[b_xc, b_identb], writes=[b_pt])
                    sl = slice(half * 1024, (half + 1) * 1024)
                    op("act", lambda e: e.copy(out=xtok[:, sl], in_=ptb[:, :]), reads=[b_pt], writes=[b_xtok])
                    op("dve", lambda e: e.tensor_tensor(out=xdt[:, sl].rearrange("p (h d) -> p h d", d=64),
                                                        in0=xtok[:, sl].rearrange("p (h d) -> p h d", d=64),
                                                        in1=dtm[:, i, half * 16:(half + 1) * 16].unsqueeze(2).to_broadcast([128, 16, 64]),
                                                        op=ALU.mult), reads=[b_xtok, b_dtm[i]], writes=[b_xdt])
                pt, b_pt = ps_next()
                ptb = pt[:].bitcast(BF16)
                for g in range(4):
                    op("pe", lambda e, g=g: e.transpose(out=ptb[:, g * 128:(g + 1) * 128], in_=BT(g), identity=ident_b[:]),
                       reads=[b_xc, b_identb], writes=[b_pt])
                op("act", lambda e: e.copy(out=Btok[:], in_=ptb[:, 0:512]), reads=[b_pt], writes=[b_Btok])
                dmk, b_dmk = dmk_rot.next()
                if issample:
                    for bi in range(nb):
                        op("pool", lambda e, bi=bi: e.tensor_tensor(out=dmk[:, bi, :, :], in0=dec[:, 0:2, :],
                                                                    in1=blkS_f[:, bi, 0:32].unsqueeze(1).to_broadcast([128, 2, 32]),
                                                                    op=ALU.mult), reads=[b_dec, b_blkSf], writes=[b_dmk])
                if need_y:
                    pc, b_pc = ps_next()
                    for g in range(4):
                        op("pe", lambda e, g=g: e.matmul(pc[:, g * 128:(g + 1) * 128], lhsT=BT(g), rhs=CT(g), start=True, stop=True),
                           reads=[b_xc], writes=[b_pc])
                    op("dve", lambda e: e.tensor_tensor(out=CBm[:], in0=pc[:].rearrange("p (g t) -> p g t", t=128),
                                                        in1=m_le_f[:].unsqueeze(1).to_broadcast([128, 4, 128]), op=ALU.mult),
                       reads=[b_pc, bm_lef], writes=[b_CBm])
                    for g in range(4):
                        for bi, (p0, pn, (H, b_H, Hb, b_Hb)) in enumerate(blocks):
                            po, b_po = ps_next()
                            op("pe", lambda e, Hb=Hb: e.matmul(po[:], lhsT=CT(g), rhs=Hb[:, g * 512:(g + 1) * 512], start=True, stop=True),
                               reads=[b_xc, b_Hb], writes=[b_po])
                            dcy = dmk[:, bi, 0, :] if issample else decay_t
                            ysl = ysb[:, g * 512:(g + 1) * 512].rearrange("p (h d) -> p h d", d=64)
                            if bi == 0:
                                op("dve", lambda e, g=g, dcy=dcy, ysl=ysl: e.tensor_tensor(
                                    out=ysl, in0=po[:].rearrange("p (h d) -> p h d", d=64),
                                    in1=dcy[:, g * 8:(g + 1) * 8].unsqueeze(2).to_broadcast([128, 8, 64]), op=ALU.mult),
                                   reads=[b_po, b_dec, b_dmk], writes=[b_ysb])
                            else:
                                jv = junk_f[:, 0:512].rearrange("p (h d) -> p h d", d=64)
                                op("dve", lambda e, g=g, dcy=dcy, jv=jv: e.tensor_tensor(
                                    out=jv, in0=po[:].rearrange("p (h d) -> p h d", d=64),
                                    in1=dcy[:, g * 8:(g + 1) * 8].unsqueeze(2).to_broadcast([128, 8, 64]), op=ALU.mult),
                                   reads=[b_po, b_dec, b_dmk], writes=[b_junk])
                                op("dve", lambda e, g=g: e.tensor_tensor(out=ysb[:, g * 512:(g + 1) * 512], in0=ysb[:, g * 512:(g + 1) * 512],
                                                                         in1=junk_f[:, 0:512], op=ALU.add),
                                   reads=[b_junk, b_ysb], writes=[b_ysb])
                for bi, (p0, pn, (H, b_H, Hb, b_Hb)) in enumerate(blocks):
                    dH = dec[:, 2 + bi, :]
                    te = dmk[:, bi, 1, :] if issample else toend
                    op("pool" if need_y else "dve", lambda e, te=te: e.tensor_tensor(out=xw[:].rearrange("p (h d) -> p h d", d=64),
                                                                in0=xdt[:].rearrange("p (h d) -> p h d", d=64),
                                                                in1=te.unsqueeze(2).to_broadcast([128, 32, 64]), op=ALU.mult),
                       reads=[b_xdt, b_dec, b_dmk], writes=[b_xw])
                    for g in range(4):
                        pu, b_pu = ps_next()
                        op("pe", lambda e, g=g: e.matmul(pu[:], lhsT=Btok[:, g * 128:(g + 1) * 128],
                                                         rhs=xw[:, g * 512:(g + 1) * 512], start=True, stop=True),
                           reads=[b_Btok, b_xw], writes=[b_pu])
                        op("pool" if need_y else "dve", lambda e, g=g: e.tensor_tensor(out=H[:, g * 512:(g + 1) * 512].rearrange("p (h d) -> p h d", d=64),
                                                                  in0=H[:, g * 512:(g + 1) * 512].rearrange("p (h d) -> p h d", d=64),
                                                                  in1=dH[:, g * 8:(g + 1) * 8].unsqueeze(2).to_broadcast([128, 8, 64]),
                                                                  op=ALU.mult), reads=[b_H, b_dec], writes=[b_H])
                        op("dve", lambda e, g=g: e.tensor_tensor(out=H[:, g * 512:(g + 1) * 512], in0=pu[:], in1=H[:, g * 512:(g + 1) * 512],
                                                                 op=ALU.add), reads=[b_pu, b_H], writes=[b_H])
                    if not issample:
                        op("act", lambda e: e.copy(out=Hb[:], in_=H[:]), reads=[b_H], writes=[b_Hb])
                    else:
                        store_state(H, b_H, 1 + bi)
                if need_y:
                    for g in range(4):
                        py, b_py = ps_acc_next()
                        for hb in range(2):
                            h0 = g * 8 + hb * 4
                            pL, b_pL = ps_next()
                            for j in range(4):
                                op("pe", lambda e, j=j: e.matmul(pL[:, j * 128:(j + 1) * 128], lhsT=LM[:, h0 + j, :], rhs=m_le_b[:],
                                                                 start=True, stop=True), reads=[b_LM, bm_leb], writes=[b_pL])
                            eL, b_eL = expL_rot.next()
                            op("act", lambda e: e.activation(out=eL[:], in_=pL[:], func=AF.Exp), reads=[b_pL], writes=[b_eL])
                            MT, b_MT = MT_rot.next()
                            op("dve", lambda e, g=g: e.tensor_tensor(out=MT[:].rearrange("p (h t) -> p h t", t=128),
                                                                     in0=eL[:].rearrange("p (h t) -> p h t", t=128),
                                                                     in1=CBm[:, g, :].unsqueeze(1).to_broadcast([128, 4, 128]),
                                                                     op=ALU.mult), reads=[b_eL, b_CBm], writes=[b_MT])
                            for j in range(4):
                                hh = hb * 4 + j
                                op("pe", lambda e, j=j, hh=hh: e.matmul(py[:, hh * 64:(hh + 1) * 64], lhsT=MT[:, j * 128:(j + 1) * 128],
                                                                        rhs=xdt[:, (h0 + j) * 64:(h0 + j + 1) * 64], start=True, stop=True),
                                   reads=[b_MT, b_xdt], writes=[b_py])
                        op("dve", lambda e, g=g: e.tensor_tensor(out=ysb[:, g * 512:(g + 1) * 512], in0=py[:], in1=ysb[:, g * 512:(g + 1) * 512],
                                                                 op=ALU.add), reads=[b_py, b_ysb], writes=[b_ysb])
                    op("pool", lambda e: e.tensor_tensor(out=junk_f[:].rearrange("p (h d) -> p h d", d=64),
                                                         in0=xtok[:].rearrange("p (h d) -> p h d", d=64),
                                                         in1=dsk_bc[:].unsqueeze(2).to_broadcast([128, 32, 64]), op=ALU.mult),
                       reads=[b_xtok, b_dsk], writes=[b_junk])
                    op("dve", lambda e: e.tensor_tensor(out=ysb[:], in0=ysb[:], in1=junk_f[:], op=ALU.add),
                       reads=[b_ysb, b_junk], writes=[b_ysb])
                    op("dve", lambda e: e.tensor_tensor(out=ysb[:], in0=ysb[:], in1=zt[:], op=ALU.mult),
                       reads=[b_ysb, b_zt], writes=[b_ysb])
                    ss, b_ss = small.next()
                    for g in range(4):
                        op("act", lambda e, g=g: e.activation(out=junk_f[:, g * 512:(g + 1) * 512], in_=ysb[:, g * 512:(g + 1) * 512],
                                                              func=AF.Square, accum_out=ss[:, g:g + 1]),
                           reads=[b_ysb], writes=[b_junk, b_ss])
                    op("act", lambda e: e.activation(out=ss[:, 0:4], in_=ss[:, 0:4], func=AF.Sqrt, scale=1.0 / 512, bias=EPS),
                       reads=[b_ss], writes=[b_ss])
                    op("dve", lambda e: e.reciprocal(out=ss[:, 4:8], in_=ss[:, 0:4]), reads=[b_ss], writes=[b_ss])
                    op("dve", lambda e: e.tensor_tensor(out=ysb[:].rearrange("p (g c) -> p g c", c=512),
                                                        in0=ysb[:].rearrange("p (g c) -> p g c", c=512),
                                                        in1=ss[:, 4:8].unsqueeze(2).to_broadcast([128, 4, 512]), op=ALU.mult),
                       reads=[b_ysb, b_ss], writes=[b_ysb])
                    op("dve", lambda e: e.tensor_tensor(out=ynb[:], in0=ysb[:], in1=gssd_bc[:], op=ALU.mult),
                       reads=[b_ysb, b_gssd], writes=[b_ynb])
                    sT, b_sT = sT_rot.next()
                    transpose_to(ynb, b_ynb, 16, lambda c0, n: sT[:, c0:c0 + n, :], b_sT)
                    dma("sp", ssdT_s[:, :, mloc(i) * 128:(mloc(i) + 1) * 128].rearrange("c p t -> p c t"), sT[:], reads=[b_sT],
                        swrites=[b_ssdT])

            checkpoint(4)
            k.new_phase()
            wba = k.sb("wba", [128, 8, D], BF16)
            b_wba = k.buf("wba")
            wbs = k.sb("wbs", [128, 16, D], BF16)
            b_wbs = k.buf("wbs")
            for hf in range(2):
                dma("pool", wba[:, hf * 4:(hf + 1) * 4, :], w_br_att[hf * 512:(hf + 1) * 512, :].rearrange("(h p) c -> p h c", p=128), swrites=[b_wba])
            for q4 in range(4):
                dma("pool", wbs[:, q4 * 4:(q4 + 1) * 4, :], w_br_ssd[q4 * 512:(q4 + 1) * 512, :].rearrange("(c p) n -> p c n", p=128), swrites=[b_wbs])
            aT_rot = Rot(k, "aT_rot", [128, 8, 512], BF16, 2)
            sTg_rot = Rot(k, "sTg_rot", [128, 16, 512], BF16, 2)
            gT_rot = Rot(k, "gT_rot", [128, 2, 512], F32, 2)
            mo_rot = Rot(k, "mo_rot", [128, 512], BF16, 2)
            t1_rot = Rot(k, "t1_rot", [128, 512], F32, 4)
            pfa = {}

            def load5a(gi):
                grp = G_M1[gi]
                t0 = (grp[0] - NPRE) * 128
                n = len(grp) * 128
                aT, b_aT = aT_rot.next()
                for two in range(2):
                    dma("sp", aT[two * 64:(two + 1) * 64, :, 0:n], attT_s[two::2, :, t0:t0 + n].rearrange("h d t -> d h t"),
                        reads=[b_attT], swrites=[b_aT])
                sTg, b_sTg = sTg_rot.next()
                dma("sp", sTg[:, :, 0:n], ssdT_s[:, :, t0:t0 + n].rearrange("c p t -> p c t"), reads=[b_ssdT], writes=[b_sTg])
                pfa[gi] = (aT, b_aT, sTg, b_sTg)

            load5a(0)
            for gi, grp in enumerate(G_M1):
                t0 = (grp[0] - NPRE) * 128
                n = len(grp) * 128
                if gi + 1 < len(G_M1):
                    load5a(gi + 1)
                aT, b_aT, sTg, b_sTg = pfa.pop(gi)
                for j in range(8):
                    gT, b_gT = gT_rot.next()
                    dma("sp", gT[:, 0, 0:n], g_s[j, :, t0:t0 + n], reads=[B_g], writes=[b_gT])
                    dma("sp", gT[:, 1, 0:n], g_s[8 + j, :, t0:t0 + n], reads=[B_g], writes=[b_gT])
                    pa, b_pa = ps_next()
                    for h in range(8):
                        op("pe", lambda e, h=h: e.matmul(pa[:, 0:n], lhsT=wba[:, h, j * 128:(j + 1) * 128], rhs=aT[:, h, 0:n],
                                                         start=(h == 0), stop=(h == 7)), reads=[b_wba, b_aT], writes=[b_pa])
                    pb, b_pb = ps_next()
                    for c in range(16):
                        op("pe", lambda e, c=c: e.matmul(pb[:, 0:n], lhsT=wbs[:, c, j * 128:(j + 1) * 128], rhs=sTg[:, c, 0:n],
                                                         start=(c == 0), stop=(c == 15)), reads=[b_wbs, b_sTg], writes=[b_pb])
                    t1, b_t1 = t1_rot.next()
                    op("dve", lambda e: e.tensor_tensor(out=t1[:, 0:n], in0=pa[:, 0:n], in1=gT[:, 0, 0:n], op=ALU.mult),
                       reads=[b_pa, b_gT], writes=[b_t1])
                    t2, b_t2 = t1_rot.next()
                    op("dve", lambda e: e.tensor_tensor(out=t2[:, 0:n], in0=pb[:, 0:n], in1=gT[:, 1, 0:n], op=ALU.mult),
                       reads=[b_pb, b_gT], writes=[b_t2])
                    mo, b_mo = mo_rot.next()
                    op("pool", lambda e: e.tensor_tensor(out=mo[:, 0:n], in0=t1[:, 0:n], in1=t2[:, 0:n], op=ALU.add),
                       reads=[b_t1, b_t2], writes=[b_mo])
                    dma("sp", mT_s[j, :, t0:t0 + n], mo[:, 0:n], reads=[b_mo], swrites=[B_mT])

            checkpoint(5)
            k.new_phase()
            gffn_bc, b_gffn = const_tile("gffn_bc", g_ffn.partition_broadcast(128), [128, D])
            wo = k.sb("wo", [128, 8, D], BF16)
            b_wo = k.buf("wo")
            for hf in range(2):
                dma("pool", wo[:, hf * 4:(hf + 1) * 4, :], w_out[hf * 512:(hf + 1) * 512, :].rearrange("(c p) n -> p c n", p=128), swrites=[b_wo])
            uT = k.sb("u2T", [128, 8, TOKM], BF16)
            b_uT = {i: k.buf("u2T%d" % i) for i in tiles_m1}
            mT_rot = Rot(k, "mT_rot", [128, 8, 128], BF16, 2)
            xrot = Rot(k, "xrot2", [128, D], F32, 2)
            ubrot = Rot(k, "ubrot2", [128, D], BF16, 2)
            hrot = Rot(k, "hrot", [128, D], F32, 2)
            pf5 = {}

            def load5(i):
                ml = mloc(i)
                mT, b_mT = mT_rot.next()
                dma("sp", mT[:], mT_s[:, :, ml * 128:(ml + 1) * 128].rearrange("c p t -> p c t"), reads=[B_mT], writes=[b_mT])
                xt, b_xt = xrot.next()
                dma("sp", xt[:], xin[i * 128:(i + 1) * 128, :], writes=[b_xt])
                pf5[i] = (mT, b_mT, xt, b_xt)

            load5(tiles_m1[0])
            for ii5, i in enumerate(tiles_m1):
                ml = mloc(i)
                if ii5 + 1 < len(tiles_m1):
                    load5(tiles_m1[ii5 + 1])
                mT, b_mT, xt, b_xt = pf5.pop(i)
                ht, b_ht = hrot.next()
                for hf in range(2):
                    ph, b_ph = ps_next()
                    for kc in range(8):
                        op("pe", lambda e, kc=kc: e.matmul(ph[:], lhsT=mT[:, kc, :], rhs=wo[:, kc, hf * 512:(hf + 1) * 512],
                                                           start=(kc == 0), stop=(kc == 7)), reads=[b_mT, b_wo], writes=[b_ph])
                    op("dve", lambda e, hf=hf: e.tensor_tensor(out=ht[:, hf * 512:(hf + 1) * 512], in0=ph[:], in1=xt[:, hf * 512:(hf + 1) * 512],
                                                               op=ALU.add), reads=[b_ph, b_xt], writes=[b_ht])
                dma("sp", h_s[ml * 128:(ml + 1) * 128, :], ht[:], reads=[b_ht], swrites=[B_h])
                ub, b_ub = ubrot.next()
                rmsnorm_to_bf16(ht[:], b_ht, gffn_bc[:], b_gffn, ub[:], b_ub, junk_f[:, 0:D], b_junk)
                transpose_to(ub, b_ub, 8, lambda c0, n_, ml=ml: uT[:, c0:c0 + n_, ml * 128:(ml + 1) * 128], b_uT[i])

            fch = k.sb("fch", [128, 44, 8], F32)
            b_fch = k.buf("fch")
            f8r = Rot(k, "f8", [8, 512], F32, 2)
            for c0 in range(0, 44, 4):
                f8, b_f8 = f8r.next()
                dma("sp", f8[:], st_fconv[:, c0 * 128:(c0 + 4) * 128], writes=[b_f8])
                pt, b_pt = ps_next()
                for j in range(4):
                    op("pe", lambda e, j=j: e.transpose(out=pt[:, j * 8:(j + 1) * 8], in_=f8[:, j * 128:(j + 1) * 128],
                                                        identity=ident_f[0:8, 0:8]), reads=[b_f8, b_identf], writes=[b_pt])
                op("dve", lambda e: e.tensor_copy(out=fch[:, c0:c0 + 4, :], in_=pt[:, 0:32].rearrange("p (c t) -> p c t", t=8)),
                   reads=[b_pt], writes=[b_fch])
            fconv_sb = k.sb("fconv_sb", [128, 44, 5, 2], F32)
            b_fconv = k.buf("fconv_sb")
            ev16 = Rot(k, "ev16b", [128, 512], BF16, 2)
            fraw = Rot(k, "fraw", [128, 4, 2 + 128], F32, 6)
            facc = Rot(k, "facc", [128, 512], F32, 4)
            dq_f = Deferred(3)
            wrot2 = Rot(k, "wrot2", [128, 8, 256], BF16, 2)
            for c in range(22):
                wt, b_wt = wrot2.next()
                dma("pool", wt[:, :, 0:128], w_up[:, c * 128:(c + 1) * 128].rearrange("(kc p) c -> p kc c", p=128), writes=[b_wt])
                dma("pool", wt[:, :, 128:256], w_up[:, DFF + c * 128:DFF + (c + 1) * 128].rearrange("(kc p) c -> p kc c", p=128), writes=[b_wt])
                prev = {0: None, 1: None}
                for grp in G_M1:
                    accs = []
                    for part in range(2):
                        ch = c + part * 22

                        def cons_f(grp, t0, n, pt, b_pt, ch=ch, part=part, prev=prev, accs=accs):
                            xr, b_xr = fraw.next()
                            acc, b_acc = facc.next()
                            if grp[0] != NT - 1:
                                flat = xr[:].rearrange("p s t -> p (s t)")
                                if prev[part] is None:
                                    op("pool", lambda e: e.memset(flat[:, 0:2], 0.0), writes=[b_xr])
                                else:
                                    pf, b_pf, pn = prev[part]
                                    op("pool", lambda e: e.tensor_copy(out=flat[:, 0:2], in_=pf[:, pn:pn + 2]), reads=[b_pf], writes=[b_xr])
                                op("act", lambda e: e.copy(out=flat[:, 2:2 + n], in_=pt[:, 0:n]), reads=[b_pt], writes=[b_xr])
                                prev[part] = (flat, b_xr, n)
                                if grp[-1] == NPT - 1:
                                    op("pool", lambda e: e.tensor_copy(out=fconv_sb[:, ch, 0, :], in_=flat[:, n:n + 2]),
                                       reads=[b_xr], writes=[b_fconv])
                                op("act", lambda e: e.activation(out=acc[:, 0:n], in_=pt[:, 0:n], func=AF.Identity,
                                                                 scale=cwf[:, ch, 2:3], bias=cbf[:, ch:ch + 1]),
                                   reads=[b_pt, b_cwf, b_cbf], writes=[b_acc])
                                for ii in range(0, 2):
                                    op("dve", lambda e, ii=ii: e.scalar_tensor_tensor(out=acc[:, 0:n], in0=flat[:, ii:ii + n],
                                                                                      scalar=cwf[:, ch, ii:ii + 1], in1=acc[:, 0:n],
                                                                                      op0=ALU.mult, op1=ALU.add),
                                       reads=[b_xr, b_cwf, b_acc], writes=[b_acc])
                            else:
                                op("pool", lambda e: e.tensor_copy(out=xr[:, :, 0:2], in_=fch[:, ch, :].rearrange("p (s t) -> p s t", t=2)),
                                   reads=[b_fch], writes=[b_xr])
                                op("act", lambda e: e.copy(out=xr[:, :, 2:34], in_=pt[:, 0:128].rearrange("p (s t) -> p s t", t=32)),
                                   reads=[b_pt], writes=[b_xr])
                                op("pool", lambda e: e.tensor_copy(out=fconv_sb[:, ch, 1:5, :], in_=xr[:, :, 16:18]),
                                   reads=[b_xr], writes=[b_fconv])
                                a3 = acc[:, 0:128].rearrange("p (s t) -> p s t", t=32)
                                op("act", lambda e: e.activation(out=acc[:, 0:128], in_=pt[:, 0:128], func=AF.Identity,
                                                                 scale=cwf[:, ch, 2:3], bias=cbf[:, ch:ch + 1]),
                                   reads=[b_pt, b_cwf, b_cbf], writes=[b_acc])
                                for ii in range(0, 2):
                                    op("dve", lambda e, ii=ii: e.scalar_tensor_tensor(out=a3, in0=xr[:, :, ii:ii + 32],
                                                                                      scalar=cwf[:, ch, ii:ii + 1], in1=a3,
                                                                                      op0=ALU.mult, op1=ALU.add),
                                       reads=[b_xr, b_cwf, b_acc], writes=[b_acc])
                            accs.append((acc, b_acc))
                        featmajor(wt, b_wt, part * 128, 128, grp, cons_f, tofs=NPRE * 128, defer=dq_f)

                    def fin_f(accs=accs, grp=grp, c=c):
                        (ga, b_ga), (va, b_va) = accs
                        n = len(grp) * 128
                        tl = (grp[0] - NPRE) * 128
                        op("act", lambda e: e.activation(out=ga[:, 0:n], in_=ga[:, 0:n], func=AF.Silu), reads=[b_ga], writes=[b_ga])
                        ao_, b_ao_ = ev16.next()
                        op("pool", lambda e: e.tensor_tensor(out=ao_[:, 0:n], in0=ga[:, 0:n], in1=va[:, 0:n], op=ALU.mult),
                           reads=[b_ga, b_va], writes=[b_ao_])
                        dma("sp", act_s[c, :, tl:tl + n], ao_[:, 0:n], reads=[b_ao_], swrites=[B_act])
                    dq_f.push(fin_f)
            dq_f.flush()
            for q5 in range(5):
                dma("sp", fconv_out[q5].rearrange("c p t -> p c t"), fconv_sb[:, :, q5, :], reads=[b_fconv])

            checkpoint(6)
            k.new_phase()
            gfin_bc, b_gfin = const_tile("gfin_bc", g_final.partition_broadcast(128), [128, D])
            hrot = Rot(k, "hrot7", [128, D], F32, 2)
            aTl_rot = Rot(k, "aTl", [128, 22, 128], BF16, 2)
            wd = k.sb("wd", [128, 22, D], BF16)
            b_wd = k.buf("wd")
            for c0 in range(0, 22, 4):
                n4 = min(4, 22 - c0)
                dma("pool", wd[:, c0:c0 + n4, :], w_down[c0 * 128:(c0 + n4) * 128, :].rearrange("(c p) n -> p c n", p=128), swrites=[b_wd])
            yrot = Rot(k, "yrot", [128, D], F32, 2)
            pf7 = {}

            def load7(i):
                ml = mloc(i)
                ht, b_ht = hrot.next()
                dma("sp", ht[:], h_s[ml * 128:(ml + 1) * 128, :], reads=[B_h], writes=[b_ht])
                aTl, b_aTl = aTl_rot.next()
                dma("sp", aTl[:], act_s[:, :, ml * 128:(ml + 1) * 128].rearrange("c p t -> p c t"), reads=[B_act], writes=[b_aTl])
                pf7[i] = (ht, b_ht, aTl, b_aTl)

            load7(tiles_m1[0])
            for ii7, i in enumerate(tiles_m1):
                ml = mloc(i)
                if ii7 + 1 < len(tiles_m1):
                    load7(tiles_m1[ii7 + 1])
                ht, b_ht, aTl, b_aTl = pf7.pop(i)
                for hf in range(2):
                    pdn, b_pdn = ps_next()
                    for c in range(22):
                        op("pe", lambda e, c=c: e.matmul(pdn[:], lhsT=aTl[:, c, :], rhs=wd[:, c, hf * 512:(hf + 1) * 512],
                                                         start=(c == 0), stop=(c == 21)), reads=[b_aTl, b_wd], writes=[b_pdn])
                    op("dve", lambda e, hf=hf: e.tensor_tensor(out=ht[:, hf * 512:(hf + 1) * 512], in0=pdn[:], in1=ht[:, hf * 512:(hf + 1) * 512],
                                                               op=ALU.add), reads=[b_pdn, b_ht], writes=[b_ht])
                yt, b_yt = yrot.next()
                rmsnorm_to_bf16(ht[:], b_ht, gfin_bc[:], b_gfin, yt[:], b_yt, junk_f[:, 0:D], b_junk)
                dma("sp", y_out[ml * 128:(ml + 1) * 128, :], yt[:], reads=[b_yt])

        try:
            body()
        except _Stop:
            pass
        k.finish()
        k.pstack.close()
        print("program: instructions=%d waits=%d" % (k.n_ins, k.n_wait))
    return nc


def _consts():
    j = np.arange(128)
    strict = (j[:, None] < j[None, :]).astype(np.float32)
    attmask = np.zeros((128, 4, 512), np.float32)
    for r in range(4):
        attmask[:, r, r * 128:(r + 1) * 128] = strict
        attmask[:, r, (r + 1) * 128:] = 1.0
    trineg = -(j[:, None] >= j[None, :]).astype(np.float32)
    le = (j[:, None] <= j[None, :]).astype(np.float32)
    gt = (j[:, None] > j[None, :]).astype(np.float32)
    blk = j // 32
    same = (blk[:, None] == blk[None, :]).astype(np.float32)
    blkS = np.zeros((128, 4, 128), np.float32)
    for b in range(4):
        blkS[blk == b, b, :] = 1.0
    return {"c_ident": np.eye(128, dtype=np.float32), "c_attmask": attmask, "c_trineg": trineg, "c_le": le, "c_gt": gt,
            "c_leS": le * same, "c_gtS": gt * same, "c_blkS": blkS}


_NC_CACHE = {}


def kernel(x_prompt, x_sample, cache_k, cache_v, state_ssm, state_ssm_conv, state_ffn_conv, meta_tokens,
           g_mix, w_in, conv_w_ssd, conv_b_ssd, dt_bias, a_log, d_skip, g_ssd_norm, w_br_att, w_br_ssd,
           w_out, g_ffn, w_up, conv_w_ffn, conv_b_ffn, w_down, g_final):
    f = lambda a: np.ascontiguousarray(np.asarray(a, dtype=np.float32))
    x_prompt, x_sample, cache_k, cache_v = f(x_prompt), f(x_sample), f(cache_k), f(cache_v)
    state_ssm, state_ssm_conv, state_ffn_conv, meta_tokens = f(state_ssm), f(state_ssm_conv), f(state_ffn_conv), f(meta_tokens)
    if "nc" not in _NC_CACHE:
        _NC_CACHE["nc"] = build_program()
    nc = _NC_CACHE["nc"]
    consts = _consts()
    shared = {
        "w_in": f(w_in[0]), "w_br_att": f(w_br_att[0]), "w_br_ssd": f(w_br_ssd[0]), "w_out": f(w_out[0]),
        "w_up": f(w_up[0]), "w_down": f(w_down[0]),
        "g_mix": f(g_mix[0]).reshape(1, D), "g_ffn": f(g_ffn[0]).reshape(1, D), "g_final": f(g_final).reshape(1, D),
        "g_ssd": f(g_ssd_norm[0]).reshape(1, SSDW),
        "cw_ssd": f(f(conv_w_ssd[0]).reshape(4, 24, 128).transpose(2, 1, 0)),
        "cb_ssd": f(f(conv_b_ssd[0]).reshape(24, 128).T),
        "cw_ffn": f(f(conv_w_ffn[0]).reshape(3, 44, 128).transpose(2, 1, 0)),
        "cb_ffn": f(f(conv_b_ffn[0]).reshape(44, 128).T),
        "dt_bias": f(dt_bias[0]).reshape(1, 32), "a_log": f(a_log[0]).reshape(1, 32), "d_skip": f(d_skip[0]).reshape(1, 32),
    }
    shared.update(consts)
    in_maps = []
    for c in range(8):
        b, half = c // 2, c % 2
        seq = np.zeros((33 * 128, D), np.float32)
        seq[112:128] = meta_tokens
        seq[128:] = x_prompt[b]
        smask = np.zeros((33 * 128,), np.float32)
        smask[112:] = 1.0
        xin = np.zeros((TOK, D), np.float32)
        mask = np.zeros((TOK,), np.float32)
        if half == 0:
            xin[NPRE * 128:NPT * 128] = seq[0:NMAIN * 128]
            mask[NPRE * 128:NPT * 128] = smask[0:NMAIN * 128]
        else:
            xin[0:NPRE * 128] = seq[0:NPRE * 128]
            mask[0:NPRE * 128] = smask[0:NPRE * 128]
            xin[NPRE * 128:NPT * 128] = seq[NPRE * 128:33 * 128]
            mask[NPRE * 128:NPT * 128] = smask[NPRE * 128:33 * 128]
        for s in range(NSTREAM):
            r0 = NPT * 128 + s * 32
            xin[r0:r0 + 16] = x_sample[c * NSTREAM + s]
            mask[r0:r0 + 16] = 1.0
        m = dict(shared)
        m["xin"] = xin
        m["tmask"] = f(mask.reshape(NT, 128).T)
        sl = slice(c * NSTREAM, (c + 1) * NSTREAM)
        m["cache_k"] = f(cache_k[0, sl].reshape(NSTREAM, NCACHE, SBW))
        m["cache_v"] = f(cache_v[0, sl].reshape(NSTREAM, NCACHE, SBW))
        m["st_ssm"] = f(state_ssm[0, sl].reshape(NSTREAM, SSDW, 128))
        m["st_sconv"] = f(state_ssm_conv[0, sl].reshape(NSTREAM * 3, XBC))
        m["st_fconv"] = f(state_ffn_conv[0, sl].reshape(NSTREAM * 2, 2 * DFF))
        in_maps.append(m)
    res = run_bass_kernel_spmd(nc, in_maps, core_ids=list(range(8)))
    R = res.results
    _LAST["R"] = R
    _LAST["in_maps"] = in_maps

    BATCH, SEQ, DB, DS = 4, 4096, 32, 16
    y_prompt = np.zeros((BATCH, SEQ, D), np.float32)
    y_sample = np.zeros((DB, DS, D), np.float32)
    k_prompt = np.zeros((1, BATCH, 16 + SEQ, 16, 64), np.float32)
    v_prompt = np.zeros_like(k_prompt)
    ssm_prompt = np.zeros((1, BATCH, 32, 64, 128), np.float32)
    sconv_prompt = np.zeros((1, BATCH, 3, XBC), np.float32)
    fconv_prompt = np.zeros((1, BATCH, 2, 2 * DFF), np.float32)
    k_sample = np.zeros((1, DB, DS, 16, 64), np.float32)
    v_sample = np.zeros_like(k_sample)
    ssm_sample = np.zeros((1, DB, 32, 64, 128), np.float32)
    sconv_sample = np.zeros((1, DB, 3, XBC), np.float32)
    fconv_sample = np.zeros((1, DB, 2, 2 * DFF), np.float32)
    for c in range(8):
        b, half = c // 2, c % 2
        r = R[c]
        yo, ko, vo = r["y_out"], r["k_out"], r["v_out"]
        kp = k_prompt[0, b].reshape(16 + SEQ, SBW)
        vp = v_prompt[0, b].reshape(16 + SEQ, SBW)
        if half == 0:
            y_prompt[b, 0:16 * 128] = yo[128:17 * 128]
            kp[0:16] = ko[112:128]
            vp[0:16] = vo[112:128]
            kp[16:16 + 16 * 128] = ko[128:17 * 128]
            vp[16:16 + 16 * 128] = vo[128:17 * 128]
        else:
            y_prompt[b, 16 * 128:] = yo[128:17 * 128]
            kp[16 + 16 * 128:] = ko[128:17 * 128]
            vp[16 + 16 * 128:] = vo[128:17 * 128]
            ssm_prompt[0, b] = r["ssm_out"][0].reshape(32, 64, 128)
            sconv_prompt[0, b] = r["sconv_out"][0].reshape(XBC, 3).T
            fconv_prompt[0, b] = r["fconv_out"][0].reshape(2 * DFF, 2).T
        for s in range(NSTREAM):
            gi = c * NSTREAM + s
            r0 = NMAIN * 128 + s * 32
            y_sample[gi] = yo[r0:r0 + 16]
            k_sample[0, gi] = ko[r0:r0 + 16].reshape(16, 16, 64)
            v_sample[0, gi] = vo[r0:r0 + 16].reshape(16, 16, 64)
            ssm_sample[0, gi] = r["ssm_out"][1 + s].reshape(32, 64, 128)
            sconv_sample[0, gi] = r["sconv_out"][1 + s].reshape(XBC, 3).T
            fconv_sample[0, gi] = r["fconv_out"][1 + s].reshape(2 * DFF, 2).T
    return (y_prompt, y_sample, k_prompt, v_prompt, ssm_prompt, sconv_prompt, fconv_prompt,
            k_sample, v_sample, ssm_sample, sconv_sample, fconv_sample)
```

```python
import contextlib
import os
import numpy as np
import concourse.bass as bass
import concourse.mybir as mybir
from concourse.bass_utils import run_bass_kernel_spmd

F32 = mybir.dt.float32
BF16 = mybir.dt.bfloat16
AF = mybir.ActivationFunctionType
ALU = mybir.AluOpType

D = 1024
NPRE = 16
NMAIN = 17
NPT = NPRE + NMAIN
NT = NPT + 1
NM1 = NMAIN + 1
TOK = NT * 128
TOKM = NM1 * 128
SBW = 1024
SSDW = 2048
XBC = 3072
DFF = 2816
C_Q, C_K, C_V, C_Z, C_X, C_DT, C_G = 0, 1024, 2048, 3072, 5120, 8192, 8224
INC = 10272
EPS = 1e-6
NSTREAM = 4
NCACHE = 1040


_STOP = float(os.environ.get("MK_STOP", "99"))
_DEBUG = int(os.environ.get("MK_DEBUG", "0"))
_LAST = {}


class _Stop(Exception):
    pass


def checkpoint(n):
    if _STOP <= n:
        raise _Stop()


class Buf:
    __slots__ = ("name", "lw", "rd", "wr")

    def __init__(self, name):
        self.name = name
        self.lw = None
        self.rd = {}
        self.wr = {}


class KB:
    def __init__(self, nc, stack, n_dma_sp=48, n_dma_pool=32):
        self.nc = nc
        self.engs = {"pe": nc.tensor, "act": nc.scalar, "dve": nc.vector, "pool": nc.gpsimd, "sp": nc.sync}
        self.sems = {}
        self.cnt = {}
        for e in ("pe", "act", "dve", "pool"):
            self.sems[e] = stack.enter_context(nc.semaphore("s_" + e))
            self.cnt[e] = 0
        self.dq = {"sp": [], "pool": []}
        for q, n in (("sp", n_dma_sp), ("pool", n_dma_pool)):
            for i in range(n):
                sid = "d_%s_%d" % (q, i)
                self.sems[sid] = stack.enter_context(nc.semaphore(sid))
                self.cnt[sid] = 0
                self.dq[q].append(sid)
        self.dq_next = {"sp": 0, "pool": 0}
        self.seen = {e: {} for e in self.engs}
        self.n_ins = 0
        self.n_wait = 0
        self._nb = 0
        self._rr = 0
        self.pstack = stack

    def sb(self, name, shape, dt):
        self._nm = getattr(self, "_nm", 0) + 1
        return self.pstack.enter_context(self.nc.sbuf_tensor("%s_%d" % (name, self._nm), list(shape), dt))

    def barrier(self):
        for e in self.engs:
            for sid, c in self.cnt.items():
                self._wait(e, sid, c)

    def new_phase(self):
        self.barrier()
        self.pstack.close()
        self.pstack = contextlib.ExitStack()

    def buf(self, name=None):
        self._nb += 1
        return Buf(name or ("b%d" % self._nb))

    def _wait(self, eng, sid, val):
        if val <= 0:
            return
        s = self.seen[eng]
        if s.get(sid, 0) >= val:
            return
        self.engs[eng].wait_ge(self.sems[sid], val)
        s[sid] = val
        self.n_wait += 1

    def _wait_ev(self, eng, ev):
        sid, val = ev
        if sid == "pe" and eng == "pe":
            return
        self._wait(eng, sid, val)

    def _deps(self, eng, reads, writes, swrites=()):
        for b in reads:
            if b.lw is not None:
                self._wait_ev(eng, b.lw)
            for sid, val in b.wr.items():
                self._wait_ev(eng, (sid, val))
        for b in writes:
            if b.lw is not None:
                self._wait_ev(eng, b.lw)
            for sid, val in b.wr.items():
                self._wait_ev(eng, (sid, val))
            for sid, val in b.rd.items():
                self._wait_ev(eng, (sid, val))
        for b in swrites:
            if b.lw is not None:
                self._wait_ev(eng, b.lw)
            for sid, val in b.rd.items():
                self._wait_ev(eng, (sid, val))

    def _commit(self, ev, reads, writes, swrites=()):
        sid, val = ev
        for b in writes:
            b.lw = ev
            b.rd = {}
            b.wr = {}
        for b in swrites:
            b.rd = {}
            if b.wr.get(sid, 0) < val:
                b.wr[sid] = val
        for b in reads:
            if b.rd.get(sid, 0) < val:
                b.rd[sid] = val

    def op(self, eng, fn, reads=(), writes=()):
        self._deps(eng, reads, writes)
        ins = fn(self.engs[eng])
        self.cnt[eng] += 1
        ins.then_inc(self.sems[eng], 1)
        ev = (eng, self.cnt[eng])
        self._commit(ev, reads, writes)
        self.n_ins += 1
        return ev

    def cp(self, out, in_, reads=(), writes=(), eng=None):
        eng = eng or self.anyop()
        if eng == "act":
            return self.op("act", lambda e: e.copy(out=out, in_=in_), reads, writes)
        return self.op(eng, lambda e: e.tensor_copy(out=out, in_=in_), reads, writes)

    def anyop(self):
        self._rr ^= 1
        return "act" if self._rr else "dve"

    def dma(self, q, out, in_, reads=(), writes=(), swrites=(), **kw):
        self._deps(q, reads, writes, swrites)
        i = self.dq_next[q]
        self.dq_next[q] = (i + 1) % len(self.dq[q])
        sid = self.dq[q][i]
        self._wait(q, sid, self.cnt[sid])
        ins = self.engs[q].dma_start(out=out, in_=in_, **kw)
        self.cnt[sid] += 16
        ins.then_inc(self.sems[sid], 16)
        ev = (sid, self.cnt[sid])
        self._commit(ev, reads, writes, swrites)
        self.n_ins += 1
        return ev

    def finish(self):
        for q in ("sp", "pool"):
            for sid in self.dq[q]:
                self._wait("sp", sid, self.cnt[sid])
        for e in ("pe", "act", "dve", "pool"):
            self._wait("sp", e, self.cnt[e])


class Rot:
    def __init__(self, k, name, shape, dt, n):
        self.items = [(k.sb("%s%d" % (name, i), shape, dt), k.buf("%s%d" % (name, i))) for i in range(n)]
        self.i = 0

    def next(self):
        it = self.items[self.i]
        self.i = (self.i + 1) % len(self.items)
        return it


def build_program():
    nc = bass.Bass("TRN2", target_bir_lowering=False)

    def din(name, shape, dt=F32):
        return nc.dram_tensor(name, list(shape), dt, kind="ExternalInput").ap()

    def dout(name, shape, dt=F32):
        return nc.dram_tensor(name, list(shape), dt, kind="ExternalOutput").ap()

    def dscr(name, shape, dt):
        return nc.dram_tensor(name, list(shape), dt, kind="ExternalOutput" if _DEBUG else "Internal").ap()

    xin = din("xin", [TOK, D])
    tmask = din("tmask", [128, NT])
    w_in = din("w_in", [D, INC])
    w_br_att = din("w_br_att", [SBW, D])
    w_br_ssd = din("w_br_ssd", [SSDW, D])
    w_out = din("w_out", [D, D])
    w_up = din("w_up", [D, 2 * DFF])
    w_down = din("w_down", [DFF, D])
    g_mix = din("g_mix", [1, D])
    g_ffn = din("g_ffn", [1, D])
    g_final = din("g_final", [1, D])
    g_ssd = din("g_ssd", [1, SSDW])
    cw_ssd = din("cw_ssd", [128, 24, 4])
    cb_ssd = din("cb_ssd", [128, 24])
    cw_ffn = din("cw_ffn", [128, 44, 3])
    cb_ffn = din("cb_ffn", [128, 44])
    dt_bias = din("dt_bias", [1, 32])
    a_log = din("a_log", [1, 32])
    d_skip = din("d_skip", [1, 32])
    cache_k = din("cache_k", [NSTREAM, NCACHE, SBW])
    cache_v = din("cache_v", [NSTREAM, NCACHE, SBW])
    st_ssm = din("st_ssm", [NSTREAM, SSDW, 128])
    st_sconv = din("st_sconv", [NSTREAM * 3, XBC])
    st_fconv = din("st_fconv", [NSTREAM * 2, 2 * DFF])
    c_ident = din("c_ident", [128, 128])
    c_attmask = din("c_attmask", [128, 4, 512])
    c_trineg = din("c_trineg", [128, 128])
    c_le = din("c_le", [128, 128])
    c_gt = din("c_gt", [128, 128])
    c_leS = din("c_leS", [128, 128])
    c_gtS = din("c_gtS", [128, 128])
    c_blkS = din("c_blkS", [128, 4, 128])

    y_out = dout("y_out", [TOKM, D])
    k_out = dout("k_out", [TOKM, SBW])
    v_out = dout("v_out", [TOKM, SBW])
    ssm_out = dout("ssm_out", [5, SSDW, 128])
    sconv_out = dout("sconv_out", [5, 24, 128, 3])
    fconv_out = dout("fconv_out", [5, 44, 128, 2])

    qT_s = dscr("qT_s", [16, 64, TOKM], BF16)
    kT_s = dscr("kT_s", [16, 64, TOK], BF16)
    v_s = dscr("v_s", [TOK, SBW], BF16)
    z_s = dscr("z_s", [TOKM, SSDW], F32)
    xc_s = dscr("xc_s", [24, 128, TOK], BF16)
    g_s = dscr("g_s", [16, 128, TOKM], F32)
    attT_s = dscr("attT_s", [16, 64, TOKM], BF16)
    ssdT_s = dscr("ssdT_s", [16, 128, TOKM], BF16)
    h_s = dscr("h_s", [TOKM, D], F32)
    mT_s = dscr("mT_s", [8, 128, TOKM], BF16)
    act_s = dscr("act_s", [22, 128, TOKM], BF16)

    with contextlib.ExitStack() as st:
        k = KB(nc, st)
        op, dma = k.op, k.dma
        B_qT, B_kT, B_v, B_z, B_xc, B_g, B_h, B_mT, B_act = [k.buf(n) for n in
                                                          ("qT_s", "kT_s", "v_s", "z_s", "xc_s", "g_s", "h_s", "mT_s", "act_s")]

        psbig = nc.alloc_psum_tensor("psbig", [128, 8 * 512], F32)
        PS = [(psbig[:, i * 512:(i + 1) * 512], k.buf("ps%d" % i)) for i in range(8)]
        psi = [0]
        psm = [8]

        def ps_next():
            it = PS[psi[0]]
            psi[0] = (psi[0] + 1) % psm[0]
            return it

        psa = [0]

        def ps_acc_next():
            it = PS[6 + psa[0]]
            psa[0] ^= 1
            return it

        def const_tile(name, src, shape, dt=F32, q="sp"):
            t = k.sb(name, shape, dt)
            b = k.buf(name)
            dma("pool" if dt == BF16 else q, t[:], src, writes=[b])
            return t, b

        ident_f, b_identf = const_tile("ident_f", c_ident, [128, 128])
        ident_b, b_identb = const_tile("ident_b", c_ident, [128, 128], BF16)
        attmask, b_attmask = const_tile("attmask", c_attmask, [128, 4, 512], BF16)
        trineg, b_trineg = const_tile("trineg", c_trineg, [128, 128], BF16)
        onesneg = k.sb("onesneg", [128, 128], BF16)
        b_onesneg = k.buf("onesneg")
        op("pool", lambda e: e.memset(onesneg[:], -1.0), writes=[b_onesneg])
        le_f, b_lef = const_tile("le_f", c_le, [128, 128])
        le_b, b_leb = const_tile("le_b", c_le, [128, 128], BF16)
        gt_f, b_gtf = const_tile("gt_f", c_gt, [128, 128])
        leS_f, b_leSf = const_tile("leS_f", c_leS, [128, 128])
        leS_b, b_leSb = const_tile("leS_b", c_leS, [128, 128], BF16)
        gtS_f, b_gtSf = const_tile("gtS_f", c_gtS, [128, 128])
        blkS_f, b_blkSf = const_tile("blkS_f", c_blkS, [128, 4, 128])
        ones_f = k.sb("ones_f", [128, 128], F32)
        b_onesf = k.buf("ones_f")
        op("pool", lambda e: e.memset(ones_f[:], 1.0), writes=[b_onesf])
        cws, b_cws = const_tile("cws", cw_ssd, [128, 24, 4])
        cbs, b_cbs = const_tile("cbs", cb_ssd, [128, 24])
        cwf, b_cwf = const_tile("cwf", cw_ffn, [128, 44, 3])
        cbf, b_cbf = const_tile("cbf", cb_ffn, [128, 44])
        dtb_bc, b_dtb = const_tile("dtb_bc", dt_bias.partition_broadcast(128), [128, 32])
        a_bc, b_abc = const_tile("a_bc", a_log.partition_broadcast(128), [128, 32])
        dsk_bc, b_dsk = const_tile("dsk_bc", d_skip.partition_broadcast(128), [128, 32])
        tm, b_tm = const_tile("tm", tmask, [128, NT])
        op("act", lambda e: e.activation(out=a_bc[:], in_=a_bc[:], func=AF.Exp), reads=[b_abc], writes=[b_abc])
        op("dve", lambda e: e.tensor_scalar(out=a_bc[:], in0=a_bc[:], scalar1=-1.0, scalar2=None, op0=ALU.mult),
           reads=[b_abc], writes=[b_abc])

        small = Rot(k, "small", [128, 8], F32, 4)
        junk_f = k.sb("junk_f", [128, SSDW], F32)
        b_junk = k.buf("junk")
        dtm = k.sb("dtm", [128, NT, 32], F32)
        b_dtm = [k.buf("dtm%d" % i) for i in range(NT)]
        k.pstack = contextlib.ExitStack()

        def rmsnorm_to_bf16(x_t, b_x, gain, b_gain, out_bf, b_out, junk, b_junk, n=D):
            ss, b_ss = small.next()
            op("act", lambda e: e.activation(out=junk, in_=x_t, func=AF.Square, accum_out=ss[:, 0:1]),
               reads=[b_x], writes=[b_junk, b_ss])
            op("act", lambda e: e.activation(out=ss[:, 1:2], in_=ss[:, 0:1], func=AF.Sqrt, scale=1.0 / n, bias=EPS),
               reads=[b_ss], writes=[b_ss])
            op("dve", lambda e: e.reciprocal(out=ss[:, 2:3], in_=ss[:, 1:2]), reads=[b_ss], writes=[b_ss])
            op("dve", lambda e: e.scalar_tensor_tensor(out=out_bf, in0=x_t, scalar=ss[:, 2:3], in1=gain,
                                                       op0=ALU.mult, op1=ALU.mult),
               reads=[b_x, b_ss, b_gain], writes=[b_out])

        def transpose_to(src_bf, b_src, nchunk, dst_fn, b_dst):
            for c0 in range(0, nchunk, 8):
                n = min(8, nchunk - c0)
                pt, b_pt = ps_next()
                ptb = pt[:].bitcast(BF16)
                for j in range(n):
                    c = c0 + j
                    op("pe", lambda e, c=c, j=j: e.transpose(out=ptb[:, j * 128:(j + 1) * 128],
                                                             in_=src_bf[:, c * 128:(c + 1) * 128], identity=ident_b[:]),
                       reads=[b_src, b_identb], writes=[b_pt])
                eng = k.anyop()
                src = ptb[:, 0:n * 128].rearrange("p (c t) -> p c t", t=128)
                if eng == "act":
                    op("act", lambda e: e.copy(out=dst_fn(c0, n), in_=src), reads=[b_pt], writes=[b_dst])
                else:
                    op("dve", lambda e: e.tensor_copy(out=dst_fn(c0, n), in_=src), reads=[b_pt], writes=[b_dst])

        def body():
            gmix_bc, b_gmix = const_tile("gmix_bc", g_mix.partition_broadcast(128), [128, D])
            uT = k.sb("uT", [128, 8, TOK], BF16)
            b_uT = [k.buf("uT%d" % i) for i in range(NT)]
            xrot = Rot(k, "xrot", [128, D], F32, 2)
            ubrot = Rot(k, "ubrot", [128, D], BF16, 2)
            for i in range(NT):
                xt, b_xt = xrot.next()
                dma("sp", xt[:], xin[i * 128:(i + 1) * 128, :], writes=[b_xt])
                ub, b_ub = ubrot.next()
                rmsnorm_to_bf16(xt[:], b_xt, gmix_bc[:], b_gmix, ub[:], b_ub, junk_f[:, 0:D], b_junk)
                transpose_to(ub, b_ub, 8, lambda c0, n, i=i: uT[:, c0:c0 + n, i * 128:(i + 1) * 128], b_uT[i])

            checkpoint(0)
            wrot = Rot(k, "wrot", [128, 8, 512], BF16, 2)

            def load_w(src2d, c0, nc_):
                wt, b_wt = wrot.next()
                dma("pool", wt[:, :, 0:nc_], src2d[:, c0:c0 + nc_].rearrange("(kc p) c -> p kc c", p=128), writes=[b_wt])
                return wt, b_wt

            tiles_all = list(range(NT))
            tiles_m1 = list(range(NPRE, NT))

            def mloc(i):
                return i - NPRE

            ev32 = Rot(k, "ev32", [128, 512], F32, 3)
            ev16 = Rot(k, "ev16", [128, 512], BF16, 3)

            def tokmajor_block(wt, b_wt, ncol, tiles, consume):
                for i in tiles:
                    pt, b_pt = ps_next()
                    for kc in range(8):
                        op("pe", lambda e, kc=kc: e.matmul(pt[:, 0:ncol], lhsT=uT[:, kc, i * 128:(i + 1) * 128],
                                                           rhs=wt[:, kc, 0:ncol], start=(kc == 0), stop=(kc == 7)),
                           reads=[b_uT[i], b_wt], writes=[b_pt])
                    consume(i, pt, b_pt)

            def groups(tiles):
                gs = []
                cur = []
                for i in tiles:
                    if i == NT - 1 or (cur and (cur[-1] + 1 != i or len(cur) == 4)):
                        if cur:
                            gs.append(cur)
                        cur = []
                    cur.append(i)
                    if i == NT - 1:
                        gs.append(cur)
                        cur = []
                if cur:
                    gs.append(cur)
                return gs

            G_ALL = groups(tiles_all)
            G_M1 = groups(tiles_m1)

            class Deferred:
                def __init__(self, depth):
                    self.q, self.depth = [], depth

                def push(self, fn):
                    self.q.append(fn)
                    while len(self.q) > self.depth:
                        self.q.pop(0)()

                def flush(self):
                    while self.q:
                        self.q.pop(0)()

            def featmajor(wt, b_wt, wc0, m, grp, consume, tofs=0, defer=None):
                t0 = grp[0] * 128
                n = len(grp) * 128
                pt, b_pt = ps_next()
                for kc in range(8):
                    op("pe", lambda e, kc=kc: e.matmul(pt[0:m, 0:n], lhsT=wt[:, kc, wc0:wc0 + m],
                                                       rhs=uT[:, kc, t0 - tofs:t0 - tofs + n], start=(kc == 0), stop=(kc == 7)),
                       reads=[b_uT[i] for i in grp] + [b_wt], writes=[b_pt])
                if defer is not None:
                    defer.push(lambda: consume(grp, t0, n, pt, b_pt))
                else:
                    consume(grp, t0, n, pt, b_pt)

            for which in ("q", "k"):
                cbase = C_Q if which == "q" else C_K
                for blk in range(2):
                    wt, b_wt = load_w(w_in, cbase + blk * 512, 512)
                    for hp in range(4):
                        h = blk * 8 + hp * 2
                        for grp in (G_M1 if which == "q" else G_ALL):
                            def cons(grp, t0, n, pt, b_pt, h=h, which=which):
                                o, b_o = ev16.next()
                                if which == "q":
                                    op("act", lambda e: e.activation(out=o[:, 0:n], in_=pt[:, 0:n], func=AF.Copy,
                                                                     scale=0.125), reads=[b_pt], writes=[b_o])
                                    for two in range(2):
                                        dma("sp", qT_s[h + two, :, t0 - NPRE * 128:t0 - NPRE * 128 + n], o[two * 64:(two + 1) * 64, 0:n],
                                            reads=[b_o], swrites=[B_qT])
                                else:
                                    op("dve", lambda e: e.tensor_copy(out=o[:, 0:n], in_=pt[:, 0:n]),
                                       reads=[b_pt], writes=[b_o])
                                    for two in range(2):
                                        dma("sp", kT_s[h + two, :, t0:t0 + n], o[two * 64:(two + 1) * 64, 0:n], reads=[b_o], swrites=[B_kT])
                            featmajor(wt, b_wt, hp * 128, 128, grp, cons)

            checkpoint(0.1)
            for blk in range(2):
                wt, b_wt = load_w(w_in, C_K + blk * 512, 512)

                def cons_k(i, pt, b_pt, blk=blk):
                    o, b_o = ev32.next()
                    k.cp(o[:], pt[:], reads=[b_pt], writes=[b_o])
                    dma("sp", k_out[mloc(i) * 128:(mloc(i) + 1) * 128, blk * 512:(blk + 1) * 512], o[:], reads=[b_o])
                tokmajor_block(wt, b_wt, 512, tiles_m1, cons_k)
            for blk in range(2):
                wt, b_wt = load_w(w_in, C_V + blk * 512, 512)

                def cons_v(i, pt, b_pt, blk=blk):
                    o2, b_o2 = ev16.next()
                    if i >= NPRE:
                        o, b_o = ev32.next()
                        op("act", lambda e: e.copy(out=o[:], in_=pt[:]), reads=[b_pt], writes=[b_o])
                        dma("sp", v_out[mloc(i) * 128:(mloc(i) + 1) * 128, blk * 512:(blk + 1) * 512], o[:], reads=[b_o])
                        op("dve", lambda e: e.tensor_copy(out=o2[:], in_=o[:]), reads=[b_o], writes=[b_o2])
                    else:
                        op("dve", lambda e: e.tensor_copy(out=o2[:], in_=pt[:]), reads=[b_pt], writes=[b_o2])
                    dma("sp", v_s[i * 128:(i + 1) * 128, blk * 512:(blk + 1) * 512], o2[:], reads=[b_o2], swrites=[B_v])
                tokmajor_block(wt, b_wt, 512, tiles_all, cons_v)

            checkpoint(0.2)
            for blk in range(4):
                wt, b_wt = load_w(w_in, C_Z + blk * 512, 512)

                def cons_z(i, pt, b_pt, blk=blk):
                    o, b_o = ev32.next()
                    op("act", lambda e: e.activation(out=o[:], in_=pt[:], func=AF.Silu), reads=[b_pt], writes=[b_o])
                    dma("sp", z_s[mloc(i) * 128:(mloc(i) + 1) * 128, blk * 512:(blk + 1) * 512], o[:], reads=[b_o], swrites=[B_z])
                tokmajor_block(wt, b_wt, 512, tiles_m1, cons_z)

            checkpoint(0.3)
            wt, b_wt = load_w(w_in, C_DT, 32)

            def cons_dt(i, pt, b_pt):
                op("dve", lambda e: e.tensor_tensor(out=dtm[:, i, :], in0=pt[:, 0:32], in1=dtb_bc[:], op=ALU.add),
                   reads=[b_pt, b_dtb], writes=[b_dtm[i]])
                op("act", lambda e: e.activation(out=dtm[:, i, :], in_=dtm[:, i, :], func=AF.Exp),
                   reads=[b_dtm[i]], writes=[b_dtm[i]])
                op("act", lambda e: e.activation(out=dtm[:, i, :], in_=dtm[:, i, :], func=AF.Ln, bias=1.0),
                   reads=[b_dtm[i]], writes=[b_dtm[i]])
                op("dve", lambda e: e.tensor_scalar(out=dtm[:, i, :], in0=dtm[:, i, :], scalar1=tm[:, i:i + 1], scalar2=None,
                                                    op0=ALU.mult), reads=[b_dtm[i], b_tm], writes=[b_dtm[i]])
            tokmajor_block(wt, b_wt, 32, tiles_all, cons_dt)

            checkpoint(0.4)
            for blk in range(4):
                wt, b_wt = load_w(w_in, C_G + blk * 512, 512)
                for cc in range(4):
                    c = blk * 4 + cc
                    for grp in G_M1:
                        def cons_g(grp, t0, n, pt, b_pt, c=c):
                            o, b_o = ev32.next()
                            op("act", lambda e: e.activation(out=o[:, 0:n], in_=pt[:, 0:n], func=AF.Sigmoid),
                               reads=[b_pt], writes=[b_o])
                            dma("sp", g_s[c, :, t0 - NPRE * 128:t0 - NPRE * 128 + n], o[:, 0:n], reads=[b_o], swrites=[B_g])
                        featmajor(wt, b_wt, cc * 128, 128, grp, cons_g)

            checkpoint(0.5)
            sch = k.sb("sch", [128, 24, 12], F32)
            b_sch = k.buf("sch")
            s12r = Rot(k, "s12", [12, 512], F32, 2)
            for c0 in range(0, 24, 4):
                s12, b_s12 = s12r.next()
                dma("sp", s12[:], st_sconv[:, c0 * 128:(c0 + 4) * 128], writes=[b_s12])
                pt, b_pt = ps_next()
                for j in range(4):
                    op("pe", lambda e, j=j: e.transpose(out=pt[:, j * 12:(j + 1) * 12], in_=s12[:, j * 128:(j + 1) * 128],
                                                        identity=ident_f[0:12, 0:12]), reads=[b_s12, b_identf], writes=[b_pt])
                op("dve", lambda e: e.tensor_copy(out=sch[:, c0:c0 + 4, :], in_=pt[:, 0:48].rearrange("p (c t) -> p c t", t=12)),
                   reads=[b_pt], writes=[b_sch])
            checkpoint(0.6)
            xraw = Rot(k, "xraw", [128, 4, 3 + 128], F32, 3)
            xacc = Rot(k, "xacc", [128, 512], F32, 2)
            dq_x = Deferred(2)
            sconv_sb = k.sb("sconv_sb", [128, 24, 5, 3], F32)
            b_sconv = k.buf("sconv_sb")
            for blk in range(6):
                wt, b_wt = load_w(w_in, C_X + blk * 512, 512)
                for cc in range(4):
                    c = blk * 4 + cc
                    prev = [None]
                    for grp in G_ALL:
                        def cons_x(grp, t0, n, pt, b_pt, c=c, prev=prev):
                            xr, b_xr = xraw.next()
                            issample = grp[0] == NT - 1
                            if not issample:
                                flat = xr[:].rearrange("p s t -> p (s t)")
                                if prev[0] is None:
                                    op("pool", lambda e: e.memset(flat[:, 0:3], 0.0), writes=[b_xr])
                                else:
                                    pf, b_pf, pn = prev[0]
                                    op("pool", lambda e: e.tensor_copy(out=flat[:, 0:3], in_=pf[:, pn:pn + 3]),
                                       reads=[b_pf], writes=[b_xr])
                                op("act", lambda e: e.copy(out=flat[:, 3:3 + n], in_=pt[:, 0:n]), reads=[b_pt], writes=[b_xr])
                                prev[0] = (flat, b_xr, n)
                                if grp[-1] == NPT - 1:
                                    op("pool", lambda e: e.tensor_copy(out=sconv_sb[:, c, 0, :], in_=flat[:, n:n + 3]),
                                       reads=[b_xr], writes=[b_sconv])
                                acc, b_acc = xacc.next()
                                op("act", lambda e: e.activation(out=acc[:, 0:n], in_=pt[:, 0:n], func=AF.Identity,
                                                                 scale=cws[:, c, 3:4], bias=cbs[:, c:c + 1]),
                                   reads=[b_pt, b_cws, b_cbs], writes=[b_acc])
                                for i in range(0, 3):
                                    op("dve", lambda e, i=i: e.scalar_tensor_tensor(out=acc[:, 0:n], in0=flat[:, i:i + n],
                                                                                    scalar=cws[:, c, i:i + 1], in1=acc[:, 0:n],
                                                                                    op0=ALU.mult, op1=ALU.add),
                                       reads=[b_xr, b_cws, b_acc], writes=[b_acc])
                                o, b_o = ev16.next()
                                op("act", lambda e: e.activation(out=o[:, 0:n], in_=acc[:, 0:n], func=AF.Silu),
                                   reads=[b_acc], writes=[b_o])
                                dma("sp", xc_s[c, :, t0:t0 + n], o[:, 0:n], reads=[b_o], swrites=[B_xc])
                            else:
                                op("pool", lambda e: e.tensor_copy(out=xr[:, :, 0:3],
                                                                   in_=sch[:, c, :].rearrange("p (s t) -> p s t", t=3)),
                                   reads=[b_sch], writes=[b_xr])
                                op("act", lambda e: e.copy(out=xr[:, :, 3:35], in_=pt[:, 0:128].rearrange("p (s t) -> p s t", t=32)),
                                   reads=[b_pt], writes=[b_xr])
                                op("pool", lambda e: e.tensor_copy(out=sconv_sb[:, c, 1:5, :], in_=xr[:, :, 16:19]),
                                   reads=[b_xr], writes=[b_sconv])
                                acc, b_acc = xacc.next()
                                a3 = acc[:, 0:128].rearrange("p (s t) -> p s t", t=32)
                                op("act", lambda e: e.activation(out=acc[:, 0:128], in_=pt[:, 0:128], func=AF.Identity,
                                                                 scale=cws[:, c, 3:4], bias=cbs[:, c:c + 1]),
                                   reads=[b_pt, b_cws, b_cbs], writes=[b_acc])
                                for i in range(0, 3):
                                    op("dve", lambda e, i=i: e.scalar_tensor_tensor(out=a3, in0=xr[:, :, i:i + 32],
                                                                                    scalar=cws[:, c, i:i + 1], in1=a3,
                                                                                    op0=ALU.mult, op1=ALU.add),
                                       reads=[b_xr, b_cws, b_acc], writes=[b_acc])
                                o, b_o = ev16.next()
                                op("act", lambda e: e.activation(out=o[:, 0:128], in_=acc[:, 0:128], func=AF.Silu),
                                   reads=[b_acc], writes=[b_o])
                                dma("sp", xc_s[c, :, t0:t0 + 128], o[:, 0:128], reads=[b_o], swrites=[B_xc])
                        featmajor(wt, b_wt, cc * 128, 128, grp, cons_x, defer=dq_x)
            dq_x.flush()
            for q5 in range(5):
                dma("sp", sconv_out[q5].rearrange("c p t -> p c t"), sconv_sb[:, :, q5, :], reads=[b_sconv])

            checkpoint(1)
            k.new_phase()
            psm[0] = 6
            psi[0] %= 6
            b_attT = k.buf("attT_s")
            def alloc_att_bufs():
                return dict(e=Rot(k, "e_rot", [128, 1024], F32, 5), sp=Rot(k, "sp_rot", [128, 1024], BF16, 6),
                            x=Rot(k, "x_rot", [128, 1024], F32, 2), w=Rot(k, "w_rot", [128, 1024], BF16, 3),
                            acc=Rot(k, "acc_rot", [128, 1024], BF16, 4), ao=Rot(k, "ao_rot", [128, 512], BF16, 2),
                            spf=(k.sb("spf", [128, 512], BF16), k.buf("spf")))
            AB = alloc_att_bufs()
            ppair = [0]

            def ps_pair_next():
                b0 = 2 * ppair[0]
                ppair[0] = (ppair[0] + 1) % 3
                return b0

            def sb_pipeline(blocks, W, b_mask):
                n = len(blocks)
                partial_first = blocks[0]["jn"] < 128
                units = []
                i = 0
                if partial_first:
                    units.append([0])
                    i = 1
                while i < n:
                    if i + 1 < n:
                        units.append([i, i + 1])
                        i += 2
                    else:
                        units.append([i])
                        i += 1
                nu = len(units)
                U = [dict() for _ in units]
                SPV = {}
                ACC = {}
                spf, b_spf = AB["spf"]

                def psv(b0, ns, jn):
                    return psbig[0:jn, b0 * 512:(b0 + ns) * 512].rearrange("p (s c) -> p s c", c=512)[:, :, 0:W]

                def sbv(t, ns, jn):
                    return t[0:jn, 0:ns * W].rearrange("p (s c) -> p s c", c=W)

                def stage_a(u):
                    idx, c = units[u], U[u]
                    ns = len(idx)
                    jn = blocks[idx[0]]["jn"]
                    b0 = ps_pair_next()
                    bufs = [PS[b0 + s_][1] for s_ in range(ns)]
                    for s_, bi in enumerate(idx):
                        blocks[bi]["zmm"](PS[b0 + s_][0], PS[b0 + s_][1], True)
                    e_t, b_e = AB["e"].next()
                    op("act", lambda e: e.activation(out=sbv(e_t, ns, jn), in_=psv(b0, ns, jn), func=AF.Exp), reads=bufs, writes=[b_e])
                    if u == 0 and partial_first:
                        sp_t, b_sp = spf, b_spf
                    else:
                        sp_t, b_sp = AB["sp"].next()
                    op("act", lambda e: e.activation(out=sp_t[0:jn, 0:ns * W], in_=e_t[0:jn, 0:ns * W], func=AF.Ln, bias=1.0),
                       reads=[b_e], writes=[b_sp])
                    for s_, bi in enumerate(idx):
                        bl = blocks[bi]
                        if bl["mask"] is not None:
                            v_ = bl["mv"](sp_t[0:jn, s_ * W:(s_ + 1) * W])
                            op("pool", lambda e, v_=v_, bl=bl: e.tensor_tensor(out=v_, in0=v_, in1=bl["mask"], op=ALU.mult),
                               reads=[b_sp, b_mask], writes=[b_sp])
                        SPV[bi] = (sp_t[:, s_ * W:(s_ + 1) * W], b_sp)
                    c.update(e=(e_t, b_e), ns=ns, jn=jn)

                def stage_b1_acc(u):
                    idx = units[u]
                    first_full = 1 if partial_first else 0
                    a_t = None
                    for s_, bi in enumerate(idx):
                        nfull_before = bi - first_full
                        if nfull_before <= 0:
                            ACC[bi] = None
                        elif nfull_before == 1:
                            ACC[bi] = SPV[bi - 1]
                        else:
                            if a_t is None:
                                a_t, b_a = AB["acc"].next()
                            pa_ap, b_pa = ACC[bi - 1]
                            ps_ap, b_ps = SPV[bi - 1]
                            o_ap = a_t[:, s_ * W:(s_ + 1) * W]
                            op("dve", lambda e, o_ap=o_ap, pa_ap=pa_ap, ps_ap=ps_ap: e.tensor_tensor(out=o_ap, in0=pa_ap, in1=ps_ap, op=ALU.add),
                               reads=[b_pa, b_ps], writes=[b_a])
                            ACC[bi] = (o_ap, b_a)

                def stage_b1(u):
                    idx, c = units[u], U[u]
                    ns, jn = c["ns"], c["jn"]
                    b0 = ps_pair_next()
                    tl = []
                    for s_, bi in enumerate(idx):
                        sp_ap, b_sp = SPV[bi]
                        terms = [(trineg[0:jn, 0:jn], sp_ap[0:jn, :], [b_trineg, b_sp])]
                        if partial_first and bi > 0:
                            j0 = blocks[0]["jn"]
                            terms.append((onesneg[0:j0, 0:jn], spf[0:j0, 0:W], [b_onesneg, b_spf]))
                        if ACC[bi] is not None:
                            terms.append((onesneg[:, 0:jn], ACC[bi][0], [b_onesneg, ACC[bi][1]]))
                        tl.append(terms)
                    for ti in range(max(len(t_) for t_ in tl)):
                        for s_, terms in enumerate(tl):
                            if ti < len(terms):
                                l_, r_, rd = terms[ti]
                                ap_, b_ap = PS[b0 + s_]
                                op("pe", lambda e, l_=l_, r_=r_, ti=ti, ap_=ap_, nt=len(terms): e.matmul(ap_[0:jn, 0:W], lhsT=l_, rhs=r_, start=(ti == 0),
                                                                                                         stop=(ti == nt - 1)), reads=rd, writes=[b_ap])
                    c["apb"] = b0

                def stage_b2a(u):
                    idx, c = units[u], U[u]
                    ns, jn, b0 = c["ns"], c["jn"], c["apb"]
                    e_t, b_e = c["e"]
                    x_t, b_x = AB["x"].next()
                    op("act", lambda e: e.activation(out=sbv(x_t, ns, jn), in_=psv(b0, ns, jn), func=AF.Exp),
                       reads=[PS[b0 + s_][1] for s_ in range(ns)], writes=[b_x])
                    w_t, b_w = AB["w"].next()
                    op("dve", lambda e: e.tensor_tensor(out=w_t[0:jn, 0:ns * W], in0=x_t[0:jn, 0:ns * W], in1=e_t[0:jn, 0:ns * W], op=ALU.mult),
                       reads=[b_x, b_e], writes=[b_w])
                    for s_, bi in enumerate(idx):
                        bl = blocks[bi]
                        if bl["mask"] is not None:
                            v_ = bl["mv"](w_t[0:jn, s_ * W:(s_ + 1) * W])
                            op("pool", lambda e, v_=v_, bl=bl: e.tensor_tensor(out=v_, in0=v_, in1=bl["mask"], op=ALU.mult),
                               reads=[b_w, b_mask], writes=[b_w])
                    c["w"] = (w_t, b_w)

                def stage_b2b(u):
                    w_t, b_w = U[u]["w"]
                    for s_, bi in enumerate(units[u]):
                        blocks[bi]["pv"](w_t[:, s_ * W:(s_ + 1) * W], b_w, bi == 0, bi == n - 1)

                L1, L2, L3 = 2, 3, 4
                for s_ in range(nu + L3):
                    if 0 <= s_ - L1 < nu:
                        stage_b1_acc(s_ - L1)
                    if 0 <= s_ - L2 < nu:
                        stage_b2a(s_ - L2)
                    if s_ < nu:
                        stage_a(s_)
                    if 0 <= s_ - L1 < nu:
                        stage_b1(s_ - L1)
                    if 0 <= s_ - L3 < nu:
                        stage_b2b(s_ - L3)

            ident_mv = lambda a: a
            kTh_rot = Rot(k, "kTh", [128, NPT * 128], BF16, 2)
            vh_rot = Rot(k, "vh", [128, NPT, 128], BF16, 2)
            qTh_rot = Rot(k, "qTh", [128, NMAIN * 128], BF16, 2)
            for (t_, b_) in kTh_rot.items + qTh_rot.items:
                op("pool", lambda e, t_=t_: e.memset(t_[64:128, :], 0.0), writes=[b_])
            pfh = {}
            vcur = [None]

            def load_head(h):
                kTh, b_kTh = kTh_rot.next()
                dma("sp", kTh[0:64, :], kT_s[h, :, 0:NPT * 128], reads=[B_kT], swrites=[b_kTh])
                if h % 2 == 0:
                    vh, b_vh = vh_rot.next()
                    dma("sp", vh[:], v_s[0:NPT * 128, h * 64:(h + 2) * 64].rearrange("(kt p) d -> p kt d", p=128), reads=[B_v], writes=[b_vh])
                    vcur[0] = (vh, b_vh)
                qTh, b_qTh = qTh_rot.next()
                dma("sp", qTh[0:64, :], qT_s[h, :, 0:NMAIN * 128], reads=[B_qT], swrites=[b_qTh])
                pfh[h] = (kTh, b_kTh, qTh, b_qTh) + vcur[0]

            load_head(0)
            for h in range(16):
                hh2 = h % 2
                if h + 1 < 16:
                    load_head(h + 1)
                kTh, b_kTh, qTh, b_qTh, vh, b_vh = pfh.pop(h)
                for (g0, nq) in ((0, 4), (4, 4), (8, 3), (11, 3), (14, 3)):
                    W = nq * 128
                    q0 = g0 * 128
                    op_, b_op = ps_acc_next()
                    blocks = []
                    for kt in range(NPRE + g0 + nq - 1, -1, -1):
                        r = kt - (NPRE + g0)

                        def zmm(ps, b_ps, final, kt=kt, kTh=kTh, qTh=qTh, b_kTh=b_kTh, b_qTh=b_qTh, W=W, q0=q0):
                            op("pe", lambda e: e.matmul(ps[:, 0:W], lhsT=kTh[:, kt * 128:(kt + 1) * 128], rhs=qTh[:, q0:q0 + W],
                                                        start=True, stop=final), reads=[b_kTh, b_qTh], writes=[b_ps])

                        def pv(w_t, b_w, first, last, kt=kt, vh=vh, b_vh=b_vh, op_=op_, b_op=b_op, W=W):
                            op("pe", lambda e: e.matmul(op_[:, 0:W], lhsT=vh[:, kt, :], rhs=w_t, start=first, stop=last),
                               reads=[b_vh, b_w], writes=[b_op])
                        blocks.append(dict(jn=128, zmm=zmm, mask=(attmask[:, r, 0:W] if r >= 0 else None), mv=ident_mv, pv=pv))
                    sb_pipeline(blocks, W, b_attmask)
                    ao, b_ao = AB["ao"].next()
                    rs = slice(hh2 * 64, (hh2 + 1) * 64)
                    op("dve", lambda e, rs=rs: e.tensor_copy(out=ao[rs, 0:W], in_=op_[rs, 0:W]), reads=[b_op], writes=[b_ao])
                    dma("sp", attT_s[h, :, q0:q0 + W], ao[rs, 0:W], reads=[b_ao], swrites=[b_attT])

            checkpoint(2)
            k.new_phase()
            AB = alloc_att_bufs()
            kc_bf = k.sb("kc_bf", [128, 9, SBW], BF16)
            b_kc = k.buf("kc_bf")
            vc_bf = k.sb("vc_bf", [128, 9, SBW], BF16)
            b_vc = k.buf("vc_bf")
            kTc = k.sb("kTc", [128, 16, 9 * 128], BF16)
            b_kTc = k.buf("kTc")
            kTn = k.sb("kTn", [128, 16, 32], BF16)
            b_kTn = k.buf("kTn")
            vn = k.sb("vn", [32, SBW], BF16)
            b_vn = k.buf("vn")
            qs = k.sb("qs", [128, 16, 32], BF16)
            b_qs = k.buf("qs")
            for (t_, b_) in ((kTc, b_kTc), (kTn, b_kTn), (qs, b_qs)):
                op("pool", lambda e, t_=t_: e.memset(t_[64:128, :, :], 0.0), writes=[b_])
            smask = attmask[0:32, 0, 0:32].unsqueeze(1).to_broadcast([32, 16, 32])
            smv = lambda a: a.rearrange("p (h t) -> p h t", t=32)
            S0 = NMAIN * 128
            for b in range(NSTREAM):
                op("pool", lambda e: e.memset(kc_bf[0:96, 0, :], 0.0), writes=[b_kc])
                op("pool", lambda e: e.memset(vc_bf[0:96, 0, :], 0.0), writes=[b_vc])
                op("pool", lambda e: e.memset(kc_bf[96:128, 0, :], 0.0), writes=[b_kc])
                op("pool", lambda e: e.memset(vc_bf[96:128, 0, :], 0.0), writes=[b_vc])
                dma("pool", kc_bf[112:128, 0, :], cache_k[b, 0:16, :], writes=[b_kc])
                dma("pool", vc_bf[112:128, 0, :], cache_v[b, 0:16, :], writes=[b_vc])
                for half in range(2):
                    dma("pool", kc_bf[:, 1 + half * 4:5 + half * 4, :],
                        cache_k[b, 16 + half * 512:16 + (half + 1) * 512, :].rearrange("(kt p) d -> p kt d", p=128), swrites=[b_kc])
                    dma("pool", vc_bf[:, 1 + half * 4:5 + half * 4, :],
                        cache_v[b, 16 + half * 512:16 + (half + 1) * 512, :].rearrange("(kt p) d -> p kt d", p=128), swrites=[b_vc])
                for kt in range(9):
                    for h0 in (0, 8):
                        pt, b_pt = ps_next()
                        ptb = pt[:].bitcast(BF16)
                        for j in range(8):
                            op("pe", lambda e, j=j: e.transpose(out=ptb[0:64, j * 128:(j + 1) * 128],
                                                                in_=kc_bf[:, kt, (h0 + j) * 64:(h0 + j + 1) * 64], identity=ident_b[:]),
                               reads=[b_kc, b_identb], writes=[b_pt])
                        k.cp(kTc[0:64, h0:h0 + 8, kt * 128:(kt + 1) * 128], ptb[0:64, :].rearrange("p (h t) -> p h t", t=128),
                             reads=[b_pt], writes=[b_kTc])
                c0 = S0 + b * 32
                dma("sp", kTn[0:64, :, :], kT_s[:, :, NPT * 128 + b * 32:NPT * 128 + b * 32 + 32].rearrange("h d t -> d h t"), reads=[B_kT], writes=[b_kTn])
                dma("sp", qs[0:64, :, :], qT_s[:, :, c0:c0 + 32].rearrange("h d t -> d h t"), reads=[B_qT], writes=[b_qs])
                dma("sp", vn[:], v_s[NPT * 128 + b * 32:NPT * 128 + b * 32 + 32, :], reads=[B_v], writes=[b_vn])
                op_, b_op = ps_acc_next()
                blocks = []
                pvfirst = [True]
                for kt in ["new"] + list(range(8, -1, -1)):
                    jn = 32 if kt == "new" else 128

                    def zmm(ps, b_ps, final, kt=kt, jn=jn):
                        for h in range(16):
                            lhsT = kTn[:, h, :] if kt == "new" else kTc[:, h, kt * 128:(kt + 1) * 128]
                            op("pe", lambda e, h=h, lhsT=lhsT: e.matmul(ps[0:jn, h * 32:(h + 1) * 32], lhsT=lhsT, rhs=qs[:, h, :],
                                                                        start=(h == 0), stop=(final and h == 15), skip_group_check=True),
                               reads=[b_kTn if kt == "new" else b_kTc, b_qs], writes=[b_ps])

                    def pv(w_t, b_w, first, last, kt=kt, jn=jn, op_=op_, b_op=b_op):
                        for h in range(16):
                            pr = (h // 2) * 128
                            lhsT = vn[:, pr:pr + 128] if kt == "new" else vc_bf[:, kt, pr:pr + 128]
                            st_ = pvfirst[0]
                            pvfirst[0] = False
                            op("pe", lambda e, h=h, lhsT=lhsT, st_=st_: e.matmul(op_[:, h * 32:(h + 1) * 32], lhsT=lhsT,
                                                                                 rhs=w_t[0:jn, h * 32:(h + 1) * 32], start=st_,
                                                                                 stop=(last and h == 15), skip_group_check=True),
                               reads=[b_vn if kt == "new" else b_vc, b_w], writes=[b_op])
                    blocks.append(dict(jn=jn, zmm=zmm, mask=(smask if kt == "new" else None), mv=smv, pv=pv))
                sb_pipeline(blocks, 512, b_attmask)
                ao, b_ao = AB["ao"].next()
                op("dve", lambda e: e.tensor_copy(out=ao[:, :], in_=op_[:, :]), reads=[b_op], writes=[b_ao])
                for two in range(2):
                    dma("sp", attT_s[two::2, :, c0:c0 + 32].rearrange("h d t -> d h t"),
                        ao[two * 64:(two + 1) * 64, :].rearrange("d (hp x t) -> d hp x t", x=2, t=32)[:, :, two, :],
                        reads=[b_ao], swrites=[b_attT])

            checkpoint(3)
            k.new_phase()
            b_ssdT = k.buf("ssdT_s")
            gssd_bc, b_gssd = const_tile("gssd_bc", g_ssd.partition_broadcast(128), [128, SSDW])
            HT = k.sb("HT", [128, SSDW], F32)
            b_HT = k.buf("HT")
            HTb = k.sb("HTb", [128, SSDW], BF16)
            b_HTb = k.buf("HTb")
            HS = [(k.sb("HS%d" % b, [128, SSDW], F32), k.buf("HS%d" % b), k.sb("HSb%d" % b, [128, SSDW], BF16), k.buf("HSb%d" % b))
                  for b in range(NSTREAM)]
            op("pool", lambda e: e.memset(HT[:], 0.0), writes=[b_HT])
            op("pool", lambda e: e.memset(HTb[:], 0.0), writes=[b_HTb])
            hld = k.sb("hld", [128, 16, 128], F32)
            b_hld = k.buf("hld")
            xc_rot = Rot(k, "xc_rot", [128, 24, 128], BF16, 2)
            da_rot = Rot(k, "da_rot", [128, 32], F32, 2)
            LM = k.sb("LM", [128, 32, 128], BF16)
            b_LM = k.buf("LM")
            dec_rot = Rot(k, "dec_rot", [128, 6, 32], F32, 2)
            dmk_rot = Rot(k, "dmk_rot", [128, 4, 2, 32], F32, 1)
            xdt = k.sb("xdt", [128, SSDW], BF16)
            b_xdt = k.buf("xdt")
            xtok = k.sb("xtok", [128, SSDW], BF16)
            b_xtok = k.buf("xtok")
            xw = k.sb("xw", [128, SSDW], BF16)
            b_xw = k.buf("xw")
            Btok = k.sb("Btok", [128, 512], BF16)
            b_Btok = k.buf("Btok")
            CBm = k.sb("CBm", [128, 4, 128], F32)
            b_CBm = k.buf("CBm")
            expL_rot = Rot(k, "expL", [128, 512], F32, 2)
            MT_rot = Rot(k, "MT", [128, 512], BF16, 2)
            ysb = k.sb("ysb", [128, SSDW], F32)
            b_ysb = k.buf("ysb")
            zt_rot = Rot(k, "zt_rot", [128, SSDW], F32, 1)
            ynb = k.sb("ynb", [128, SSDW], BF16)
            b_ynb = k.buf("ynb")
            sT_rot = Rot(k, "sT_rot", [128, 16, 128], BF16, 1)
            hout, b_hout = hld, b_hld

            def load_state(b):
                H, b_H, Hb, b_Hb = HS[b]
                dma("sp", hld[:], st_ssm[b].rearrange("(c p) n -> p c n", p=128), writes=[b_hld])
                for c0 in range(0, 16, 4):
                    pt, b_pt = ps_next()
                    for j in range(4):
                        op("pe", lambda e, j=j: e.transpose(out=pt[:, j * 128:(j + 1) * 128], in_=hld[:, c0 + j, :], identity=ident_f[:]),
                           reads=[b_hld, b_identf], writes=[b_pt])
                    op("dve", lambda e: e.tensor_copy(out=H[:, c0 * 128:(c0 + 4) * 128], in_=pt[:]), reads=[b_pt], writes=[b_H])
                    op("act", lambda e: e.copy(out=Hb[:, c0 * 128:(c0 + 4) * 128], in_=H[:, c0 * 128:(c0 + 4) * 128]), reads=[b_H], writes=[b_Hb])

            def store_state(H, b_H, q5):
                for c0 in range(0, 16, 4):
                    pt, b_pt = ps_next()
                    for j in range(4):
                        op("pe", lambda e, j=j: e.transpose(out=pt[:, j * 128:(j + 1) * 128], in_=H[:, (c0 + j) * 128:(c0 + j + 1) * 128],
                                                            identity=ident_f[:]), reads=[b_H, b_identf], writes=[b_pt])
                    op("dve", lambda e: e.tensor_copy(out=hout[:, c0:c0 + 4, :], in_=pt[:].rearrange("p (c n) -> p c n", n=128)),
                       reads=[b_pt], writes=[b_hout])
                dma("sp", ssm_out[q5].rearrange("(c p) n -> p c n", p=128), hout[:], reads=[b_hout])

            xcs = {}

            def load_xc(i):
                xc, b_xc = xc_rot.next()
                dma("sp", xc[:], xc_s[:, :, i * 128:(i + 1) * 128].rearrange("c p t -> p c t"), reads=[B_xc], writes=[b_xc])
                xcs[i] = (xc, b_xc)

            for i in range(NT):
                issample = i == NT - 1
                need_y = i >= NPRE
                if issample:
                    store_state(HT, b_HT, 0)
                    for b in range(NSTREAM):
                        load_state(b)
                    blocks = [(b * 32, 32, HS[b]) for b in range(NSTREAM)]
                    m_le_f, bm_lef, m_le_b, bm_leb, m_gt_f, bm_gtf = leS_f, b_leSf, leS_b, b_leSb, gtS_f, b_gtSf
                else:
                    blocks = [(0, 128, (HT, b_HT, HTb, b_HTb))]
                    m_le_f, bm_lef, m_le_b, bm_leb, m_gt_f, bm_gtf = le_f, b_lef, le_b, b_leb, gt_f, b_gtf
                nb = len(blocks)
                if i == 0:
                    load_xc(0)
                if i + 1 < NT:
                    load_xc(i + 1)
                xc, b_xc = xcs.pop(i)
                if need_y:
                    zt, b_zt = zt_rot.next()
                    dma("sp", zt[:], z_s[mloc(i) * 128:(mloc(i) + 1) * 128, :], reads=[B_z], writes=[b_zt])
                xsT = lambda c: xc[:, c, :]
                BT = lambda g: xc[:, 16 + g, :]
                CT = lambda g: xc[:, 20 + g, :]
                da, b_da = da_rot.next()
                op("dve", lambda e: e.tensor_tensor(out=da[:], in0=dtm[:, i, :], in1=a_bc[:], op=ALU.mult),
                   reads=[b_dtm[i], b_abc], writes=[b_da])
                if need_y:
                    op("pool", lambda e: e.tensor_tensor(out=LM[:], in0=m_gt_f[:].unsqueeze(1).to_broadcast([128, 32, 128]),
                                                         in1=da[:].unsqueeze(2).to_broadcast([128, 32, 128]), op=ALU.mult),
                       reads=[bm_gtf, b_da], writes=[b_LM])
                pd, b_pd = ps_next()
                op("pe", lambda e: e.matmul(pd[:, 0:32], lhsT=m_le_f[:], rhs=da[:], start=True, stop=True),
                   reads=[bm_lef, b_da], writes=[b_pd])
                op("pe", lambda e: e.matmul(pd[:, 32:64], lhsT=m_gt_f[:], rhs=da[:], start=True, stop=True),
                   reads=[bm_gtf, b_da], writes=[b_pd])
                for bi in range(nb):
                    lhsT = blkS_f[:, bi, :] if issample else ones_f[:]
                    op("pe", lambda e, bi=bi, lhsT=lhsT: e.matmul(pd[:, 64 + bi * 32:96 + bi * 32], lhsT=lhsT, rhs=da[:], start=True, stop=True),
                       reads=[b_blkSf, b_onesf, b_da], writes=[b_pd])
                dec, b_dec = dec_rot.next()
                op("act", lambda e: e.activation(out=dec[:, 0:2 + nb, :], in_=pd[:, 0:64 + nb * 32].rearrange("p (a h) -> p a h", h=32),
                                                 func=AF.Exp), reads=[b_pd], writes=[b_dec])
                decay_t = dec[:, 0, :]
                toend = dec[:, 1, :]
                for half in range(2):
                    pt, b_pt = ps_next()
                    ptb = pt[:].bitcast(BF16)
                    for j in range(8):
                        c = half * 8 + j
                        op("pe", lambda e, j=j, c=c: e.transpose(out=ptb[:, j * 128:(j + 1) * 128], in_=xsT(c), identity=ident_b[:]),
                           reads=[b_xc, b_identb], writes=[b_pt])
                    sl = slice(half * 1024, (half + 1) * 1024)
                    op("act", lambda e: e.copy(out=xtok[:, sl], in_=ptb[:, :]), reads=[b_pt], writes=[b_xtok])
                    op("dve", lambda e: e.tensor_tensor(out=xdt[:, sl].rearrange("p (h d) -> p h d", d=64),
                                                        in0=xtok[:, sl].rearrange("p (h d) -> p h d", d=64),
                                                        in1=dtm[:, i, half * 16:(half + 1) * 16].unsqueeze(2).to_broadcast([128, 16, 64]),
                                                        op=ALU.mult), reads=[b_xtok, b_dtm[i]], writes=[b_xdt])
                pt, b_pt = ps_next()
                ptb = pt[:].bitcast(BF16)
                for g in range(4):
                    op("pe", lambda e, g=g: e.transpose(out=ptb[:, g * 128:(g + 1) * 128], in_=BT(g), identity=ident_b[:]),
                       reads=[b_xc, b_identb], writes=[b_pt])
                op("act", lambda e: e.copy(out=Btok[:], in_=ptb[:, 0:512]), reads=[b_pt], writes=[b_Btok])
                dmk, b_dmk = dmk_rot.next()
                if issample:
                    for bi in range(nb):
                        op("pool", lambda e, bi=bi: e.tensor_tensor(out=dmk[:, bi, :, :], in0=dec[:, 0:2, :],
                                                                    in1=blkS_f[:, bi, 0:32].unsqueeze(1).to_broadcast([128, 2, 32]),
                                                                    op=ALU.mult), reads=[b_dec, b_blkSf], writes=[b_dmk])
                if need_y:
                    pc, b_pc = ps_next()
                    for g in range(4):
                        op("pe", lambda e, g=g: e.matmul(pc[:, g * 128:(g + 1) * 128], lhsT=BT(g), rhs=CT(g), start=True, stop=True),
                           reads=[b_xc], writes=[b_pc])
                    op("dve", lambda e: e.tensor_tensor(out=CBm[:], in0=pc[:].rearrange("p (g t) -> p g t", t=128),
                                                        in1=m_le_f[:].unsqueeze(1).to_broadcast([128, 4, 128]), op=ALU.mult),
                       reads=[b_pc, bm_lef], writes=[b_CBm])
                    for g in range(4):
                        for bi, (p0, pn, (H, b_H, Hb, b_Hb)) in enumerate(blocks):
                            po, b_po = ps_next()
                            op("pe", lambda e, Hb=Hb: e.matmul(po[:], lhsT=CT(g), rhs=Hb[:, g * 512:(g + 1) * 512], start=True, stop=True),
                               reads=[b_xc, b_Hb], writes=[b_po])
                            dcy = dmk[:, bi, 0, :] if issample else decay_t
                            ysl = ysb[:, g * 512:(g + 1) * 512].rearrange("p (h d) -> p h d", d=64)
                            if bi == 0:
                                op("dve", lambda e, g=g, dcy=dcy, ysl=ysl: e.tensor_tensor(
                                    out=ysl, in0=po[:].rearrange("p (h d) -> p h d", d=64),
                                    in1=dcy[:, g * 8:(g + 1) * 8].unsqueeze(2).to_broadcast([128, 8, 64]), op=ALU.mult),
                                   reads=[b_po, b_dec, b_dmk], writes=[b_ysb])
                            else:
                                jv = junk_f[:, 0:512].rearrange("p (h d) -> p h d", d=64)
                                op("dve", lambda e, g=g, dcy=dcy, jv=jv: e.tensor_tensor(
                                    out=jv, in0=po[:].rearrange("p (h d) -> p h d", d=64),
                                    in1=dcy[:, g * 8:(g + 1) * 8].unsqueeze(2).to_broadcast([128, 8, 64]), op=ALU.mult),
                                   reads=[b_po, b_dec, b_dmk], writes=[b_junk])
                                op("dve", lambda e, g=g: e.tensor_tensor(out=ysb[:, g * 512:(g + 1) * 512], in0=ysb[:, g * 512:(g + 1) * 512],
                                                                         in1=junk_f[:, 0:512], op=ALU.add),
                                   reads=[b_junk, b_ysb], writes=[b_ysb])
                for bi, (p0, pn, (H, b_H, Hb, b_Hb)) in enumerate(blocks):
                    dH = dec[:, 2 + bi, :]
                    te = dmk[:, bi, 1, :] if issample else toend
                    op("pool" if need_y else "dve", lambda e, te=te: e.tensor_tensor(out=xw[:].rearrange("p (h d) -> p h d", d=64),
                                                                in0=xdt[:].rearrange("p (h d) -> p h d", d=64),
                                                                in1=te.unsqueeze(2).to_broadcast([128, 32, 64]), op=ALU.mult),
                       reads=[b_xdt, b_dec, b_dmk], writes=[b_xw])
                    for g in range(4):
                        pu, b_pu = ps_next()
                        op("pe", lambda e, g=g: e.matmul(pu[:], lhsT=Btok[:, g * 128:(g + 1) * 128],
                                                         rhs=xw[:, g * 512:(g + 1) * 512], start=True, stop=True),
                           reads=[b_Btok, b_xw], writes=[b_pu])
                        op("pool" if need_y else "dve", lambda e, g=g: e.tensor_tensor(out=H[:, g * 512:(g + 1) * 512].rearrange("p (h d) -> p h d", d=64),
                                                                  in0=H[:, g * 512:(g + 1) * 512].rearrange("p (h d) -> p h d", d=64),
                                                                  in1=dH[:, g * 8:(g + 1) * 8].unsqueeze(2).to_broadcast([128, 8, 64]),
                                                                  op=ALU.mult), reads=[b_H, b_dec], writes=[b_H])
                        op("dve", lambda e, g=g: e.tensor_tensor(out=H[:, g * 512:(g + 1) * 512], in0=pu[:], in1=H[:, g * 512:(g + 1) * 512],
                                                                 op=ALU.add), reads=[b_pu, b_H], writes=[b_H])
                    if not issample:
                        op("act", lambda e: e.copy(out=Hb[:], in_=H[:]), reads=[b_H], writes=[b_Hb])
                    else:
                        store_state(H, b_H, 1 + bi)
                if need_y:
                    for g in range(4):
                        py, b_py = ps_acc_next()
                        for hb in range(2):
                            h0 = g * 8 + hb * 4
                            pL, b_pL = ps_next()
                            for j in range(4):
                                op("pe", lambda e, j=j: e.matmul(pL[:, j * 128:(j + 1) * 128], lhsT=LM[:, h0 + j, :], rhs=m_le_b[:],
                                                                 start=True, stop=True), reads=[b_LM, bm_leb], writes=[b_pL])
                            eL, b_eL = expL_rot.next()
                            op("act", lambda e: e.activation(out=eL[:], in_=pL[:], func=AF.Exp), reads=[b_pL], writes=[b_eL])
                            MT, b_MT = MT_rot.next()
                            op("dve", lambda e, g=g: e.tensor_tensor(out=MT[:].rearrange("p (h t) -> p h t", t=128),
                                                                     in0=eL[:].rearrange("p (h t) -> p h t", t=128),
                                                                     in1=CBm[:, g, :].unsqueeze(1).to_broadcast([128, 4, 128]),
                                                                     op=ALU.mult), reads=[b_eL, b_CBm], writes=[b_MT])
                            for j in range(4):
                                hh = hb * 4 + j
                                op("pe", lambda e, j=j, hh=hh: e.matmul(py[:, hh * 64:(hh + 1) * 64], lhsT=MT[:, j * 128:(j + 1) * 128],
                                                                        rhs=xdt[:, (h0 + j) * 64:(h0 + j + 1) * 64], start=True, stop=True),
                                   reads=[b_MT, b_xdt], writes=[b_py])
                        op("dve", lambda e, g=g: e.tensor_tensor(out=ysb[:, g * 512:(g + 1) * 512], in0=py[:], in1=ysb[:, g * 512:(g + 1) * 512],
                                                                 op=ALU.add), reads=[b_py, b_ysb], writes=[b_ysb])
                    op("pool", lambda e: e.tensor_tensor(out=junk_f[:].rearrange("p (h d) -> p h d", d=64),
                                                         in0=xtok[:].rearrange("p (h d) -> p h d", d=64),
                                                         in1=dsk_bc[:].unsqueeze(2).to_broadcast([128, 32, 64]), op=ALU.mult),
                       reads=[b_xtok, b_dsk], writes=[b_junk])
                    op("dve", lambda e: e.tensor_tensor(out=ysb[:], in0=ysb[:], in1=junk_f[:], op=ALU.add),
                       reads=[b_ysb, b_junk], writes=[b_ysb])
                    op("dve", lambda e: e.tensor_tensor(out=ysb[:], in0=ysb[:], in1=zt[:], op=ALU.mult),
                       reads=[b_ysb, b_zt], writes=[b_ysb])
                    ss, b_ss = small.next()
                    for g in range(4):
                        op("act", lambda e, g=g: e.activation(out=junk_f[:, g * 512:(g + 1) * 512], in_=ysb[:, g * 512:(g + 1) * 512],
                                                              func=AF.Square, accum_out=ss[:, g:g + 1]),
                           reads=[b_ysb], writes=[b_junk, b_ss])
                    op("act", lambda e: e.activation(out=ss[:, 0:4], in_=ss[:, 0:4], func=AF.Sqrt, scale=1.0 / 512, bias=EPS),
                       reads=[b_ss], writes=[b_ss])
                    op("dve", lambda e: e.reciprocal(out=ss[:, 4:8], in_=ss[:, 0:4]), reads=[b_ss], writes=[b_ss])
                    op("dve", lambda e: e.tensor_tensor(out=ysb[:].rearrange("p (g c) -> p g c", c=512),
                                                        in0=ysb[:].rearrange("p (g c) -> p g c", c=512),
                                                        in1=ss[:, 4:8].unsqueeze(2).to_broadcast([128, 4, 512]), op=ALU.mult),
                       reads=[b_ysb, b_ss], writes=[b_ysb])
                    op("dve", lambda e: e.tensor_tensor(out=ynb[:], in0=ysb[:], in1=gssd_bc[:], op=ALU.mult),
                       reads=[b_ysb, b_gssd], writes=[b_ynb])
                    sT, b_sT = sT_rot.next()
                    transpose_to(ynb, b_ynb, 16, lambda c0, n: sT[:, c0:c0 + n, :], b_sT)
                    dma("sp", ssdT_s[:, :, mloc(i) * 128:(mloc(i) + 1) * 128].rearrange("c p t -> p c t"), sT[:], reads=[b_sT],
                        swrites=[b_ssdT])

            checkpoint(4)
            k.new_phase()
            psm[0] = 8
            wba = k.sb("wba", [128, 8, D], BF16)
            b_wba = k.buf("wba")
            wbs = k.sb("wbs", [128, 16, D], BF16)
            b_wbs = k.buf("wbs")
            for hf in range(2):
                dma("pool", wba[:, hf * 4:(hf + 1) * 4, :], w_br_att[hf * 512:(hf + 1) * 512, :].rearrange("(h p) c -> p h c", p=128), swrites=[b_wba])
            for q4 in range(4):
                dma("pool", wbs[:, q4 * 4:(q4 + 1) * 4, :], w_br_ssd[q4 * 512:(q4 + 1) * 512, :].rearrange("(c p) n -> p c n", p=128), swrites=[b_wbs])
            aT_rot = Rot(k, "aT_rot", [128, 8, 512], BF16, 2)
            sTg_rot = Rot(k, "sTg_rot", [128, 16, 512], BF16, 2)
            gT_rot = Rot(k, "gT_rot", [128, 2, 512], F32, 2)
            mo_rot = Rot(k, "mo_rot", [128, 512], BF16, 2)
            t1_rot = Rot(k, "t1_rot", [128, 512], F32, 4)
            pfa = {}

            def load5a(gi):
                grp = G_M1[gi]
                t0 = (grp[0] - NPRE) * 128
                n = len(grp) * 128
                aT, b_aT = aT_rot.next()
                for two in range(2):
                    dma("sp", aT[two * 64:(two + 1) * 64, :, 0:n], attT_s[two::2, :, t0:t0 + n].rearrange("h d t -> d h t"),
                        reads=[b_attT], swrites=[b_aT])
                sTg, b_sTg = sTg_rot.next()
                dma("sp", sTg[:, :, 0:n], ssdT_s[:, :, t0:t0 + n].rearrange("c p t -> p c t"), reads=[b_ssdT], writes=[b_sTg])
                pfa[gi] = (aT, b_aT, sTg, b_sTg)

            load5a(0)
            for gi, grp in enumerate(G_M1):
                t0 = (grp[0] - NPRE) * 128
                n = len(grp) * 128
                if gi + 1 < len(G_M1):
                    load5a(gi + 1)
                aT, b_aT, sTg, b_sTg = pfa.pop(gi)
                for j in range(8):
                    gT, b_gT = gT_rot.next()
                    dma("sp", gT[:, 0, 0:n], g_s[j, :, t0:t0 + n], reads=[B_g], writes=[b_gT])
                    dma("sp", gT[:, 1, 0:n], g_s[8 + j, :, t0:t0 + n], reads=[B_g], writes=[b_gT])
                    pa, b_pa = ps_next()
                    for h in range(8):
                        op("pe", lambda e, h=h: e.matmul(pa[:, 0:n], lhsT=wba[:, h, j * 128:(j + 1) * 128], rhs=aT[:, h, 0:n],
                                                         start=(h == 0), stop=(h == 7)), reads=[b_wba, b_aT], writes=[b_pa])
                    pb, b_pb = ps_next()
                    for c in range(16):
                        op("pe", lambda e, c=c: e.matmul(pb[:, 0:n], lhsT=wbs[:, c, j * 128:(j + 1) * 128], rhs=sTg[:, c, 0:n],
                                                         start=(c == 0), stop=(c == 15)), reads=[b_wbs, b_sTg], writes=[b_pb])
                    t1, b_t1 = t1_rot.next()
                    op("dve", lambda e: e.tensor_tensor(out=t1[:, 0:n], in0=pa[:, 0:n], in1=gT[:, 0, 0:n], op=ALU.mult),
                       reads=[b_pa, b_gT], writes=[b_t1])
                    t2, b_t2 = t1_rot.next()
                    op("dve", lambda e: e.tensor_tensor(out=t2[:, 0:n], in0=pb[:, 0:n], in1=gT[:, 1, 0:n], op=ALU.mult),
                       reads=[b_pb, b_gT], writes=[b_t2])
                    mo, b_mo = mo_rot.next()
                    op("pool", lambda e: e.tensor_tensor(out=mo[:, 0:n], in0=t1[:, 0:n], in1=t2[:, 0:n], op=ALU.add),
                       reads=[b_t1, b_t2], writes=[b_mo])
                    dma("sp", mT_s[j, :, t0:t0 + n], mo[:, 0:n], reads=[b_mo], swrites=[B_mT])

            checkpoint(5)
            k.new_phase()
            gffn_bc, b_gffn = const_tile("gffn_bc", g_ffn.partition_broadcast(128), [128, D])
            wo = k.sb("wo", [128, 8, D], BF16)
            b_wo = k.buf("wo")
            for hf in range(2):
                dma("pool", wo[:, hf * 4:(hf + 1) * 4, :], w_out[hf * 512:(hf + 1) * 512, :].rearrange("(c p) n -> p c n", p=128), swrites=[b_wo])
            uT = k.sb("u2T", [128, 8, TOKM], BF16)
            b_uT = {i: k.buf("u2T%d" % i) for i in tiles_m1}
            mT_rot = Rot(k, "mT_rot", [128, 8, 128], BF16, 2)
            xrot = Rot(k, "xrot2", [128, D], F32, 2)
            ubrot = Rot(k, "ubrot2", [128, D], BF16, 2)
            hrot = Rot(k, "hrot", [128, D], F32, 2)
            pf5 = {}

            def load5(i):
                ml = mloc(i)
                mT, b_mT = mT_rot.next()
                dma("sp", mT[:], mT_s[:, :, ml * 128:(ml + 1) * 128].rearrange("c p t -> p c t"), reads=[B_mT], writes=[b_mT])
                xt, b_xt = xrot.next()
                dma("sp", xt[:], xin[i * 128:(i + 1) * 128, :], writes=[b_xt])
                pf5[i] = (mT, b_mT, xt, b_xt)

            load5(tiles_m1[0])
            for ii5, i in enumerate(tiles_m1):
                ml = mloc(i)
                if ii5 + 1 < len(tiles_m1):
                    load5(tiles_m1[ii5 + 1])
                mT, b_mT, xt, b_xt = pf5.pop(i)
                ht, b_ht = hrot.next()
                for hf in range(2):
                    ph, b_ph = ps_next()
                    for kc in range(8):
                        op("pe", lambda e, kc=kc: e.matmul(ph[:], lhsT=mT[:, kc, :], rhs=wo[:, kc, hf * 512:(hf + 1) * 512],
                                                           start=(kc == 0), stop=(kc == 7)), reads=[b_mT, b_wo], writes=[b_ph])
                    op("dve", lambda e, hf=hf: e.tensor_tensor(out=ht[:, hf * 512:(hf + 1) * 512], in0=ph[:], in1=xt[:, hf * 512:(hf + 1) * 512],
                                                               op=ALU.add), reads=[b_ph, b_xt], writes=[b_ht])
                dma("sp", h_s[ml * 128:(ml + 1) * 128, :], ht[:], reads=[b_ht], swrites=[B_h])
                ub, b_ub = ubrot.next()
                rmsnorm_to_bf16(ht[:], b_ht, gffn_bc[:], b_gffn, ub[:], b_ub, junk_f[:, 0:D], b_junk)
                transpose_to(ub, b_ub, 8, lambda c0, n_, ml=ml: uT[:, c0:c0 + n_, ml * 128:(ml + 1) * 128], b_uT[i])

            fch = k.sb("fch", [128, 44, 8], F32)
            b_fch = k.buf("fch")
            f8r = Rot(k, "f8", [8, 512], F32, 2)
            for c0 in range(0, 44, 4):
                f8, b_f8 = f8r.next()
                dma("sp", f8[:], st_fconv[:, c0 * 128:(c0 + 4) * 128], writes=[b_f8])
                pt, b_pt = ps_next()
                for j in range(4):
                    op("pe", lambda e, j=j: e.transpose(out=pt[:, j * 8:(j + 1) * 8], in_=f8[:, j * 128:(j + 1) * 128],
                                                        identity=ident_f[0:8, 0:8]), reads=[b_f8, b_identf], writes=[b_pt])
                op("dve", lambda e: e.tensor_copy(out=fch[:, c0:c0 + 4, :], in_=pt[:, 0:32].rearrange("p (c t) -> p c t", t=8)),
                   reads=[b_pt], writes=[b_fch])
            fconv_sb = k.sb("fconv_sb", [128, 44, 5, 2], F32)
            b_fconv = k.buf("fconv_sb")
            ev16 = Rot(k, "ev16b", [128, 512], BF16, 2)
            fraw = Rot(k, "fraw", [128, 4, 2 + 128], F32, 6)
            facc = Rot(k, "facc", [128, 512], F32, 4)
            dq_f = Deferred(6)
            wrot2 = Rot(k, "wrot2", [128, 8, 256], BF16, 2)
            for c in range(22):
                wt, b_wt = wrot2.next()
                dma("pool", wt[:, :, 0:128], w_up[:, c * 128:(c + 1) * 128].rearrange("(kc p) c -> p kc c", p=128), writes=[b_wt])
                dma("pool", wt[:, :, 128:256], w_up[:, DFF + c * 128:DFF + (c + 1) * 128].rearrange("(kc p) c -> p kc c", p=128), writes=[b_wt])
                prev = {0: None, 1: None}
                for grp in G_M1:
                    accs = []
                    for part in range(2):
                        ch = c + part * 22

                        def cons_f(grp, t0, n, pt, b_pt, ch=ch, part=part, prev=prev, accs=accs):
                            xr, b_xr = fraw.next()
                            acc, b_acc = facc.next()
                            if grp[0] != NT - 1:
                                flat = xr[:].rearrange("p s t -> p (s t)")
                                if prev[part] is None:
                                    op("pool", lambda e: e.memset(flat[:, 0:2], 0.0), writes=[b_xr])
                                else:
                                    pf, b_pf, pn = prev[part]
                                    op("pool", lambda e: e.tensor_copy(out=flat[:, 0:2], in_=pf[:, pn:pn + 2]), reads=[b_pf], writes=[b_xr])
                                op("act", lambda e: e.copy(out=flat[:, 2:2 + n], in_=pt[:, 0:n]), reads=[b_pt], writes=[b_xr])
                                prev[part] = (flat, b_xr, n)
                                if grp[-1] == NPT - 1:
                                    op("pool", lambda e: e.tensor_copy(out=fconv_sb[:, ch, 0, :], in_=flat[:, n:n + 2]),
                                       reads=[b_xr], writes=[b_fconv])
                                op("act", lambda e: e.activation(out=acc[:, 0:n], in_=pt[:, 0:n], func=AF.Identity,
                                                                 scale=cwf[:, ch, 2:3], bias=cbf[:, ch:ch + 1]),
                                   reads=[b_pt, b_cwf, b_cbf], writes=[b_acc])
                                for ii in range(0, 2):
                                    op("dve", lambda e, ii=ii: e.scalar_tensor_tensor(out=acc[:, 0:n], in0=flat[:, ii:ii + n],
                                                                                      scalar=cwf[:, ch, ii:ii + 1], in1=acc[:, 0:n],
                                                                                      op0=ALU.mult, op1=ALU.add),
                                       reads=[b_xr, b_cwf, b_acc], writes=[b_acc])
                            else:
                                op("pool", lambda e: e.tensor_copy(out=xr[:, :, 0:2], in_=fch[:, ch, :].rearrange("p (s t) -> p s t", t=2)),
                                   reads=[b_fch], writes=[b_xr])
                                op("act", lambda e: e.copy(out=xr[:, :, 2:34], in_=pt[:, 0:128].rearrange("p (s t) -> p s t", t=32)),
                                   reads=[b_pt], writes=[b_xr])
                                op("pool", lambda e: e.tensor_copy(out=fconv_sb[:, ch, 1:5, :], in_=xr[:, :, 16:18]),
                                   reads=[b_xr], writes=[b_fconv])
                                a3 = acc[:, 0:128].rearrange("p (s t) -> p s t", t=32)
                                op("act", lambda e: e.activation(out=acc[:, 0:128], in_=pt[:, 0:128], func=AF.Identity,
                                                                 scale=cwf[:, ch, 2:3], bias=cbf[:, ch:ch + 1]),
                                   reads=[b_pt, b_cwf, b_cbf], writes=[b_acc])
                                for ii in range(0, 2):
                                    op("dve", lambda e, ii=ii: e.scalar_tensor_tensor(out=a3, in0=xr[:, :, ii:ii + 32],
                                                                                      scalar=cwf[:, ch, ii:ii + 1], in1=a3,
                                                                                      op0=ALU.mult, op1=ALU.add),
                                       reads=[b_xr, b_cwf, b_acc], writes=[b_acc])
                            accs.append((acc, b_acc))
                        featmajor(wt, b_wt, part * 128, 128, grp, cons_f, tofs=NPRE * 128, defer=dq_f)

                    def fin_f(accs=accs, grp=grp, c=c):
                        (ga, b_ga), (va, b_va) = accs
                        n = len(grp) * 128
                        tl = (grp[0] - NPRE) * 128
                        op("act", lambda e: e.activation(out=ga[:, 0:n], in_=ga[:, 0:n], func=AF.Silu), reads=[b_ga], writes=[b_ga])
                        ao_, b_ao_ = ev16.next()
                        op("pool", lambda e: e.tensor_tensor(out=ao_[:, 0:n], in0=ga[:, 0:n], in1=va[:, 0:n], op=ALU.mult),
                           reads=[b_ga, b_va], writes=[b_ao_])
                        dma("sp", act_s[c, :, tl:tl + n], ao_[:, 0:n], reads=[b_ao_], swrites=[B_act])
                    dq_f.push(fin_f)
            dq_f.flush()
            for q5 in range(5):
                dma("sp", fconv_out[q5].rearrange("c p t -> p c t"), fconv_sb[:, :, q5, :], reads=[b_fconv])

            checkpoint(6)
            k.new_phase()
            gfin_bc, b_gfin = const_tile("gfin_bc", g_final.partition_broadcast(128), [128, D])
            hrot = Rot(k, "hrot7", [128, D], F32, 2)
            aTl_rot = Rot(k, "aTl", [128, 22, 128], BF16, 2)
            wd = k.sb("wd", [128, 22, D], BF16)
            b_wd = k.buf("wd")
            for c0 in range(0, 22, 4):
                n4 = min(4, 22 - c0)
                dma("pool", wd[:, c0:c0 + n4, :], w_down[c0 * 128:(c0 + n4) * 128, :].rearrange("(c p) n -> p c n", p=128), swrites=[b_wd])
            yrot = Rot(k, "yrot", [128, D], F32, 2)
            pf7 = {}

            def load7(i):
                ml = mloc(i)
                ht, b_ht = hrot.next()
                dma("sp", ht[:], h_s[ml * 128:(ml + 1) * 128, :], reads=[B_h], writes=[b_ht])
                aTl, b_aTl = aTl_rot.next()
                dma("sp", aTl[:], act_s[:, :, ml * 128:(ml + 1) * 128].rearrange("c p t -> p c t"), reads=[B_act], writes=[b_aTl])
                pf7[i] = (ht, b_ht, aTl, b_aTl)

            load7(tiles_m1[0])
            for ii7, i in enumerate(tiles_m1):
                ml = mloc(i)
                if ii7 + 1 < len(tiles_m1):
                    load7(tiles_m1[ii7 + 1])
                ht, b_ht, aTl, b_aTl = pf7.pop(i)
                for hf in range(2):
                    pdn, b_pdn = ps_next()
                    for c in range(22):
                        op("pe", lambda e, c=c: e.matmul(pdn[:], lhsT=aTl[:, c, :], rhs=wd[:, c, hf * 512:(hf + 1) * 512],
                                                         start=(c == 0), stop=(c == 21)), reads=[b_aTl, b_wd], writes=[b_pdn])
                    op("dve", lambda e, hf=hf: e.tensor_tensor(out=ht[:, hf * 512:(hf + 1) * 512], in0=pdn[:], in1=ht[:, hf * 512:(hf + 1) * 512],
                                                               op=ALU.add), reads=[b_pdn, b_ht], writes=[b_ht])
                yt, b_yt = yrot.next()
                rmsnorm_to_bf16(ht[:], b_ht, gfin_bc[:], b_gfin, yt[:], b_yt, junk_f[:, 0:D], b_junk)
                dma("sp", y_out[ml * 128:(ml + 1) * 128, :], yt[:], reads=[b_yt])

        try:
            body()
        except _Stop:
            pass
        k.finish()
        k.pstack.close()
        print("program: instructions=%d waits=%d" % (k.n_ins, k.n_wait))
    return nc


def _consts():
    j = np.arange(128)
    strict = (j[:, None] < j[None, :]).astype(np.float32)
    attmask = np.zeros((128, 4, 512), np.float32)
    for r in range(4):
        attmask[:, r, r * 128:(r + 1) * 128] = strict
        attmask[:, r, (r + 1) * 128:] = 1.0
    trineg = -(j[:, None] >= j[None, :]).astype(np.float32)
    le = (j[:, None] <= j[None, :]).astype(np.float32)
    gt = (j[:, None] > j[None, :]).astype(np.float32)
    blk = j // 32
    same = (blk[:, None] == blk[None, :]).astype(np.float32)
    blkS = np.zeros((128, 4, 128), np.float32)
    for b in range(4):
        blkS[blk == b, b, :] = 1.0
    return {"c_ident": np.eye(128, dtype=np.float32), "c_attmask": attmask, "c_trineg": trineg, "c_le": le, "c_gt": gt,
            "c_leS": le * same, "c_gtS": gt * same, "c_blkS": blkS}


_NC_CACHE = {}


def kernel(x_prompt, x_sample, cache_k, cache_v, state_ssm, state_ssm_conv, state_ffn_conv, meta_tokens,
           g_mix, w_in, conv_w_ssd, conv_b_ssd, dt_bias, a_log, d_skip, g_ssd_norm, w_br_att, w_br_ssd,
           w_out, g_ffn, w_up, conv_w_ffn, conv_b_ffn, w_down, g_final):
    f = lambda a: np.ascontiguousarray(np.asarray(a, dtype=np.float32))
    x_prompt, x_sample, cache_k, cache_v = f(x_prompt), f(x_sample), f(cache_k), f(cache_v)
    state_ssm, state_ssm_conv, state_ffn_conv, meta_tokens = f(state_ssm), f(state_ssm_conv), f(state_ffn_conv), f(meta_tokens)
    if "nc" not in _NC_CACHE:
        _NC_CACHE["nc"] = build_program()
    nc = _NC_CACHE["nc"]
    consts = _consts()
    shared = {
        "w_in": f(w_in[0]), "w_br_att": f(w_br_att[0]), "w_br_ssd": f(w_br_ssd[0]), "w_out": f(w_out[0]),
        "w_up": f(w_up[0]), "w_down": f(w_down[0]),
        "g_mix": f(g_mix[0]).reshape(1, D), "g_ffn": f(g_ffn[0]).reshape(1, D), "g_final": f(g_final).reshape(1, D),
        "g_ssd": f(g_ssd_norm[0]).reshape(1, SSDW),
        "cw_ssd": f(f(conv_w_ssd[0]).reshape(4, 24, 128).transpose(2, 1, 0)),
        "cb_ssd": f(f(conv_b_ssd[0]).reshape(24, 128).T),
        "cw_ffn": f(f(conv_w_ffn[0]).reshape(3, 44, 128).transpose(2, 1, 0)),
        "cb_ffn": f(f(conv_b_ffn[0]).reshape(44, 128).T),
        "dt_bias": f(dt_bias[0]).reshape(1, 32), "a_log": f(a_log[0]).reshape(1, 32), "d_skip": f(d_skip[0]).reshape(1, 32),
    }
    shared.update(consts)
    in_maps = []
    for c in range(8):
        b, half = c // 2, c % 2
        seq = np.zeros((33 * 128, D), np.float32)
        seq[112:128] = meta_tokens
        seq[128:] = x_prompt[b]
        smask = np.zeros((33 * 128,), np.float32)
        smask[112:] = 1.0
        xin = np.zeros((TOK, D), np.float32)
        mask = np.zeros((TOK,), np.float32)
        if half == 0:
            xin[NPRE * 128:NPT * 128] = seq[0:NMAIN * 128]
            mask[NPRE * 128:NPT * 128] = smask[0:NMAIN * 128]
        else:
            xin[0:NPRE * 128] = seq[0:NPRE * 128]
            mask[0:NPRE * 128] = smask[0:NPRE * 128]
            xin[NPRE * 128:NPT * 128] = seq[NPRE * 128:33 * 128]
            mask[NPRE * 128:NPT * 128] = smask[NPRE * 128:33 * 128]
        for s in range(NSTREAM):
            r0 = NPT * 128 + s * 32
            xin[r0:r0 + 16] = x_sample[c * NSTREAM + s]
            mask[r0:r0 + 16] = 1.0
        m = dict(shared)
        m["xin"] = xin
        m["tmask"] = f(mask.reshape(NT, 128).T)
        sl = slice(c * NSTREAM, (c + 1) * NSTREAM)
        m["cache_k"] = f(cache_k[0, sl].reshape(NSTREAM, NCACHE, SBW))
        m["cache_v"] = f(cache_v[0, sl].reshape(NSTREAM, NCACHE, SBW))
        m["st_ssm"] = f(state_ssm[0, sl].reshape(NSTREAM, SSDW, 128))
        m["st_sconv"] = f(state_ssm_conv[0, sl].reshape(NSTREAM * 3, XBC))
        m["st_fconv"] = f(state_ffn_conv[0, sl].reshape(NSTREAM * 2, 2 * DFF))
        in_maps.append(m)
    res = run_bass_kernel_spmd(nc, in_maps, core_ids=list(range(8)))
    R = res.results
    _LAST["R"] = R
    _LAST["in_maps"] = in_maps

    BATCH, SEQ, DB, DS = 4, 4096, 32, 16
    y_prompt = np.zeros((BATCH, SEQ, D), np.float32)
    y_sample = np.zeros((DB, DS, D), np.float32)
    k_prompt = np.zeros((1, BATCH, 16 + SEQ, 16, 64), np.float32)
    v_prompt = np.zeros_like(k_prompt)
    ssm_prompt = np.zeros((1, BATCH, 32, 64, 128), np.float32)
    sconv_prompt = np.zeros((1, BATCH, 3, XBC), np.float32)
    fconv_prompt = np.zeros((1, BATCH, 2, 2 * DFF), np.float32)
    k_sample = np.zeros((1, DB, DS, 16, 64), np.float32)
    v_sample = np.zeros_like(k_sample)
    ssm_sample = np.zeros((1, DB, 32, 64, 128), np.float32)
    sconv_sample = np.zeros((1, DB, 3, XBC), np.float32)
    fconv_sample = np.zeros((1, DB, 2, 2 * DFF), np.float32)
    for c in range(8):
        b, half = c // 2, c % 2
        r = R[c]
        yo, ko, vo = r["y_out"], r["k_out"], r["v_out"]
        kp = k_prompt[0, b].reshape(16 + SEQ, SBW)
        vp = v_prompt[0, b].reshape(16 + SEQ, SBW)
        if half == 0:
            y_prompt[b, 0:16 * 128] = yo[128:17 * 128]
            kp[0:16] = ko[112:128]
            vp[0:16] = vo[112:128]
            kp[16:16 + 16 * 128] = ko[128:17 * 128]
            vp[16:16 + 16 * 128] = vo[128:17 * 128]
        else:
            y_prompt[b, 16 * 128:] = yo[128:17 * 128]
            kp[16 + 16 * 128:] = ko[128:17 * 128]
            vp[16 + 16 * 128:] = vo[128:17 * 128]
            ssm_prompt[0, b] = r["ssm_out"][0].reshape(32, 64, 128)
            sconv_prompt[0, b] = r["sconv_out"][0].reshape(XBC, 3).T
            fconv_prompt[0, b] = r["fconv_out"][0].reshape(2 * DFF, 2).T
        for s in range(NSTREAM):
            gi = c * NSTREAM + s
            r0 = NMAIN * 128 + s * 32
            y_sample[gi] = yo[r0:r0 + 16]
            k_sample[0, gi] = ko[r0:r0 + 16].reshape(16, 16, 64)
            v_sample[0, gi] = vo[r0:r0 + 16].reshape(16, 16, 64)
            ssm_sample[0, gi] = r["ssm_out"][1 + s].reshape(32, 64, 128)
            sconv_sample[0, gi] = r["sconv_out"][1 + s].reshape(XBC, 3).T
            fconv_sample[0, gi] = r["fconv_out"][1 + s].reshape(2 * DFF, 2).T
    return (y_prompt, y_sample, k_prompt, v_prompt, ssm_prompt, sconv_prompt, fconv_prompt,
            k_sample, v_sample, ssm_sample, sconv_sample, fconv_sample)
```
